# Optimizing a Trainium2 kernel written in Bass

```python
import numpy as np
import jax
import jax.numpy as jnp
from jax import lax

D_MODEL = 2048
BATCH = 4
SEQ = 2048
DEPTH = 1
DEC_BATCH = 32
DEC_SEQ = 4
PAST_LEN = 8192
PAGE_SIZE = 128

HEAD_DIM = 128
H_A = 8
H_B = 4
C_B = 128
H_M = 4
W_A = H_A * HEAD_DIM
W_B = H_B * C_B
W_M = H_M * HEAD_DIM
MIX_WIDTH = W_A + W_B + W_M
D_FF = 5632
DILATED = ((128, 1), (512, 4), (2048, 16))
WIN_MAX = 2048
BLK = 128
CHUNK = 128
N_MEM = 256
N_BUCKETS = 32
REL_MAX_DIST = WIN_MAX
EPS = 1e-6
SCALE = HEAD_DIM ** -0.5
IN_SPLITS = (W_A, 2 * W_A, 3 * W_A, 3 * W_A + W_B, 3 * W_A + 2 * W_B)
IN_WIDTH = 3 * W_A + 2 * W_B + W_M

kernel_name = 'hybrid_dilated_sgu_memory_decoder_step'


def _rel_bucket(dist):
    dist = np.asarray(dist, np.int64)
    max_exact = N_BUCKETS // 2
    large = max_exact + (np.log(np.maximum(dist, 1) / max_exact) / np.log(REL_MAX_DIST / max_exact)
                         * (N_BUCKETS - max_exact)).astype(np.int64)
    large = np.minimum(large, N_BUCKETS - 1)
    return np.where(dist < max_exact, dist, large).astype(np.int32)


def _rms(t, g):
    t32 = t.astype(jnp.float32)
    y = t32 * lax.rsqrt(jnp.mean(t32 * t32, axis=-1, keepdims=True) + EPS) * g.astype(jnp.float32)
    return y.astype(t.dtype)


def _swiglu(x, wg, wu, wd):
    return (jax.nn.silu(x @ wg) * (x @ wu)) @ wd


def _pre_mix(x, g_ffn1, w1_gate, w1_up, w1_down, g_mix, w_in, g_qa, g_ka, g_sgu, g_qm):
    B, T, _ = x.shape
    h = x + 0.5 * _swiglu(_rms(x, g_ffn1), w1_gate, w1_up, w1_down)
    z = _rms(h, g_mix) @ w_in
    qa, ka, va, u, v, qm = jnp.split(z, IN_SPLITS, axis=-1)
    qa = _rms(qa.reshape(B, T, H_A, HEAD_DIM), g_qa)
    ka = _rms(ka.reshape(B, T, H_A, HEAD_DIM), g_ka)
    va = va.reshape(B, T, H_A, HEAD_DIM)
    u = u.reshape(B, T, H_B, C_B)
    v = _rms(v.reshape(B, T, H_B, C_B), g_sgu)
    qm = _rms(qm.reshape(B, T, H_M, HEAD_DIM), g_qm)
    return h, qa, ka, va, u, v, qm


def _post_mix(h, oa, ob, om, g_mix_out, w_out, g_ffn2, w2_gate, w2_up, w2_down):
    B, T, _ = h.shape
    ga, gb, gm = jnp.split(g_mix_out, (W_A, W_A + W_B))
    cat = jnp.concatenate([_rms(oa.reshape(B, T, W_A), ga),
                           _rms(ob.reshape(B, T, W_B), gb),
                           _rms(om.reshape(B, T, W_M), gm)], axis=-1)
    h = h + cat @ w_out
    return h + 0.5 * _swiglu(_rms(h, g_ffn2), w2_gate, w2_up, w2_down)


def _dilated_prompt(q, k, v, win, dil, rel_bias):
    B, S, H, D = q.shape
    n_steps = win // dil + 1
    span = dil * BLK
    s_pad = -(-S // span) * span
    L = s_pad // dil
    nb = L // BLK

    def to_res(t):
        t = jnp.pad(t.astype(jnp.float32), ((0, 0), (0, s_pad - S), (0, 0), (0, 0)))
        t = t.reshape(B, L, dil, H, D).transpose(0, 2, 1, 3, 4)
        return t.reshape(B, dil, nb, BLK, H, D)

    def with_prev(t):
        prev = jnp.pad(t[:, :, :-1], ((0, 0), (0, 0), (1, 0), (0, 0), (0, 0), (0, 0)))
        return jnp.concatenate([prev, t], axis=3)

    qr = to_res(q)
    kb = with_prev(to_res(k))
    vb = with_prev(to_res(v))
    step = np.arange(BLK)[:, None] + BLK - np.arange(2 * BLK)[None, :]
    band = (step >= 0) & (step < n_steps)
    first = (np.arange(nb)[:, None, None] > 0) | (np.arange(2 * BLK)[None, None, :] >= BLK)
    valid = jnp.asarray(band[None] & first)
    bias = rel_bias.astype(jnp.float32)[_rel_bucket(np.clip(step, 0, n_steps - 1) * dil)]
    bias = jnp.transpose(bias, (2, 0, 1))
    logits = jnp.einsum('brnihd,brnjhd->brnhij', qr, kb) * SCALE + bias
    logits = jnp.where(valid[:, None], logits, -jnp.inf)
    m = jnp.max(logits, axis=-1)
    p = jnp.exp(logits - m[..., None])
    s = jnp.sum(p, axis=-1)
    o = jnp.einsum('brnhij,brnjhd->brnihd', p, vb)

    def from_res(t):
        t = t.reshape((B, dil, L) + t.shape[4:])
        return jnp.swapaxes(t, 1, 2).reshape((B, s_pad) + t.shape[3:])[:, :S]

    return from_res(o), from_res(jnp.swapaxes(m, 3, 4)), from_res(jnp.swapaxes(s, 3, 4))


def _dilated_sample(q, k_all, v_all, n_past, win, dil, rel_bias):
    T = q.shape[1]
    n_steps = win // dil + 1
    dist = np.arange(n_steps) * dil
    idx = n_past + np.arange(T)[:, None] - dist[None, :]
    valid = jnp.asarray(idx >= 0)
    idx = np.maximum(idx, 0)
    kg = k_all[:, idx].astype(jnp.float32)
    vg = v_all[:, idx].astype(jnp.float32)
    bias = rel_bias.astype(jnp.float32)[_rel_bucket(dist)].T[:, None, :]
    logits = jnp.einsum('bthd,btkhd->bhtk', q.astype(jnp.float32), kg) * SCALE + bias
    logits = jnp.where(valid, logits, -jnp.inf)
    m = jnp.max(logits, axis=-1)
    p = jnp.exp(logits - m[..., None])
    s = jnp.sum(p, axis=-1)
    o = jnp.einsum('bhtk,btkhd->bthd', p, vg)
    return o, jnp.swapaxes(m, 1, 2), jnp.swapaxes(s, 1, 2)


def _combine(stats):
    m_all = jnp.stack([m for _, m, _ in stats])
    mx = jnp.max(m_all, axis=0)
    num = sum(jnp.exp(m - mx)[..., None] * o for o, m, _ in stats)
    den = sum(jnp.exp(m - mx) * s for _, m, s in stats)
    return num / den[..., None]


def _sgu_prompt(u, v, w_s, b_s):
    B, S, G, C = v.shape
    w = w_s * jnp.tril(jnp.ones((CHUNK, CHUNK), w_s.dtype))
    vc = v.reshape(B, S // CHUNK, CHUNK, G, C)
    mixed = jnp.einsum('gts,bnsgc->bntgc', w, vc) + b_s.T[None, None, :, :, None]
    return u * mixed.reshape(B, S, G, C)


def _sgu_sample(u, v, w_s, b_s):
    T = v.shape[1]
    w = (w_s * jnp.tril(jnp.ones((CHUNK, CHUNK), w_s.dtype)))[:, :T, :T]
    mixed = jnp.einsum('gts,bsgc->btgc', w, v) + b_s[:, :T].T[None, :, :, None]
    return u * mixed


def _mem_kv(mem, g_mem, w_mem_kv, g_km):
    B, M, _ = mem.shape
    k, v = jnp.split(_rms(mem, g_mem) @ w_mem_kv, 2, axis=-1)
    return _rms(k.reshape(B, M, H_M, HEAD_DIM), g_km), v.reshape(B, M, H_M, HEAD_DIM)


def _mem_attend(q, k, v):
    logits = jnp.einsum('bthd,bmhd->bhtm', q.astype(jnp.float32), k.astype(jnp.float32)) * SCALE
    p = jax.nn.softmax(logits, axis=-1)
    return jnp.einsum('bhtm,bmhd->bthd', p, v.astype(jnp.float32)).astype(q.dtype)


def setup_inputs(seed: int = 0) -> dict:
    key = jax.random.key(seed)
    keys = iter(jax.random.split(key, 40))

    def nrm(shape, scale=1.0):
        return scale * jax.random.normal(next(keys), shape, jnp.float32)

    def gain(shape):
        return 1.0 + 0.05 * nrm(shape)

    w_buf = min(WIN_MAX, PAST_LEN)
    return {
        'x_prompt': nrm((BATCH, SEQ, D_MODEL)),
        'x_sample': nrm((DEC_BATCH, DEC_SEQ, D_MODEL)),
        'cache_win_k': nrm((DEPTH, DEC_BATCH, w_buf, H_A, HEAD_DIM)),
        'cache_win_v': nrm((DEPTH, DEC_BATCH, w_buf, H_A, HEAD_DIM)),
        'cache_mem_k': nrm((DEPTH, DEC_BATCH, N_MEM, H_M, HEAD_DIM)),
        'cache_mem_v': nrm((DEPTH, DEC_BATCH, N_MEM, H_M, HEAD_DIM)),
        'mem_prompt': nrm((BATCH, N_MEM, D_MODEL)),
        'rel_bias': nrm((N_BUCKETS, H_A), 0.5),
        'g_ffn1': gain((DEPTH, D_MODEL)),
        'w1_gate': nrm((DEPTH, D_MODEL, D_FF), D_MODEL ** -0.5),
        'w1_up': nrm((DEPTH, D_MODEL, D_FF), D_MODEL ** -0.5),
        'w1_down': nrm((DEPTH, D_FF, D_MODEL), D_FF ** -0.5),
        'g_mix': gain((DEPTH, D_MODEL)),
        'w_in': nrm((DEPTH, D_MODEL, IN_WIDTH), D_MODEL ** -0.5),
        'g_qa': gain((DEPTH, HEAD_DIM)),
        'g_ka': gain((DEPTH, HEAD_DIM)),
        'g_sgu': gain((DEPTH, H_B, C_B)),
        'w_sgu': nrm((DEPTH, H_B, CHUNK, CHUNK), CHUNK ** -0.5),
        'b_sgu': 1.0 + 0.1 * nrm((DEPTH, H_B, CHUNK)),
        'g_qm': gain((DEPTH, HEAD_DIM)),
        'g_mem': gain((DEPTH, D_MODEL)),
        'w_mem_kv': nrm((DEPTH, D_MODEL, 2 * W_M), D_MODEL ** -0.5),
        'g_km': gain((DEPTH, HEAD_DIM)),
        'g_mix_out': gain((DEPTH, MIX_WIDTH)),
        'w_out': nrm((DEPTH, MIX_WIDTH, D_MODEL), MIX_WIDTH ** -0.5),
        'g_ffn2': gain((DEPTH, D_MODEL)),
        'w2_gate': nrm((DEPTH, D_MODEL, D_FF), D_MODEL ** -0.5),
        'w2_up': nrm((DEPTH, D_MODEL, D_FF), D_MODEL ** -0.5),
        'w2_down': nrm((DEPTH, D_FF, D_MODEL), D_FF ** -0.5),
    }


def reference(x_prompt, x_sample, cache_win_k, cache_win_v, cache_mem_k, cache_mem_v, mem_prompt,
              rel_bias, g_ffn1, w1_gate, w1_up, w1_down, g_mix, w_in, g_qa, g_ka, g_sgu, w_sgu, b_sgu,
              g_qm, g_mem, w_mem_kv, g_km, g_mix_out, w_out, g_ffn2, w2_gate, w2_up, w2_down):
    y_p = x_prompt
    y_s = x_sample
    n_past = cache_win_k.shape[2]
    wk_p, wv_p, mk_p, mv_p, wk_s, wv_s, cv_s = [], [], [], [], [], [], []
    for l in range(DEPTH):
        pre = (g_ffn1[l], w1_gate[l], w1_up[l], w1_down[l], g_mix[l], w_in[l], g_qa[l], g_ka[l], g_sgu[l], g_qm[l])
        post = (g_mix_out[l], w_out[l], g_ffn2[l], w2_gate[l], w2_up[l], w2_down[l])

        h, qa, ka, va, u, v, qm = _pre_mix(y_p, *pre)
        oa = _combine([_dilated_prompt(qa, ka, va, win, dil, rel_bias) for win, dil in DILATED]).astype(h.dtype)
        ob = _sgu_prompt(u, v, w_sgu[l], b_sgu[l])
        mk, mv = _mem_kv(mem_prompt, g_mem[l], w_mem_kv[l], g_km[l])
        om = _mem_attend(qm, mk, mv)
        y_p = _post_mix(h, oa, ob, om, *post)
        keep = min(WIN_MAX, ka.shape[1])
        wk_p.append(ka[:, -keep:])
        wv_p.append(va[:, -keep:])
        mk_p.append(mk)
        mv_p.append(mv)

        h, qa, ka, va, u, v, qm = _pre_mix(y_s, *pre)
        k_all = jnp.concatenate([cache_win_k[l], ka], axis=1)
        v_all = jnp.concatenate([cache_win_v[l], va], axis=1)
        oa = _combine([_dilated_sample(qa, k_all, v_all, n_past, win, dil, rel_bias)
                       for win, dil in DILATED]).astype(h.dtype)
        ob = _sgu_sample(u, v, w_sgu[l], b_sgu[l])
        om = _mem_attend(qm, cache_mem_k[l], cache_mem_v[l])
        y_s = _post_mix(h, oa, ob, om, *post)
        wk_s.append(ka)
        wv_s.append(va)
        cv_s.append(v)

    return (y_p, y_s, jnp.stack(wk_p), jnp.stack(wv_p), jnp.stack(mk_p), jnp.stack(mv_p),
            jnp.stack(wk_s), jnp.stack(wv_s), jnp.stack(cv_s))
```

```python
import contextlib
import numpy as np
import concourse.bass as bass
import concourse.mybir as mybir
from concourse.bass_utils import run_bass_kernel_spmd

F32 = mybir.dt.float32
BF16 = mybir.dt.bfloat16
AF = mybir.ActivationFunctionType
ALU = mybir.AluOpType
AX = mybir.AxisListType

NCORES = 8
D = 2048
NCH = 16
DFF = 5632
TP = 1024
TS = 16
T = TP + TS
EPS = 1e-6
SCALE = 128 ** -0.5
NEG = -30000.0
N_BUCKETS = 32
REL_MAX_DIST = 2048


class TokSet:
    def __init__(self, n, tiles):
        self.n = n
        self.tiles = tiles


MAIN = TokSet(T, [(0, 347), (347, 347), (694, 346)])
PREV = TokSet(TP, [(0, 342), (342, 341), (683, 341)])

C_GQA, C_GKA, C_GQM, C_GKM, C_GSGU, C_PM0, C_PM1, C_ZERO = 0, 1, 2, 3, 4, 8, 9, 10
C_TRIL = 12
C_BSGU = 140
C_BSS = 652
C_ASGU = 656
C_Z = 720
C_BIASS = 832
C_BIASN = 928
NCST = 1312


class Op:
    __slots__ = ("eng", "fn", "deps", "signal", "ticket", "dma", "slot")


class Prog:
    ENGS = ("pe", "act", "dve", "pool", "sp")

    def __init__(self, nc, stack):
        self.nc = nc
        self.stack = stack
        self.pending = []
        self.last_w = {}
        self.readers = {}
        self.eng_sem = {e: stack.enter_context(nc.semaphore("s_" + e)) for e in self.ENGS}
        self.eng_cnt = {e: 0 for e in self.ENGS}
        self.slot_sem = {}
        self.slot_cnt = {}
        self.waited = {e: {} for e in self.ENGS}
        self.frontier = {}
        self.barrier_ops = []

    def add(self, eng, fn, reads=(), writes=(), slot=None, after=()):
        op = Op()
        op.eng, op.fn, op.slot = eng, fn, slot
        op.dma = slot is not None
        op.signal = op.dma
        op.ticket = None
        deps = []
        for k in reads:
            w = self.last_w.get(k)
            if w is not None:
                deps.append(w)
        for k in writes:
            w = self.last_w.get(k)
            if w is not None:
                deps.append(w)
            deps.extend(self.readers.get(k, ()))
        deps.extend(after)
        deps.extend(self.barrier_ops)
        seen = set()
        op.deps = []
        for d in deps:
            if d is op or id(d) in seen:
                continue
            seen.add(id(d))
            if d.dma or d.eng != eng or eng != "pe":
                d.signal = True
                op.deps.append(d)
        for k in reads:
            self.readers.setdefault(k, []).append(op)
        for k in writes:
            self.last_w[k] = op
            self.readers[k] = []
        if op.dma:
            if slot not in self.slot_sem:
                self.slot_sem[slot] = self.stack.enter_context(self.nc.semaphore("d_" + str(len(self.slot_sem))))
                self.slot_cnt[slot] = 0
            self.slot_cnt[slot] += 16
            op.ticket = self.slot_cnt[slot]
            self.frontier[("slot", slot)] = op
        else:
            self.frontier[("eng", eng)] = op
        self.pending.append(op)
        return op

    def barrier(self):
        ops = list(self.frontier.values())
        for o in ops:
            o.signal = True
        self.barrier_ops = ops
        self.last_w = {}
        self.readers = {}

    def _sem_of(self, op):
        return self.slot_sem[op.slot] if op.dma else self.eng_sem[op.eng]

    def flush(self, final=False):
        nc = self.nc
        for op in self.pending:
            if not op.dma and op.signal:
                self.eng_cnt[op.eng] += 1
                op.ticket = self.eng_cnt[op.eng]
        per = {e: [o for o in self.pending if o.eng == e] for e in self.ENGS}

        def emit(ename, eng):
            waited = self.waited[ename]
            for op in per[ename]:
                for d in op.deps:
                    sem = self._sem_of(d)
                    key = id(sem)
                    if waited.get(key, 0) >= d.ticket:
                        continue
                    eng.wait_ge(sem, d.ticket)
                    waited[key] = d.ticket
                ins = op.fn(eng)
                if op.signal:
                    ins.then_inc(self._sem_of(op), 16 if op.dma else 1)
            if final and ename == "sp":
                for slot, sem in self.slot_sem.items():
                    if waited.get(id(sem), 0) < self.slot_cnt[slot]:
                        eng.wait_ge(sem, self.slot_cnt[slot])
                for e2 in self.ENGS:
                    if e2 != "sp" and self.eng_cnt[e2] > 0:
                        eng.wait_ge(self.eng_sem[e2], self.eng_cnt[e2])

        with nc.Block() as block:
            @block.tensor
            def _(e):
                emit("pe", e)

            @block.scalar
            def _(e):
                emit("act", e)

            @block.vector
            def _(e):
                emit("dve", e)

            @block.gpsimd
            def _(e):
                emit("pool", e)

            @block.sync
            def _(e):
                emit("sp", e)
        self.pending = []


class Alloc:
    def __init__(self, b, off):
        self.b = b
        self.off = off

    def __call__(self, shape, dt):
        n = 1
        for s in shape:
            n *= s
        words = n if dt == F32 else (n + 1) // 2
        v = self.b.view(self.off, shape, dt)
        self.off += words
        return v


def _run(gen):
    while True:
        try:
            next(gen)
        except StopIteration as st:
            return st.value


class Pipe:
    def __init__(self, lag=1):
        self.prev = None

    def add(self, front, back):
        fgen = front()
        bgen = self.prev[0](self.prev[1]) if self.prev else None
        res, fdone, bdone = None, False, bgen is None
        while not (fdone and bdone):
            if not fdone:
                try:
                    next(fgen)
                except StopIteration as st:
                    res, fdone = st.value, True
            if not bdone:
                try:
                    next(bgen)
                except StopIteration:
                    bdone = True
        self.prev = (back, res)

    def drain(self):
        if self.prev:
            _run(self.prev[0](self.prev[1]))
        self.prev = None


class PipeLag:
    def __init__(self, lag):
        self.lag = lag
        self.q = []

    def add(self, front, back):
        self.q.append((back, front()))
        if len(self.q) > self.lag:
            b, r = self.q.pop(0)
            b(r)

    def drain(self):
        while self.q:
            b, r = self.q.pop(0)
            b(r)


class Builder:
    NW = 52000

    def __init__(self, nc, stack, cfg):
        self.nc = nc
        self.stack = stack
        self.cfg = cfg
        self.P = Prog(nc, stack)
        self.psum_i = 0
        self.ring = list(range(8))
        self.ctr = {}

    def din(self, name, shape, dt=F32):
        return self.nc.dram_tensor(name, list(shape), dt, kind="ExternalInput").ap()

    def dout(self, name, shape, dt=F32):
        return self.nc.dram_tensor(name, list(shape), dt, kind="ExternalOutput").ap()

    def dscr(self, name, shape, dt):
        return self.nc.dram_tensor(name, list(shape), dt).ap()

    def psum(self):
        i = self.ring[self.psum_i % len(self.ring)]
        self.psum_i += 1
        return self.PS[i], ("ps", i)

    def nxt(self, name, n):
        v = self.ctr.get(name, 0)
        self.ctr[name] = v + 1
        return v % n

    def view(self, off, shape, dt):
        n = 1
        for s in shape:
            n *= s
        words = n if dt == F32 else (n + 1) // 2
        assert off + words <= self.NW, (off, words)
        a = self.arena[:, off:off + words]
        if dt != F32:
            a = a.bitcast(dt)[:, 0:n]
        if len(shape) == 2:
            a = a.rearrange("p (a b) -> p a b", a=shape[0])
        elif len(shape) == 3:
            a = a.rearrange("p (a b c) -> p a b c", a=shape[0], b=shape[1])
        return a

    def setup(self):
        nc, P, st = self.nc, self.P, self.stack
        self.PS = [st.enter_context(nc.psum_tensor("ps%d" % i, [128, 512], F32)) for i in range(8)]
        self.arena = st.enter_context(nc.sbuf_tensor("arena", [128, self.NW], F32))
        al = Alloc(self, 0)
        self.ident = al([128], F32)
        self.ones = {n: al([128], BF16) for n in (2048, 1024, 512, 128, 1)}
        self.gains = al([4, NCH], F32)
        self.cst = al([NCST], F32)
        self.epsc = al([2], F32)
        self.rstd = al([T], F32)
        self.sq = [al([348], BF16) for _ in range(3)]
        self.mkT = al([4, 256], BF16)
        self.mvtok = al([2, 512], BF16)
        self.WR_off = al.off
        self.WR = [al([NCH, 256], BF16) for _ in range(4)]
        self.BIGB_off = al.off
        self.BIGB = al([NCH * T], BF16)
        self.CAT = self.view(self.BIGB_off, [NCH, T], BF16)
        self.XN_off = al.off
        self.XN = al([NCH, T], BF16)
        self.R_off = al.off
        self.R = al([NCH, T], F32)
        self.free_off = al.off
        d_ident = self.din("ident", [128, 128])
        d_gains = self.din("gains", [128, 4 * NCH])
        d_cst = self.din("cst", [128, NCST])
        P.add("sp", lambda e: e.dma_start(out=self.ident, in_=d_ident), writes=[("ident",)], slot="c0")
        P.add("sp", lambda e: e.dma_start(out=self.gains.rearrange("p a c -> p (a c)"), in_=d_gains),
              writes=[("gains",)], slot="c1")
        P.add("sp", lambda e: e.dma_start(out=self.cst, in_=d_cst), writes=[("cst",)], slot="c2")
        for n in (2048, 1024, 512, 128, 1):
            P.add("pool", lambda e, n=n: e.memset(self.ones[n], 1.0 / n), writes=[("ones", n)])
        P.add("pool", lambda e: e.memset(self.epsc, EPS), writes=[("epsc",)])

    def ccol(self, c, n=1, rows=128):
        return self.cst[:rows, c:c + n]

    def load_slab(self, w, r0, nrows, c0, ncols=256):
        i = self.nxt("wr", 4)
        slab = self.WR[i]
        nk = nrows // 128
        self.P.add("pool", lambda e: e.dma_start(
            out=slab[:, :nk, :ncols], in_=w[r0:r0 + nrows, c0:c0 + ncols].rearrange("(k p) c -> p k c", p=128)),
            writes=[("WR", i)], slot=("WR", i))
        return slab, ("WR", i)

    def load_transpose(self, src, ntok, dst, dst_key, stg):
        P = self.P
        nblk = (ntok + 127) // 128
        for b in range(nblk):
            t0 = b * 128
            nt = min(128, ntok - t0)
            s = self.nxt("ltstg", len(stg))
            P.add("sp", lambda e, s=s, t0=t0, nt=nt: e.dma_start(out=stg[s][:nt, :], in_=src[t0:t0 + nt, :]),
                  writes=[("ltstg", s)], slot=("ltstg", s))
            for cg in range(4):
                ps, psk = self.psum()
                for j in range(4):
                    c = cg * 4 + j
                    P.add("pe", lambda e, ps=ps, s=s, c=c, j=j, nt=nt: e.transpose(
                        out=ps[:, j * 128:j * 128 + nt], in_=stg[s][:nt, c * 128:(c + 1) * 128],
                        identity=self.ident[:nt, :nt]),
                        reads=[("ltstg", s), ("ident",)], writes=[psk])
                src_v = lambda ps, nt: ps[:, :].rearrange("p (j t) -> p j t", j=4)[:, :, :nt]
                if cg % 2 == 0:
                    fn = lambda e, ps=ps, cg=cg, t0=t0, nt=nt: e.activation(
                        out=dst[:, cg * 4:cg * 4 + 4, t0:t0 + nt], in_=src_v(ps, nt), func=AF.Copy)
                    eng = "act"
                else:
                    fn = lambda e, ps=ps, cg=cg, t0=t0, nt=nt: e.tensor_copy(
                        out=dst[:, cg * 4:cg * 4 + 4, t0:t0 + nt], in_=src_v(ps, nt))
                    eng = "dve"
                P.add(eng, fn, reads=[psk], writes=[(dst_key, cg * 4 + j, "all") for j in range(4)])

    def rms_stats(self, src_fn, keys_fn, nchunks, ones_n, ts, rstd, rstd_key):
        P = self.P
        for ti, (t0, n) in enumerate(ts.tiles):
            ps, psk = self.psum()
            for c in range(nchunks):
                i = self.nxt("sq", 3)
                sq, sqk = self.sq[i], ("sq", i)
                P.add("act", lambda e, sq=sq, c=c, t0=t0, n=n: e.activation(
                    out=sq[:, :n], in_=src_fn(c, t0, n), func=AF.Square),
                    reads=keys_fn(c, ti), writes=[sqk])
                P.add("pe", lambda e, ps=ps, sq=sq, c=c, n=n: e.matmul(
                    ps[:, :n], lhsT=self.ones[ones_n], rhs=sq[:, :n], start=(c == 0), stop=(c == nchunks - 1)),
                    reads=[sqk, ("ones", ones_n)], writes=[psk])
            P.add("act", lambda e, ps=ps, t0=t0, n=n: e.activation(
                out=rstd[:, t0:t0 + n], in_=ps[:, :n], func=AF.Ln, bias=self.epsc[:, 0:1], scale=1.0),
                reads=[psk, ("epsc",)], writes=[(rstd_key, ti)])
            P.add("act", lambda e, t0=t0, n=n: e.activation(
                out=rstd[:, t0:t0 + n], in_=rstd[:, t0:t0 + n], func=AF.Exp, scale=-0.5),
                reads=[(rstd_key, ti)], writes=[(rstd_key, ti)])

    def rmsnorm_R(self, which, ts):
        P, R, XN = self.P, self.R, self.XN
        rk = lambda c, ti: [("R", c, ti), ("R", c, "all")]
        self.rms_stats(lambda c, t0, n: R[:, c, t0:t0 + n], rk, NCH, 2048, ts, self.rstd, "rstd")
        for ti, (t0, n) in enumerate(ts.tiles):
            for c in range(NCH):
                P.add("dve", lambda e, c=c, t0=t0, n=n: e.scalar_tensor_tensor(
                    out=XN[:, c, t0:t0 + n], in0=R[:, c, t0:t0 + n], scalar=self.gains[:, which, c:c + 1],
                    in1=self.rstd[:, t0:t0 + n], op0=ALU.mult, op1=ALU.mult),
                    reads=rk(c, ti) + [("rstd", ti), ("gains",)], writes=[("XN", c, ti)])

    def ffn(self, wg, wu, wd, ts, sg):
        P, XN, R = self.P, self.XN, self.R
        groups = [(0, 5), (5, 5), (10, 4), (14, 4), (18, 4)]
        ACTB = self.view(self.BIGB_off, [10, T], BF16)
        WD = [self.view(self.BIGB_off + 5 * T + i * 5 * 256, [10, 256], BF16) for i in range(2)]
        for g0, gn in groups:
            for s in range(g0, g0 + gn):
                c0 = s * 256
                sl_g, kg = self.load_slab(wg, 0, D, c0)
                sl_u, ku = self.load_slab(wu, 0, D, c0)
                for j in range(2):
                    fl = (s - g0) * 2 + j
                    for ti, (t0, n) in enumerate(ts.tiles):
                        pg, pgk = self.psum()
                        pu, puk = self.psum()
                        for ko in range(NCH):
                            P.add("pe", lambda e, pg=pg, sl=sl_g, ko=ko, j=j, t0=t0, n=n: e.matmul(
                                pg[:, :n], lhsT=sl[:, ko, j * 128:(j + 1) * 128], rhs=XN[:, ko, t0:t0 + n],
                                start=(ko == 0), stop=(ko == NCH - 1)),
                                reads=[kg, ("XN", ko, ti)], writes=[pgk])
                        for ko in range(NCH):
                            P.add("pe", lambda e, pu=pu, sl=sl_u, ko=ko, j=j, t0=t0, n=n: e.matmul(
                                pu[:, :n], lhsT=sl[:, ko, j * 128:(j + 1) * 128], rhs=XN[:, ko, t0:t0 + n],
                                start=(ko == 0), stop=(ko == NCH - 1)),
                                reads=[ku, ("XN", ko, ti)], writes=[puk])
                        i = self.nxt("sg", 3)
                        sgt, sgk = sg[i], ("sg", i)
                        P.add("act", lambda e, sgt=sgt, pg=pg, n=n: e.activation(out=sgt[:, :n], in_=pg[:, :n], func=AF.Silu),
                              reads=[pgk], writes=[sgk])
                        P.add("dve", lambda e, sgt=sgt, pu=pu, fl=fl, t0=t0, n=n: e.tensor_tensor(
                            out=ACTB[:, fl, t0:t0 + n], in0=sgt[:, :n], in1=pu[:, :n], op=ALU.mult),
                            reads=[sgk, puk], writes=[("ACTB", fl, ti)])
            nk = gn * 2
            r0 = g0 * 256
            for ds in range(8):
                slot = self.nxt("wd", 2)
                c0 = ds * 256
                P.add("pool", lambda e, slot=slot, c0=c0, r0=r0, nk=nk: e.dma_start(
                    out=WD[slot][:, :nk, :], in_=wd[r0:r0 + nk * 128, c0:c0 + 256].rearrange("(k p) c -> p k c", p=128)),
                    writes=[("WD", slot)], slot=("WD", slot))
                for j in range(2):
                    m = ds * 2 + j
                    for ti, (t0, n) in enumerate(ts.tiles):
                        pd, pdk = self.psum()
                        for k in range(nk):
                            P.add("pe", lambda e, pd=pd, slot=slot, k=k, j=j, t0=t0, n=n, nk=nk: e.matmul(
                                pd[:, :n], lhsT=WD[slot][:, k, j * 128:(j + 1) * 128], rhs=ACTB[:, k, t0:t0 + n],
                                start=(k == 0), stop=(k == nk - 1)),
                                reads=[("WD", slot), ("ACTB", k, ti)], writes=[pdk])
                        P.add("dve", lambda e, pd=pd, m=m, t0=t0, n=n: e.scalar_tensor_tensor(
                            out=R[:, m, t0:t0 + n], in0=pd[:, :n], scalar=0.5, in1=R[:, m, t0:t0 + n],
                            op0=ALU.mult, op1=ALU.add),
                            reads=[pdk, ("R", m, ti), ("R", m, "all")], writes=[("R", m, ti)])

    def transpose_store(self, src_fn, keys_fn, nchunks, ntok, dst, stg):
        P = self.P
        nblk = (ntok + 127) // 128
        ncg = (nchunks + 3) // 4
        for b in range(nblk):
            t0 = b * 128
            nt = min(128, ntok - t0)
            s = self.nxt("tsstg", len(stg))
            for cg in range(ncg):
                ps, psk = self.psum()
                nj = min(4, nchunks - cg * 4)
                for j in range(nj):
                    c = cg * 4 + j
                    P.add("pe", lambda e, ps=ps, c=c, j=j, t0=t0, nt=nt: e.transpose(
                        out=ps[:nt, j * 128:(j + 1) * 128], in_=src_fn(c, t0, nt), identity=self.ident),
                        reads=keys_fn(c) + [("ident",)], writes=[psk])
                if cg % 2 == 0:
                    fn = lambda e, ps=ps, cg=cg, nt=nt, s=s, nj=nj: e.activation(
                        out=stg[s][:nt, cg * 512:cg * 512 + nj * 128], in_=ps[:nt, :nj * 128], func=AF.Copy)
                    eng = "act"
                else:
                    fn = lambda e, ps=ps, cg=cg, nt=nt, s=s, nj=nj: e.tensor_copy(
                        out=stg[s][:nt, cg * 512:cg * 512 + nj * 128], in_=ps[:nt, :nj * 128])
                    eng = "dve"
                P.add(eng, fn, reads=[psk, ("tsstg", s)], writes=[("tsstg", s, cg)])
            P.add("sp", lambda e, s=s, t0=t0, nt=nt: e.dma_start(out=dst[t0:t0 + nt, :], in_=stg[s][:nt, :nchunks * 128]),
                  reads=[("tsstg", s, cg) for cg in range(ncg)], writes=[("tsstg", s)], slot=("tsstg", s))

    def g_project_chunk(self, slab, slabk, j, xin, xkey, ts, zT, zk):
        P = self.P
        for ti, (t0, n) in enumerate(ts.tiles):
            ps, psk = self.psum()
            for ko in range(NCH):
                P.add("pe", lambda e, ps=ps, ko=ko, t0=t0, n=n: e.matmul(
                    ps[:, :n], lhsT=slab[:, ko, j * 128:(j + 1) * 128], rhs=xin[:, ko, t0:t0 + n],
                    start=(ko == 0), stop=(ko == NCH - 1)),
                    reads=[slabk, (xkey, ko, ti)], writes=[psk])
                if ko == 7:
                    yield
            P.add("act", lambda e, ps=ps, t0=t0, n=n: e.activation(out=zT[:, t0:t0 + n], in_=ps[:, :n], func=AF.Copy),
                  reads=[psk], writes=[(zk, ti)])
            yield

    def project_chunk(self, *a):
        _run(self.g_project_chunk(*a))

    def g_chunk_norm(self, zT, zk, gcol, ts, out_f32=None, out_bf=None, ob_key=None, ob_keyfn=None):
        P = self.P
        rstd = self.rstd
        pss = []
        for ti, (t0, n) in enumerate(ts.tiles):
            ps, psk = self.psum()
            pss.append((ps, psk))
            i = self.nxt("sq", 3)
            sq, sqk = self.sq[i], ("sq", i)
            P.add("act", lambda e, sq=sq, t0=t0, n=n: e.activation(out=sq[:, :n], in_=zT[:, t0:t0 + n], func=AF.Square),
                  reads=[(zk, ti)], writes=[sqk])
            P.add("pe", lambda e, ps=ps, sq=sq, n=n: e.matmul(ps[:, :n], lhsT=self.ones[128], rhs=sq[:, :n], start=True, stop=True),
                  reads=[sqk, ("ones", 128)], writes=[psk])
        yield
        for ti, (t0, n) in enumerate(ts.tiles):
            ps, psk = pss[ti]
            P.add("act", lambda e, ps=ps, t0=t0, n=n: e.activation(
                out=rstd[:, t0:t0 + n], in_=ps[:, :n], func=AF.Ln, bias=self.epsc[:, 0:1], scale=1.0),
                reads=[psk, ("epsc",)], writes=[("rstd", ti)])
        for ti, (t0, n) in enumerate(ts.tiles):
            P.add("act", lambda e, t0=t0, n=n: e.activation(
                out=rstd[:, t0:t0 + n], in_=rstd[:, t0:t0 + n], func=AF.Exp, scale=-0.5),
                reads=[("rstd", ti)], writes=[("rstd", ti)])
        yield
        for ti, (t0, n) in enumerate(ts.tiles):
            if out_bf is not None:
                wk = [ob_keyfn(ti)] if ob_keyfn else [(ob_key, ti)]
                P.add("dve", lambda e, t0=t0, n=n: e.scalar_tensor_tensor(
                    out=out_bf[:, t0:t0 + n], in0=zT[:, t0:t0 + n], scalar=gcol, in1=rstd[:, t0:t0 + n],
                    op0=ALU.mult, op1=ALU.mult), reads=[(zk, ti), ("rstd", ti), ("cst",)], writes=wk)
            if out_f32 is not None:
                P.add("dve", lambda e, t0=t0, n=n: e.scalar_tensor_tensor(
                    out=out_f32[:, t0:t0 + n], in0=zT[:, t0:t0 + n], scalar=gcol, in1=rstd[:, t0:t0 + n],
                    op0=ALU.mult, op1=ALU.mult), reads=[(zk, ti), ("rstd", ti), ("cst",)], writes=[(zk, ti)])
        yield

    def chunk_norm(self, *a, **k):
        _run(self.g_chunk_norm(*a, **k))

    def g_to_tokmajor(self, zT, zkeys, ntok, stgt, stk):
        P = self.P
        nblk = (ntok + 127) // 128
        for b0 in range(0, nblk, 4):
            ps, psk = self.psum()
            nb = min(4, nblk - b0)
            for j in range(nb):
                t0 = (b0 + j) * 128
                nt = min(128, ntok - t0)
                P.add("pe", lambda e, ps=ps, j=j, t0=t0, nt=nt: e.transpose(
                    out=ps[:nt, j * 128:(j + 1) * 128], in_=zT[:, t0:t0 + nt], identity=self.ident),
                    reads=zkeys + [("ident",)], writes=[psk])
            nfull = sum(1 for j in range(nb) if (b0 + j) * 128 + 128 <= ntok)
            if nfull:
                P.add("act", lambda e, ps=ps, b0=b0, nfull=nfull: e.activation(
                    out=stgt[:, b0:b0 + nfull, :], in_=ps[:, :nfull * 128].rearrange("p (j f) -> p j f", j=nfull),
                    func=AF.Copy), reads=[psk, (stk, "dma"), (stk, "dma2")], writes=[(stk, "f", b0)])
            if nfull < nb:
                j = nfull
                nt = ntok - (b0 + j) * 128
                P.add("act", lambda e, ps=ps, b0=b0, j=j, nt=nt: e.activation(
                    out=stgt[:nt, b0 + j, :], in_=ps[:nt, j * 128:(j + 1) * 128], func=AF.Copy),
                    reads=[psk, (stk, "dma"), (stk, "dma2")], writes=[(stk, "p")])
            yield

    def to_tokmajor(self, *a):
        _run(self.g_to_tokmajor(*a))

    def stk_keys(self, stk, ntok):
        nblk = (ntok + 127) // 128
        ks = [(stk, "f", b0) for b0 in range(0, nblk, 4) if b0 * 128 + 128 <= ntok]
        if ntok % 128:
            ks.append((stk, "p"))
        return ks

    def build(self):
        nc, P, cfg = self.nc, self.P, self.cfg
        dn = self.din
        mem = dn("mem", [256, D])
        wmem = dn("wmem", [D, 1024])
        gm = dn("g_mem", [128, NCH])
        y = self.dout("y", [T, D])
        o_wk, o_wv = self.dout("o_wk", [T, 1024]), self.dout("o_wv", [T, 1024])
        o_mk, o_mv = self.dout("o_mk", [256, 512]), self.dout("o_mv", [256, 512])
        o_cv = self.dout("o_cv", [TS, 512])
        kvp_k = self.dscr("kvp_k", [1024, TP], BF16)
        kvp_v = self.dscr("kvp_v", [TP, 1024], BF16)
        kvl_k = self.dscr("kvl_k", [1024, TP], BF16)
        kvl_v = self.dscr("kvl_v", [TP, 1024], BF16)
        hscr = self.dscr("hscr", [NCH, 128, T], F32)
        qscr = self.dscr("qscr", [TS, 1024], F32)
        qmscr = self.dscr("qmscr", [TS, 512], F32)
        self.setup()
        R, XN = self.R, self.XN

        al = Alloc(self, self.R_off)
        stg = [al([D], F32)]
        memT = self.view(self.XN_off, [NCH, 256], F32)
        memN = al([NCH, 256], BF16)
        zTm = [al([256], F32) for _ in range(2)]
        stgm = [al([2, 128], F32) for _ in range(2)]
        gmem = al([NCH], F32)
        MT = TokSet(256, [(0, 256)])
        self.load_transpose(mem, 256, memT, "memT", stg)
        mk_ = lambda c, ti: [("memT", c, "all")]
        self.rms_stats(lambda c, t0, n: memT[:, c, t0:t0 + n], mk_, NCH, 2048, MT, self.rstd, "rstd")
        P.add("sp", lambda e: e.dma_start(out=gmem, in_=gm), writes=[("gmem",)], slot="c3")
        for c in range(NCH):
            P.add("dve", lambda e, c=c: e.scalar_tensor_tensor(
                out=memN[:, c, :], in0=memT[:, c, :], scalar=gmem[:, c:c + 1], in1=self.rstd[:, 0:256],
                op0=ALU.mult, op1=ALU.mult), reads=[("memT", c, "all"), ("rstd", 0), ("gmem",)], writes=[("memN", c, 0)])
        for sidx in range(4):
            slab, sk = self.load_slab(wmem, 0, D, sidx * 256)
            for j in range(2):
                hc = sidx * 2 + j
                zi = self.nxt("zTm", 2)
                z, zk = zTm[zi], ("zTm", zi)
                self.project_chunk(slab, sk, j, memN, "memN", MT, z, zk)
                si = self.nxt("stgm", 2)
                sm, smk = stgm[si], ("stgm", si)
                if hc < 4:
                    self.chunk_norm(z, zk, self.ccol(C_GKM), MT, out_f32=z, out_bf=self.mkT[:, hc, :], ob_key=("mkT", hc))
                    self.to_tokmajor(z, [(zk, 0)], 256, sm, smk)
                    P.add("sp", lambda e, sm=sm, hc=hc: e.dma_start(
                        out=o_mk[:, hc * 128:(hc + 1) * 128].rearrange("(b p) c -> p b c", p=128), in_=sm),
                        reads=self.stk_keys(smk, 256), writes=[(smk, "dma")], slot=("stgm", si))
                else:
                    hv = hc - 4
                    self.to_tokmajor(z, [(zk, 0)], 256, sm, smk)
                    P.add("sp", lambda e, sm=sm, hv=hv: e.dma_start(
                        out=o_mv[:, hv * 128:(hv + 1) * 128].rearrange("(b p) c -> p b c", p=128), in_=sm),
                        reads=self.stk_keys(smk, 256), writes=[(smk, "dma")], slot=("stgm", si))
                    P.add("dve", lambda e, sm=sm, hv=hv: e.tensor_copy(out=self.mvtok[:, :, hv * 128:(hv + 1) * 128], in_=sm),
                          reads=self.stk_keys(smk, 256), writes=[("mvtok", hv)])
        P.barrier()
        P.flush()
        if cfg.get("stop") == "M":
            P.flush(final=True)
            return

        xs = dn("xs", [T, D])
        w1g, w1u, w1d = dn("w1g", [D, DFF]), dn("w1u", [D, DFF]), dn("w1d", [DFF, D])
        win = dn("win", [D, 4608])
        wsgu = dn("wsgu", [4, 128, 128])

        def ffn1_pass(xsrc, ts):
            al = Alloc(self, self.free_off)
            stg = [al([D], F32)]
            sg = [al([348], F32) for _ in range(3)]
            self.load_transpose(xsrc, ts.n, R, "R", stg)
            self.rmsnorm_R(0, ts)
            self.ffn(w1g, w1u, w1d, ts, sg)
            self.rmsnorm_R(1, ts)

        def kv_project(ts, dk, dv, main, al, zTs=None, stgs=None):
            if zTs is None:
                zTs = [al([T], F32) for _ in range(3)]
                stgs = [al([9, 128], F32) for _ in range(2)]
            nzt_ = len(zTs)
            nbs = [al([T], BF16) for _ in range(2)]
            vbs = [al([8, 128], BF16) for _ in range(2)]
            pipe = Pipe(1)
            slabs = {}

            def front(hc):
                sidx, j = hc // 2, hc % 2
                if j == 0:
                    slabs[sidx] = self.load_slab(win, 0, D, 1024 + sidx * 256)
                slab, sk = slabs[sidx]
                zi = self.nxt("zT%d" % nzt_, nzt_)
                z, zk = zTs[zi], ("zT", zi)
                yield from self.g_project_chunk(slab, sk, j, XN, "XN", ts, z, zk)
                return (hc, z, zk)

            def back(r):
                hc, z, zk = r
                si = self.nxt("stgs", 2)
                sm, smk = stgs[si], ("stgs", si)
                if hc < 8:
                    ni = self.nxt("nb", 2)
                    nb, nbk = nbs[ni], ("nb", ni)
                    yield from self.g_chunk_norm(z, zk, self.ccol(C_GKA), ts, out_f32=(z if main else None), out_bf=nb, ob_key=nbk)
                    P.add("sp", lambda e, nb=nb, hc=hc: e.dma_start(out=dk[hc * 128:(hc + 1) * 128, :], in_=nb[:, 0:TP]),
                          reads=[(nbk, ti) for ti in range(3)], writes=[(nbk, ti) for ti in range(3)], slot=("nb", ni))
                    if main:
                        yield from self.g_to_tokmajor(z, [(zk, ti) for ti in range(3)], T, sm, smk)
                        P.add("sp", lambda e, sm=sm, hc=hc: e.dma_start(
                            out=o_wk[0:TP, hc * 128:(hc + 1) * 128].rearrange("(b p) c -> p b c", p=128), in_=sm[:, 0:8, :]),
                            reads=self.stk_keys(smk, T), writes=[(smk, "dma")], slot=("stgs", si))
                        P.add("sp", lambda e, sm=sm, hc=hc: e.dma_start(
                            out=o_wk[TP:T, hc * 128:(hc + 1) * 128], in_=sm[:TS, 8, :]),
                            reads=self.stk_keys(smk, T), writes=[(smk, "dma2")], slot=("stgs2", si))
                        P.add("dve", lambda e, sm=sm, hc=hc: e.tensor_copy(out=self.ks[:TS, hc * 128:(hc + 1) * 128], in_=sm[:TS, 8, :]),
                              reads=self.stk_keys(smk, T), writes=[("ks", hc)])
                else:
                    hv = hc - 8
                    yield from self.g_to_tokmajor(z, [(zk, ti) for ti in range(3)], ts.n, sm, smk)
                    vi = self.nxt("vb", 2)
                    vb, vbk = vbs[vi], ("vb", vi)
                    P.add("dve", lambda e, sm=sm, vb=vb: e.tensor_copy(out=vb, in_=sm[:, 0:8, :]),
                          reads=self.stk_keys(smk, ts.n), writes=[vbk])
                    P.add("sp", lambda e, vb=vb, hv=hv: e.dma_start(
                        out=dv[:, hv * 128:(hv + 1) * 128].rearrange("(b p) c -> p b c", p=128), in_=vb),
                        reads=[vbk], writes=[vbk], slot=("vb", vi))
                    if main:
                        P.add("sp", lambda e, sm=sm, hv=hv: e.dma_start(
                            out=o_wv[0:TP, hv * 128:(hv + 1) * 128].rearrange("(b p) c -> p b c", p=128), in_=sm[:, 0:8, :]),
                            reads=self.stk_keys(smk, T), writes=[(smk, "dma")], slot=("stgs", si))
                        P.add("sp", lambda e, sm=sm, hv=hv: e.dma_start(
                            out=o_wv[TP:T, hv * 128:(hv + 1) * 128], in_=sm[:TS, 8, :]),
                            reads=self.stk_keys(smk, T), writes=[(smk, "dma2")], slot=("stgs2", si))
                        P.add("dve", lambda e, sm=sm, hv=hv: e.tensor_copy(out=self.vsa[:TS, hv, 0:128], in_=sm[:TS, 8, :]),
                              reads=self.stk_keys(smk, T), writes=[("vsa", hv)])
                yield

            for hc in range(16):
                pipe.add(lambda hc=hc: front(hc), back)
            pipe.drain()

        if cfg.get("pass0", True):
            xprev = dn("xprev", [TP, D])
            ffn1_pass(xprev, PREV)
            kv_project(PREV, kvp_k, kvp_v, False, Alloc(self, self.BIGB_off))
            P.barrier()
            P.flush()

        ffn1_pass(xs, MAIN)
        for c in range(NCH):
            P.add("sp", lambda e, c=c: e.dma_start(out=hscr[c], in_=R[:, c, :]),
                  reads=[("R", c, ti) for ti in range(3)] + [("R", c, "all")], writes=[("hscr", c)], slot=("hs", c % 2))
        P.barrier()
        P.flush()

        CAT = self.CAT
        al = Alloc(self, self.R_off)
        self.qs = al([1024], F32)
        self.ks = al([1024], F32)
        self.vsa = al([8, 130], F32)
        self.us = al([512], F32)
        self.vns = al([512], F32)
        self.qms = al([512], F32)
        self.oas = al([2048], F32)
        samp_off = al.off
        P.add("pool", lambda e: e.memset(self.vsa[:TS, :, 128:130], 1.0), writes=[("vsa1",)])
        zTs = [al([T], F32) for _ in range(4)]
        stgs = [al([9, 128], F32) for _ in range(2)]
        WT = al([4, 128], BF16)
        wtmp = al([4, 128], F32)
        for g in range(4):
            P.add("sp", lambda e, g=g: e.dma_start(out=wtmp[:, g, :], in_=wsgu[g]), writes=[("wtmp", g)], slot=("wtmp", g))
        for g in range(4):
            P.add("dve", lambda e, g=g: e.tensor_tensor(out=wtmp[:, g, :], in0=wtmp[:, g, :], in1=self.cst[:, C_TRIL:C_TRIL + 128], op=ALU.mult),
                  reads=[("wtmp", g), ("cst",)], writes=[("wtmp", g)])
            ps, psk = self.psum()
            P.add("pe", lambda e, ps=ps, g=g: e.transpose(out=ps[:, 0:128], in_=wtmp[:, g, :], identity=self.ident),
                  reads=[("wtmp", g), ("ident",)], writes=[psk])
            P.add("act", lambda e, ps=ps, g=g: e.activation(out=WT[:, g, :], in_=ps[:, 0:128], func=AF.Copy),
                  reads=[psk], writes=[("WT", g)])

        kv_project(MAIN, kvl_k, kvl_v, True, al, zTs, stgs)
        vtok = al([8, 128], BF16)
        obf = al([TP], F32)
        Pm = [al([512], BF16) for _ in range(2)]
        rden = al([512], F32)
        qmn = al([T], BF16)

        def tiny_T(zsrc, zkeys, dst_fn, dkey):
            ps, psk = self.psum()
            P.add("pe", lambda e, ps=ps: e.transpose(out=ps[:TS, 0:128], in_=zsrc[:, TP:T], identity=self.ident),
                  reads=zkeys + [("ident",)], writes=[psk])
            P.add("act", lambda e, ps=ps: e.activation(out=dst_fn(), in_=ps[:TS, 0:128], func=AF.Copy),
                  reads=[psk], writes=[dkey])

        pipeA = Pipe(1)
        nzt = len(zTs)

        def projA(col0, ncols, j, slabref):
            if slabref[0] is None:
                slabref[0] = self.load_slab(win, 0, D, col0, ncols)
            slab, sk = slabref[0]
            zi = self.nxt("zT%d" % nzt, nzt)
            z, zk = zTs[zi], ("zT", zi)
            yield from self.g_project_chunk(slab, sk, j, XN, "XN", MAIN, z, zk)
            return z, zk

        ustate = {}
        for g in range(4):
            def back_u(r, g=g):
                u, uk = r
                ustate[g] = (u, uk)
                tiny_T(u, [(uk, 2)], lambda g=g: self.us[:TS, g * 128:(g + 1) * 128], ("us", g))
                yield

            def back_v(r, g=g):
                v, vk = r
                u, uk = ustate[g]
                yield from self.g_chunk_norm(v, vk, self.ccol(C_GSGU + g), MAIN, out_f32=v)
                si = self.nxt("stgs", 2)
                sm, smk = stgs[si], ("stgs", si)
                yield from self.g_to_tokmajor(v, [(vk, ti) for ti in range(3)], T, sm, smk)
                P.add("dve", lambda e, sm=sm: e.tensor_copy(out=vtok, in_=sm[:, 0:8, :]),
                      reads=self.stk_keys(smk, T), writes=[("vtok",)])
                P.add("dve", lambda e, sm=sm, g=g: e.tensor_copy(out=self.vns[:TS, g * 128:(g + 1) * 128], in_=sm[:TS, 8, :]),
                      reads=self.stk_keys(smk, T), writes=[("vns", g)])
                for half in range(2):
                    ps, psk = self.psum()
                    for j in range(4):
                        n = half * 4 + j
                        P.add("pe", lambda e, ps=ps, n=n, j=j, g=g: e.matmul(
                            ps[:, j * 128:(j + 1) * 128], lhsT=vtok[:, n, :], rhs=WT[:, g, :], start=True, stop=True),
                            reads=[("vtok",), ("WT", g)], writes=[psk])
                    q0 = half * 512
                    P.add("dve", lambda e, ps=ps, g=g, q0=q0: e.tensor_tensor(
                        out=obf[:, q0:q0 + 512].rearrange("p (j t) -> p j t", j=4),
                        in0=ps[:, :].rearrange("p (j t) -> p j t", j=4),
                        in1=self.cst[:, C_BSGU + g * 128:C_BSGU + (g + 1) * 128].unsqueeze(1).to_broadcast([128, 4, 128]),
                        op=ALU.add), reads=[psk, ("cst",)], writes=[("obf", half)])
                    P.add("dve", lambda e, g=g, q0=q0, u=u: e.tensor_tensor(
                        out=CAT[:, 8 + g, q0:q0 + 512], in0=obf[:, q0:q0 + 512], in1=u[:, q0:q0 + 512], op=ALU.mult),
                        reads=[("obf", half)] + [(uk, ti) for ti in range(3)], writes=[("CAT", 8 + g, "p", half)])
                    yield

            pipeA.add(lambda g=g: projA(3072 + g * 128, 128, 0, [None]), back_u)
            pipeA.add(lambda g=g: projA(3584 + g * 128, 128, 0, [None]), back_v)

        for hm in range(4):
            def back_qm(r, hm=hm):
                z, zk = r
                yield from self.g_chunk_norm(z, zk, self.ccol(C_GQM), MAIN, out_f32=z, out_bf=qmn, ob_key="qmn")
                tiny_T(z, [(zk, 2)], lambda hm=hm: self.qms[:TS, hm * 128:(hm + 1) * 128], ("qms", hm))
                for qt in range(2):
                    q0 = qt * 512
                    po, pok = self.psum()
                    pdn, pdk = self.psum()
                    pms = []
                    for mt in range(2):
                        ps, psk = self.psum()
                        P.add("pe", lambda e, ps=ps, mt=mt, hm=hm, q0=q0: e.matmul(
                            ps[:, :], lhsT=self.mkT[:, hm, mt * 128:(mt + 1) * 128], rhs=qmn[:, q0:q0 + 512], start=True, stop=True),
                            reads=[("qmn", ti) for ti in range(3)], writes=[psk])
                        pi = self.nxt("Pm", 2)
                        pm_, pmk = Pm[pi], ("Pm", pi)
                        P.add("act", lambda e, ps=ps, pm_=pm_: e.activation(out=pm_, in_=ps[:, :], func=AF.Exp, scale=SCALE),
                              reads=[psk], writes=[pmk])
                        pms.append((pm_, pmk))
                    yield
                    for mt in range(2):
                        pm_, pmk = pms[mt]
                        P.add("pe", lambda e, po=po, pm_=pm_, mt=mt, hm=hm: e.matmul(
                            po[:, :], lhsT=self.mvtok[:, mt, hm * 128:(hm + 1) * 128], rhs=pm_, start=(mt == 0), stop=(mt == 1)),
                            reads=[pmk], writes=[pok])
                        P.add("pe", lambda e, pdn=pdn, pm_=pm_, mt=mt: e.matmul(
                            pdn[:, :], lhsT=self.ones[1], rhs=pm_, start=(mt == 0), stop=(mt == 1)),
                            reads=[pmk], writes=[pdk])
                    P.add("act", lambda e, pdn=pdn: e.activation(out=rden, in_=pdn[:, :], func=AF.Ln), reads=[pdk], writes=[("rden",)])
                    P.add("act", lambda e: e.activation(out=rden, in_=rden, func=AF.Exp, scale=-1.0), reads=[("rden",)], writes=[("rden",)])
                    P.add("dve", lambda e, po=po, hm=hm, q0=q0: e.tensor_tensor(
                        out=CAT[:, 12 + hm, q0:q0 + 512], in0=po[:, :], in1=rden, op=ALU.mult),
                        reads=[pok, ("rden",)], writes=[("CAT", 12 + hm, "p", qt)])
                    yield

            pipeA.add(lambda hm=hm: projA(4096 + hm * 128, 128, 0, [None]), back_qm)

        qslab = {}
        for h in range(8):
            def back_q(r, h=h):
                z, zk = r
                yield from self.g_chunk_norm(z, zk, self.ccol(C_GQA), MAIN, out_f32=z, out_bf=CAT[:, h, :],
                                             ob_keyfn=lambda ti, h=h: ("QN", h, ti))
                tiny_T(z, [(zk, 2)], lambda h=h: self.qs[:TS, h * 128:(h + 1) * 128], ("qs", h))
                yield

            ref = qslab.setdefault(h // 2, [None])
            pipeA.add(lambda h=h, ref=ref: projA((h // 2) * 256, 256, h % 2, ref), back_q)
        pipeA.drain()
        P.add("sp", lambda e: e.dma_start(out=o_cv, in_=self.vns[:TS, :]), reads=[("vns", g) for g in range(4)],
              writes=[("ocv",)], slot="ocv")
        P.add("sp", lambda e: e.dma_start(out=qscr, in_=self.qs[:TS, :]), reads=[("qs", h) for h in range(8)],
              writes=[("qscr",)], slot="qscr")
        P.add("sp", lambda e: e.dma_start(out=qmscr, in_=self.qms[:TS, :]), reads=[("qms", h) for h in range(4)],
              writes=[("qmscr",)], slot="qmscr")
        P.barrier()
        P.flush()
        if cfg.get("stop") == "A":
            P.flush(final=True)
            return

        pbias = dn("pbias", [8, 128, 576])
        al = Alloc(self, samp_off)
        KT = [al([2048], BF16) for _ in range(2)]
        V1 = [al([9, 128], BF16) for _ in range(2)]
        V4 = [al([3, 4, 128], BF16) for _ in range(2)]
        V16 = [al([16, 128], BF16) for _ in range(2)]
        PB = [al([576], F32) for _ in range(2)]
        tmp = [al([256], F32) for _ in range(3)]
        Pt = [al([256], BF16) for _ in range(6)]
        rden = al([TP], F32)
        self.ring = [0, 1, 2, 3]
        OB = [self.PS[4], self.PS[5]]
        DB = [self.PS[6], self.PS[7]]
        OK_ = [("ps", 4), ("ps", 5)]
        DK_ = [("ps", 6), ("ps", 7)]
        zc = self.ccol(C_ZERO)
        pipeB = PipeLag(3)

        def score(lhsT_ap, rhs_ap, bias_ap, pmcol, width, ldk, qk):
            ps, psk = self.psum()
            P.add("pe", lambda e, ps=ps: e.matmul(ps[:, 0:width], lhsT=lhsT_ap, rhs=rhs_ap, start=True, stop=True),
                  reads=ldk + qk, writes=[psk])
            ti_ = self.nxt("tmpB", 3)
            tm, tmk = tmp[ti_], ("tmpB", ti_)
            P.add("dve", lambda e, ps=ps, tm=tm: e.scalar_tensor_tensor(
                out=tm[:, :width], in0=ps[:, :width], scalar=SCALE, in1=bias_ap, op0=ALU.mult, op1=ALU.add),
                reads=[psk] + ldk, writes=[tmk])
            pi = self.nxt("PtB", 6)
            pt, ptk = Pt[pi], ("PtB", pi)
            P.add("act", lambda e, tm=tm, pt=pt: e.activation(out=pt[:, :width], in_=tm[:, :width], func=AF.Exp,
                                                               bias=pmcol, scale=1.0),
                  reads=[tmk], writes=[ptk])
            return pt, ptk

        def pv(pt, ptk, poff, w, vl_ap, bank, out_sl, ldk):
            P.add("pe", lambda e: e.matmul(OB[bank][:, out_sl], lhsT=vl_ap, rhs=pt[:, poff:poff + w],
                                           start=False, stop=False, skip_group_check=True),
                  reads=[ptk] + ldk, writes=[OK_[bank]])
            P.add("pe", lambda e: e.matmul(DB[bank][:, out_sl], lhsT=self.ones[1], rhs=pt[:, poff:poff + w],
                                           start=False, stop=False, skip_group_check=True),
                  reads=[ptk], writes=[DK_[bank]])

        for h in range(8):
            hi = h % 2
            kt, v1, v4, v16, pb = KT[hi], V1[hi], V4[hi], V16[hi], PB[hi]
            hsl = slice(h * 128, (h + 1) * 128)
            ld = lambda fn, k: P.add("sp", fn, writes=[("ld", hi, k)], slot=("ld", hi, k))
            ld(lambda e, kt=kt, hsl=hsl: e.dma_start(out=kt[:, 0:TP], in_=kvp_k[hsl, :]), 0)
            ld(lambda e, kt=kt, hsl=hsl: e.dma_start(out=kt[:, TP:2 * TP], in_=kvl_k[hsl, :]), 1)
            ld(lambda e, v1=v1, hsl=hsl: e.dma_start(out=v1[:, 0, :], in_=kvp_v[896:1024, hsl]), 2)
            ld(lambda e, v1=v1, hsl=hsl: e.dma_start(out=v1[:, 1:9, :], in_=kvl_v[:, hsl].rearrange("(t p) c -> p t c", p=128)), 3)
            ld(lambda e, v4=v4, hsl=hsl: e.dma_start(out=v4[:, 0, :, :], in_=kvp_v[512:1024, hsl].rearrange("(i r) c -> i r c", r=4)), 4)
            ld(lambda e, v4=v4, hsl=hsl: e.dma_start(out=v4[:, 1, :, :], in_=kvl_v[0:512, hsl].rearrange("(i r) c -> i r c", r=4)), 5)
            ld(lambda e, v4=v4, hsl=hsl: e.dma_start(out=v4[:, 2, :, :], in_=kvl_v[512:1024, hsl].rearrange("(i r) c -> i r c", r=4)), 6)
            ld(lambda e, v16=v16, hsl=hsl: e.dma_start(out=v16[0:64, :, :], in_=kvp_v[:, hsl].rearrange("(i r) c -> i r c", r=16)), 7)
            ld(lambda e, v16=v16, hsl=hsl: e.dma_start(out=v16[64:128, :, :], in_=kvl_v[:, hsl].rearrange("(i r) c -> i r c", r=16)), 8)
            ld(lambda e, pb=pb, h=h: e.dma_start(out=pb, in_=pbias[h]), 9)
            ldk = [("ld", hi, k) for k in range(10)]
            qn = CAT[:, h, :]
            qk = [("QN", h)]
            items = []

            for kt_ in range(7, 16):
                qlo = max(0, (kt_ - 8) * 128)
                qhi = min(TP, (kt_ - 6) * 128)
                w = qhi - qlo
                joff = qlo - (kt_ - 8) * 128

                def fr(kt_=kt_, qlo=qlo, w=w, joff=joff, kt=kt, qn=qn, pb=pb, ldk=ldk, qk=qk):
                    return score(kt[:, kt_ * 128:(kt_ + 1) * 128], qn[:, qlo:qlo + w], pb[:, joff:joff + w],
                                 self.ccol(C_PM0) if kt_ == 7 else zc, w, ldk, qk)

                def bk(r, kt_=kt_, qlo=qlo, w=w, v1=v1, ldk=ldk):
                    pt, ptk = r
                    for o in range(0, w, 128):
                        q0 = qlo + o
                        pv(pt, ptk, o, 128, v1[:, kt_ - 7, :], q0 // 512, slice(q0 % 512, q0 % 512 + 128), ldk)
                items.append((fr, bk))
            for r in range(4):
                for ct in range(1, 4):
                    clo = max(256, ct * 128)
                    chi = min(512, ct * 128 + 256)
                    w = chi - clo
                    joff = clo - ct * 128

                    def fr(ct=ct, r=r, clo=clo, w=w, joff=joff, kt=kt, qn=qn, pb=pb, ldk=ldk, qk=qk):
                        return score(kt[:, ct * 512 + r:(ct + 1) * 512:4], qn[:, (clo - 256) * 4 + r:(clo - 256 + w) * 4:4],
                                     pb[:, 256 + joff:256 + joff + w], self.ccol(C_PM0) if ct == 1 else zc, w, ldk, qk)

                    def bk(rr_, ct=ct, r=r, clo=clo, w=w, v4=v4, ldk=ldk):
                        pt, ptk = rr_
                        for o in range(0, w, 128):
                            c0 = clo + o
                            pv(pt, ptk, o, 128, v4[:, ct - 1, r, :], (c0 - 256) // 128, slice(r, 512, 4), ldk)
                    items.append((fr, bk))
            for r0 in range(0, 16, 4):
                def fr(r0=r0, kt=kt, qn=qn, pb=pb, ldk=ldk, qk=qk):
                    ps, psk = self.psum()
                    for rr in range(4):
                        r = r0 + rr
                        P.add("pe", lambda e, ps=ps, rr=rr, a=kt[:, r:2048:16], b_=qn[:, r:TP:16]: e.matmul(
                            ps[:, rr * 64:(rr + 1) * 64], lhsT=a, rhs=b_, start=True, stop=True),
                            reads=ldk + qk, writes=[psk])
                    ti_ = self.nxt("tmpB", 3)
                    tm, tmk = tmp[ti_], ("tmpB", ti_)
                    P.add("dve", lambda e, ps=ps, tm=tm, b16=pb[:, 512:576].unsqueeze(1).to_broadcast([128, 4, 64]): e.scalar_tensor_tensor(
                        out=tm[:, :].rearrange("p (a b) -> p a b", a=4), in0=ps[:, 0:256].rearrange("p (a b) -> p a b", a=4),
                        scalar=SCALE, in1=b16, op0=ALU.mult, op1=ALU.add),
                        reads=[psk] + ldk, writes=[tmk])
                    pi = self.nxt("PtB", 6)
                    pt, ptk = Pt[pi], ("PtB", pi)
                    P.add("act", lambda e, tm=tm, pt=pt: e.activation(out=pt, in_=tm, func=AF.Exp, bias=self.ccol(C_PM1), scale=1.0),
                          reads=[tmk], writes=[ptk])
                    return pt, ptk

                def bk(rr_, r0=r0, v16=v16, ldk=ldk):
                    pt, ptk = rr_
                    for rr in range(4):
                        r = r0 + rr
                        for bank in range(2):
                            pv(pt, ptk, rr * 64 + bank * 32, 32, v16[:, r, :], bank, slice(r, 512, 16), ldk)
                items.append((fr, bk))

            def first_back(r, bk0=items[0][1]):
                for b in range(2):
                    P.add("dve", lambda e, b=b: e.memset(OB[b][:, :], 0.0), writes=[OK_[b]])
                    P.add("dve", lambda e, b=b: e.memset(DB[b][:, :], 0.0), writes=[DK_[b]])
                bk0(r)

            def last_back(r, bkl=items[-1][1], h=h, qk=qk):
                bkl(r)
                for b in range(2):
                    P.add("act", lambda e, b=b: e.activation(out=rden[:, b * 512:(b + 1) * 512], in_=DB[b][:, :], func=AF.Ln),
                          reads=[DK_[b]], writes=[("rdenB", b)])
                    P.add("act", lambda e, b=b: e.activation(out=rden[:, b * 512:(b + 1) * 512], in_=rden[:, b * 512:(b + 1) * 512],
                                                             func=AF.Exp, scale=-1.0),
                          reads=[("rdenB", b)], writes=[("rdenB", b)])
                    P.add("dve", lambda e, b=b, h=h: e.tensor_tensor(
                        out=CAT[:, h, b * 512:(b + 1) * 512], in0=OB[b][:, :], in1=rden[:, b * 512:(b + 1) * 512], op=ALU.mult),
                        reads=[OK_[b], ("rdenB", b)], writes=qk)

            items[0] = (items[0][0], first_back)
            items[-1] = (items[-1][0], last_back)
            for fr_, bk_ in items:
                pipeB.add(fr_, bk_)
        pipeB.drain()
        self.ring = list(range(8))
        P.barrier()
        P.flush()
        if cfg.get("stop") == "B":
            P.flush(final=True)
            return
        if cfg.get("stop") == "Bd":
            dbg = Alloc(self, samp_off)([D], F32)
            for b in range(8):
                P.add("dve", lambda e, b=b: e.tensor_copy(out=dbg.rearrange("p (c t) -> p c t", c=NCH), in_=CAT[:, :, b * 128:(b + 1) * 128]),
                      writes=[("dbg",)])
                P.add("sp", lambda e, b=b: e.dma_start(out=y[b * 128:(b + 1) * 128, :], in_=dbg), reads=[("dbg",)], writes=[("dbg",)], slot="dbg")
            P.flush(final=True)
            return

        cwk, cwv = dn("cwk", [4, 2048, 1024]), dn("cwv", [4, 2048, 1024])
        cmk, cmv = dn("cmk", [4, 256, 512]), dn("cmv", [4, 256, 512])
        alx = Alloc(self, self.XN_off)
        Kc0 = alx([9, 1024], BF16)
        qb = [alx([1024], F32) for _ in range(2)]
        prod = alx([1024], F32)
        assert alx.off <= self.R_off
        alw = Alloc(self, self.WR_off)
        Kc1 = alw([9, 1024], BF16)
        MKs1 = alw([2, 512], BF16)
        assert alw.off <= self.BIGB_off
        al = Alloc(self, samp_off)
        Vc = al([9, 8, 130], BF16)
        qmb = [al([512], F32) for _ in range(2)]
        L = al([4, 3, 8], F32)
        Pe = al([4, 24, 16], BF16)
        Ln = al([16, 3, 8], F32)
        Pn = al([16, 8], F32)
        MKs0 = al([2, 512], BF16)
        MVs = al([2, 4, 130], BF16)
        Lm = al([4, 2, 4], F32)
        Pme = al([4, 8, 16], BF16)
        rd = al([16], F32)
        self.ring = [0, 1, 2]
        OS = [self.PS[3], self.PS[4], self.PS[5]]
        OSK = [("ps", 3), ("ps", 4), ("ps", 5)]
        OM = [self.PS[6], self.PS[7]]
        OMK = [("ps", 6), ("ps", 7)]
        for i in range(3):
            P.add("dve", lambda e, i=i: e.memset(OS[i][:, :], 0.0), writes=[OSK[i]])
        for i in range(2):
            P.add("dve", lambda e, i=i: e.memset(OM[i][:, :], 0.0), writes=[OMK[i]])
        P.add("pool", lambda e: e.memset(Vc[:, :, :, 128:130], 1.0), writes=[("Vc1",)])
        P.add("pool", lambda e: e.memset(MVs[:, :, :, 128:130], 1.0), writes=[("MVs1",)])
        Z = self.cst[:, C_Z:C_Z + 112].rearrange("p (t x) -> p t x", t=4)
        biass = self.cst[:, C_BIASS:C_BIASS + 96].rearrange("p (a b c) -> p a b c", a=4, b=3)
        for bs in range(4):
            kb = bs % 2
            Kc = (Kc0, Kc1)[kb]
            MKs = (MKs0, MKs1)[kb]
            P.add("pool", lambda e, bs=bs, Kc=Kc: e.dma_start(out=Kc[:, 0, :], in_=cwk[bs, 1920:2048, :]), writes=[("Kc", kb, 0)], slot=("Kc", kb, 0))
            P.add("pool", lambda e, bs=bs, Kc=Kc: e.dma_start(out=Kc[:, 1:5, :], in_=cwk[bs, 1536:2048, :].rearrange("(i t) c -> i t c", t=4)),
                  writes=[("Kc", kb, 1)], slot=("Kc", kb, 1))
            P.add("pool", lambda e, bs=bs, Kc=Kc: e.dma_start(out=Kc[:, 5:9, :], in_=cwk[bs].rearrange("(i r) c -> i r c", r=16)[:, 0:4, :]),
                  writes=[("Kc", kb, 2)], slot=("Kc", kb, 2))
            P.add("pool", lambda e, bs=bs, MKs=MKs: e.dma_start(out=MKs, in_=cmk[bs].rearrange("(m p) c -> p m c", p=128)),
                  writes=[("MKs", kb)], slot=("MKs", kb))
            P.add("pool", lambda e, bs=bs: e.dma_start(out=Vc[:, 0, :, 0:128], in_=cwv[bs, 1920:2048, :].rearrange("i (h d) -> i h d", h=8)),
                  reads=[("Vc1",)], writes=[("Vc", 0)], slot=("Vc", 0))
            for t in range(4):
                P.add("pool", lambda e, bs=bs, t=t: e.dma_start(
                    out=Vc[:, 1 + t, :, 0:128],
                    in_=cwv[bs, 1536:2048, :].rearrange("(i t) (h d) -> i t h d", t=4, h=8)[:, t, :, :]),
                    reads=[("Vc1",)], writes=[("Vc", 1 + t)], slot=("Vc", 1 + t))
                P.add("pool", lambda e, bs=bs, t=t: e.dma_start(
                    out=Vc[:, 5 + t, :, 0:128],
                    in_=cwv[bs].rearrange("(i r) (h d) -> i r h d", r=16, h=8)[:, t, :, :]),
                    reads=[("Vc1",)], writes=[("Vc", 5 + t)], slot=("Vc", 5 + t))
            for mt in range(2):
                P.add("pool", lambda e, bs=bs, mt=mt: e.dma_start(
                    out=MVs[:, mt, :, 0:128], in_=cmv[bs, mt * 128:(mt + 1) * 128, :].rearrange("i (h d) -> i h d", h=4)),
                    reads=[("MVs1",)], writes=[("MVs", mt)], slot=("MVs", mt))
            kck = [("Kc", kb, i) for i in range(3)]
            vck = [("Vc", i) for i in range(9)]
            for t in range(4):
                q = bs * 4 + t
                bi = self.nxt("qb", 2)
                qbt, qbk = qb[bi], ("qb", bi)
                qmt, qmk = qmb[bi], ("qmb", bi)
                P.add("sp", lambda e, qbt=qbt, q=q: e.dma_start(out=qbt, in_=qscr[q:q + 1, :].to_broadcast([128, 1024])),
                      writes=[qbk], slot=("qb", bi))
                P.add("sp", lambda e, qmt=qmt, q=q: e.dma_start(out=qmt, in_=qmscr[q:q + 1, :].to_broadcast([128, 512])),
                      writes=[qmk], slot=("qmb", bi))
                for dl in range(3):
                    ktile = Kc[:, 0, :] if dl == 0 else Kc[:, 1 + (dl - 1) * 4 + t, :]
                    P.add("dve", lambda e, ktile=ktile, qbt=qbt: e.tensor_tensor(out=prod, in0=ktile, in1=qbt, op=ALU.mult),
                          reads=kck + [qbk], writes=[("prod",)])
                    P.add("dve", lambda e, t=t, dl=dl: e.tensor_reduce(
                        out=L[:, t, dl, :], in_=prod.rearrange("p (h d) -> p h d", h=8), axis=AX.X, op=ALU.add),
                        reads=[("prod",)], writes=[("L", t)])
                P.add("dve", lambda e, qbt=qbt: e.tensor_tensor(out=prod[:TS, :], in0=self.ks[:TS, :], in1=qbt[:TS, :], op=ALU.mult),
                      reads=[qbk], writes=[("prod",)])
                P.add("dve", lambda e, q=q: e.tensor_reduce(
                    out=Ln[:TS, q, 0, :], in_=prod[:TS, :].rearrange("p (h d) -> p h d", h=8), axis=AX.X, op=ALU.add),
                    reads=[("prod",)], writes=[("Ln", q)])
                for mt in range(2):
                    P.add("dve", lambda e, mt=mt, qmt=qmt, MKs=MKs: e.tensor_tensor(out=prod[:, 0:512], in0=MKs[:, mt, :], in1=qmt, op=ALU.mult),
                          reads=[("MKs", kb), qmk], writes=[("prod",)])
                    P.add("dve", lambda e, t=t, mt=mt: e.tensor_reduce(
                        out=Lm[:, t, mt, :], in_=prod[:, 0:512].rearrange("p (h d) -> p h d", h=4), axis=AX.X, op=ALU.add),
                        reads=[("prod",)], writes=[("Lm", t)])
            P.add("dve", lambda e: e.scalar_tensor_tensor(out=L, in0=L, scalar=SCALE, in1=biass, op0=ALU.mult, op1=ALU.add),
                  reads=[("L", t) for t in range(4)] + [("cst",)], writes=[("Lb",)])
            P.add("act", lambda e: e.activation(out=L, in_=L, func=AF.Exp), reads=[("Lb",)], writes=[("Lp",)])
            P.add("act", lambda e: e.activation(out=Lm, in_=Lm, func=AF.Exp, scale=SCALE),
                  reads=[("Lm", t) for t in range(4)], writes=[("Lmp",)])
            w0 = 12 - 4 * bs
            for t in range(4):
                P.add("dve", lambda e, t=t, w0=w0: e.tensor_tensor(
                    out=Pe[:, t, :, :], in0=L[:, t, :, :].rearrange("p a b -> p (a b)").unsqueeze(2).to_broadcast([128, 24, 16]),
                    in1=Z[:, t, w0:w0 + 16].unsqueeze(1).to_broadcast([128, 24, 16]), op=ALU.mult),
                    reads=[("Lp",), ("cst",)], writes=[("Pe", t)])
                P.add("dve", lambda e, t=t, w0=w0: e.tensor_tensor(
                    out=Pme[:, t, :, :], in0=Lm[:, t, :, :].rearrange("p a b -> p (a b)").unsqueeze(2).to_broadcast([128, 8, 16]),
                    in1=Z[:, t, w0:w0 + 16].unsqueeze(1).to_broadcast([128, 8, 16]), op=ALU.mult),
                    reads=[("Lmp",), ("cst",)], writes=[("Pme", t)])
            for t in range(4):
                for dl in range(3):
                    vi = 0 if dl == 0 else 1 + (dl - 1) * 4 + t
                    for h in range(8):
                        bk, off = h // 3, (h % 3) * 129
                        P.add("pe", lambda e, t=t, dl=dl, h=h, vi=vi, bk=bk, off=off: e.matmul(
                            OS[bk][:TS, off:off + 129], lhsT=Pe[:, t, dl * 8 + h, :], rhs=Vc[:, vi, h, 0:129],
                            start=False, stop=False, skip_group_check=True),
                            reads=[("Pe", t)] + vck, writes=[OSK[bk]])
                for mt in range(2):
                    for h in range(4):
                        bk, off = h // 3, (h % 3) * 129
                        P.add("pe", lambda e, t=t, mt=mt, h=h, bk=bk, off=off: e.matmul(
                            OM[bk][:TS, off:off + 129], lhsT=Pme[:, t, mt * 4 + h, :], rhs=MVs[:, mt, h, 0:129],
                            start=False, stop=False, skip_group_check=True),
                            reads=[("Pme", t), ("MVs", 0), ("MVs", 1)], writes=[OMK[bk]])
        biasn = self.cst[:TS, C_BIASN:C_BIASN + 384].rearrange("p (q c h) -> p q c h", q=16, c=3)
        Lnf = Ln[:TS, :, :, :]
        P.add("dve", lambda e: e.tensor_copy(out=Ln[:TS, :, 1, :], in_=Ln[:TS, :, 0, :]), reads=[("Ln", q) for q in range(16)], writes=[("Ln1",)])
        P.add("dve", lambda e: e.tensor_copy(out=Ln[:TS, :, 2, :], in_=Ln[:TS, :, 0, :]), reads=[("Ln1",)], writes=[("Ln2",)])
        P.add("dve", lambda e: e.scalar_tensor_tensor(out=Lnf, in0=Lnf, scalar=SCALE, in1=biasn, op0=ALU.mult, op1=ALU.add),
              reads=[("Ln2",), ("cst",)], writes=[("Lnb",)])
        P.add("act", lambda e: e.activation(out=Lnf, in_=Lnf, func=AF.Exp), reads=[("Lnb",)], writes=[("Lnp",)])
        P.add("dve", lambda e: e.tensor_reduce(out=Pn[:TS, :, :], in_=Ln[:TS, :, :, :].rearrange("p q c h -> p q h c"),
                                               axis=AX.X, op=ALU.add), reads=[("Lnp",)], writes=[("Pn",)])
        for h in range(8):
            bk, off = h // 3, (h % 3) * 129
            P.add("pe", lambda e, h=h, bk=bk, off=off: e.matmul(
                OS[bk][:TS, off:off + 129], lhsT=Pn[:TS, :, h], rhs=self.vsa[:TS, h, 0:129],
                start=False, stop=False, skip_group_check=True),
                reads=[("Pn",)], writes=[OSK[bk]])
        for h in range(8):
            bk, off = h // 3, (h % 3) * 129
            P.add("dve", lambda e, h=h, bk=bk, off=off: e.reciprocal(out=rd[:TS, h:h + 1], in_=OS[bk][:TS, off + 128:off + 129]),
                  reads=[OSK[bk]], writes=[("rd", h)])
            P.add("dve", lambda e, h=h, bk=bk, off=off: e.tensor_scalar(
                out=self.oas[:TS, h * 128:(h + 1) * 128], in0=OS[bk][:TS, off:off + 128], scalar1=rd[:TS, h:h + 1], scalar2=None,
                op0=ALU.mult), reads=[OSK[bk], ("rd", h)], writes=[("oas", h)])
        for h in range(4):
            bk, off = h // 3, (h % 3) * 129
            P.add("dve", lambda e, h=h, bk=bk, off=off: e.reciprocal(out=rd[:TS, 8 + h:9 + h], in_=OM[bk][:TS, off + 128:off + 129]),
                  reads=[OMK[bk]], writes=[("rd", 8 + h)])
            P.add("dve", lambda e, h=h, bk=bk, off=off: e.tensor_scalar(
                out=self.oas[:TS, 1536 + h * 128:1536 + (h + 1) * 128], in0=OM[bk][:TS, off:off + 128],
                scalar1=rd[:TS, 8 + h:9 + h], scalar2=None, op0=ALU.mult), reads=[OMK[bk], ("rd", 8 + h)], writes=[("oas", 12 + h)])
        ps, psk = self.psum()
        for g in range(4):
            P.add("pe", lambda e, ps=ps, g=g: e.matmul(
                ps[:TS, g * 128:(g + 1) * 128], lhsT=self.cst[:TS, C_ASGU + g * 16:C_ASGU + (g + 1) * 16],
                rhs=self.vns[:TS, g * 128:(g + 1) * 128], start=True, stop=True),
                reads=[("cst",)], writes=[psk])
        for g in range(4):
            P.add("dve", lambda e, ps=ps, g=g: e.scalar_tensor_tensor(
                out=self.oas[:TS, 1024 + g * 128:1024 + (g + 1) * 128], in0=ps[:TS, g * 128:(g + 1) * 128],
                scalar=self.cst[:TS, C_BSS + g:C_BSS + g + 1], in1=self.us[:TS, g * 128:(g + 1) * 128],
                op0=ALU.add, op1=ALU.mult), reads=[psk, ("cst",)], writes=[("oas", 8 + g)])
        ps, psk = self.psum()
        for c in range(NCH):
            P.add("pe", lambda e, ps=ps, c=c: e.transpose(out=ps[:, c * 16:(c + 1) * 16], in_=self.oas[:TS, c * 128:(c + 1) * 128],
                                                           identity=self.ident[:TS, :TS]),
                  reads=[("oas", c), ("ident",)], writes=[psk])
        P.add("dve", lambda e, ps=ps: e.tensor_copy(out=CAT[:, :, TP:T], in_=ps[:, 0:256].rearrange("p (c t) -> p c t", c=NCH)),
              reads=[psk], writes=[("CATs",)])
        self.ring = list(range(8))
        P.barrier()
        P.flush()

        wout = dn("wout", [D, D])
        w2g, w2u, w2d = dn("w2g", [D, DFF]), dn("w2u", [D, DFF]), dn("w2d", [DFF, D])
        al = Alloc(self, self.free_off)
        hb = [al([T], F32) for _ in range(2)]
        sg = [al([348], F32) for _ in range(3)]
        stg = [al([D], F32)]
        for (c0, c1, on) in ((0, 8, 1024), (8, 12, 512), (12, 16, 512)):
            self.rms_stats(lambda c, t0, n, c0=c0: CAT[:, c0 + c, t0:t0 + n], lambda c, ti: [], c1 - c0, on, MAIN, self.rstd, "rstd")
            for ti, (t0, n) in enumerate(MAIN.tiles):
                for c in range(c0, c1):
                    P.add("dve", lambda e, c=c, t0=t0, n=n: e.scalar_tensor_tensor(
                        out=CAT[:, c, t0:t0 + n], in0=CAT[:, c, t0:t0 + n], scalar=self.gains[:, 2, c:c + 1],
                        in1=self.rstd[:, t0:t0 + n], op0=ALU.mult, op1=ALU.mult),
                        reads=[("rstd", ti), ("gains",)], writes=[("CATn", c, ti)])
        for sidx in range(8):
            slab, sk = self.load_slab(wout, 0, D, sidx * 256)
            for j in range(2):
                m = sidx * 2 + j
                hi = self.nxt("hb", 2)
                hbt, hbk = hb[hi], ("hb", hi)
                P.add("sp", lambda e, hbt=hbt, m=m: e.dma_start(out=hbt, in_=hscr[m]), writes=[hbk], slot=("hb", hi))
                for ti, (t0, n) in enumerate(MAIN.tiles):
                    ps, psk = self.psum()
                    for ko in range(NCH):
                        P.add("pe", lambda e, ps=ps, slab=slab, ko=ko, j=j, t0=t0, n=n: e.matmul(
                            ps[:, :n], lhsT=slab[:, ko, j * 128:(j + 1) * 128], rhs=CAT[:, ko, t0:t0 + n],
                            start=(ko == 0), stop=(ko == NCH - 1)),
                            reads=[sk, ("CATn", ko, ti)], writes=[psk])
                    P.add("dve", lambda e, ps=ps, m=m, t0=t0, n=n, hbt=hbt: e.tensor_tensor(
                        out=R[:, m, t0:t0 + n], in0=ps[:, :n], in1=hbt[:, t0:t0 + n], op=ALU.add),
                        reads=[psk, hbk], writes=[("R", m, ti)])
        P.barrier()
        P.flush()
        self.rmsnorm_R(3, MAIN)
        self.ffn(w2g, w2u, w2d, MAIN, sg)
        self.transpose_store(lambda c, t0, nt: R[:, c, t0:t0 + nt],
                             lambda c: [("R", c, ti) for ti in range(3)] + [("R", c, "all")], NCH, T, y, stg)
        P.flush(final=True)


def build_nc(cfg=None):
    cfg = cfg or {}
    nc = bass.Bass("TRN2", target_bir_lowering=False)
    with contextlib.ExitStack() as stack:
        b = Builder(nc, stack, cfg)
        b.build()
    return nc


def _rel_bucket(dist):
    dist = np.asarray(dist, np.int64)
    max_exact = N_BUCKETS // 2
    large = max_exact + (np.log(np.maximum(dist, 1) / max_exact) / np.log(REL_MAX_DIST / max_exact)
                         * (N_BUCKETS - max_exact)).astype(np.int64)
    large = np.minimum(large, N_BUCKETS - 1)
    return np.where(dist < max_exact, dist, large).astype(np.int32)


def make_in_maps(inputs):
    f = lambda k: np.asarray(inputs[k], np.float32)
    xp, xsm = f("x_prompt"), f("x_sample")
    rel = f("rel_bias")
    ident = np.eye(128, dtype=np.float32)

    def gl(v):
        return np.ascontiguousarray(np.asarray(v, np.float32).reshape(NCH, 128).T)

    gains = np.ascontiguousarray(np.concatenate(
        [gl(f("g_ffn1")[0]), gl(f("g_mix")[0]), gl(f("g_mix_out")[0]), gl(f("g_ffn2")[0])], axis=1))
    g_mem = gl(f("g_mem")[0])
    i = np.arange(128)[:, None]
    pbias = np.empty((8, 128, 576), np.float32)
    for dil, lo in ((1, 0), (4, 256)):
        jj = np.arange(256)[None, :]
        step = jj - i
        valid = (step >= 0) & (step <= 128)
        bk = _rel_bucket(np.clip(step, 0, 128) * dil)
        for h in range(8):
            pbias[h, :, lo:lo + 256] = np.where(valid, rel[bk, h], NEG)
    jj = np.arange(64)[None, :]
    step = 64 + jj - i
    valid = step >= 0
    bk = _rel_bucket(np.clip(step, 0, 128) * 16)
    for h in range(8):
        pbias[h, :, 512:576] = np.where(valid, rel[bk, h], NEG)

    w_sgu = f("w_sgu")[0]
    b_sgu = f("b_sgu")[0]
    g_sgu = f("g_sgu")[0]
    maps = []
    for c in range(NCORES):
        b, half = c // 2, c % 2
        cst = np.zeros((128, NCST), np.float32)
        cst[:, C_GQA] = f("g_qa")[0]
        cst[:, C_GKA] = f("g_ka")[0]
        cst[:, C_GQM] = f("g_qm")[0]
        cst[:, C_GKM] = f("g_km")[0]
        for g in range(4):
            cst[:, C_GSGU + g] = g_sgu[g]
        if half == 0:
            cst[:, C_PM0] = NEG
            cst[:64, C_PM1] = NEG
        cst[:, C_TRIL:C_TRIL + 128] = np.tril(np.ones((128, 128), np.float32))
        cst[:, C_BSGU:C_BSGU + 512] = b_sgu.reshape(1, 512)
        for tok in range(16):
            cst[tok, C_BSS:C_BSS + 4] = b_sgu[:, tok % 4]
        for g in range(4):
            for tk in range(16):
                for tq in range(16):
                    if tk // 4 == tq // 4 and tk % 4 <= tq % 4:
                        cst[tk, C_ASGU + g * 16 + tq] = w_sgu[g, tq % 4, tk % 4]
        for t in range(4):
            cst[:, C_Z + t * 28 + 12 + t] = 1.0
        ii = np.arange(128)
        for t in range(4):
            for dl, dil in enumerate((1, 4, 16)):
                if dl == 0:
                    j = 128 + t - ii
                    valid = ii >= t
                else:
                    j = 128 - ii
                    valid = np.ones(128, bool)
                bk = _rel_bucket(np.clip(j, 0, 128) * dil)
                for h in range(8):
                    cst[:, C_BIASS + (t * 3 + dl) * 8 + h] = np.where(valid, rel[bk, h], NEG)
        b0 = _rel_bucket(np.array([0]))[0]
        for tk in range(16):
            for q in range(16):
                for cc in range(3):
                    col = C_BIASN + (q * 3 + cc) * 8
                    same = tk // 4 == q // 4
                    tk_, tq_ = tk % 4, q % 4
                    if cc == 0 and same and tk_ <= tq_:
                        cst[tk, col:col + 8] = rel[_rel_bucket(np.array([tq_ - tk_]))[0], :]
                    elif cc > 0 and same and tk_ == tq_:
                        cst[tk, col:col + 8] = rel[b0, :]
                    else:
                        cst[tk, col:col + 8] = NEG
        xs = np.concatenate([xp[b, half * TP:(half + 1) * TP], xsm[4 * c:4 * c + 4].reshape(TS, D)], axis=0)
        xprev = xp[b, 0:TP]
        maps.append({
            "xs": np.ascontiguousarray(xs), "xprev": np.ascontiguousarray(xprev), "mem": f("mem_prompt")[b],
            "w1g": f("w1_gate")[0], "w1u": f("w1_up")[0], "w1d": f("w1_down")[0],
            "w2g": f("w2_gate")[0], "w2u": f("w2_up")[0], "w2d": f("w2_down")[0],
            "win": f("w_in")[0], "wout": f("w_out")[0], "wmem": f("w_mem_kv")[0], "wsgu": w_sgu,
            "pbias": pbias,
            "cwk": f("cache_win_k")[0, 4 * c:4 * c + 4].reshape(4, 2048, 1024),
            "cwv": f("cache_win_v")[0, 4 * c:4 * c + 4].reshape(4, 2048, 1024),
            "cmk": f("cache_mem_k")[0, 4 * c:4 * c + 4].reshape(4, 256, 512),
            "cmv": f("cache_mem_v")[0, 4 * c:4 * c + 4].reshape(4, 256, 512),
            "ident": ident, "gains": gains, "cst": cst, "g_mem": g_mem,
        })
    return maps


def assemble(outs):
    n = len(outs)
    y_p = np.zeros((4, 2048, D), np.float32)
    y_s = np.zeros((32, 4, D), np.float32)
    wk_p = np.zeros((1, 4, 2048, 8, 128), np.float32)
    wv_p = np.zeros((1, 4, 2048, 8, 128), np.float32)
    mk_p = np.zeros((1, 4, 256, 4, 128), np.float32)
    mv_p = np.zeros((1, 4, 256, 4, 128), np.float32)
    wk_s = np.zeros((1, 32, 4, 8, 128), np.float32)
    wv_s = np.zeros((1, 32, 4, 8, 128), np.float32)
    cv_s = np.zeros((1, 32, 4, 4, 128), np.float32)
    for c in range(n):
        b, half = c // 2, c % 2
        o = outs[c]
        sl = slice(half * TP, (half + 1) * TP)
        y_p[b, sl] = o["y"][:TP]
        y_s[4 * c:4 * c + 4] = o["y"][TP:].reshape(4, 4, D)
        wk_p[0, b, sl] = o["o_wk"][:TP].reshape(TP, 8, 128)
        wv_p[0, b, sl] = o["o_wv"][:TP].reshape(TP, 8, 128)
        wk_s[0, 4 * c:4 * c + 4] = o["o_wk"][TP:].reshape(4, 4, 8, 128)
        wv_s[0, 4 * c:4 * c + 4] = o["o_wv"][TP:].reshape(4, 4, 8, 128)
        cv_s[0, 4 * c:4 * c + 4] = o["o_cv"].reshape(4, 4, 4, 128)
        if half == 0:
            mk_p[0, b] = o["o_mk"].reshape(256, 4, 128)
            mv_p[0, b] = o["o_mv"].reshape(256, 4, 128)
    return (y_p, y_s, wk_p, wv_p, mk_p, mv_p, wk_s, wv_s, cv_s)


def kernel(**inputs):
    nc = build_nc()
    maps = make_in_maps(inputs)
    res = run_bass_kernel_spmd(nc, maps, core_ids=list(range(NCORES)))
    return assemble(res.results)
```

```python
import contextlib
import numpy as np
import concourse.bass as bass
import concourse.mybir as mybir
from concourse.bass_utils import run_bass_kernel_spmd

F32 = mybir.dt.float32
BF16 = mybir.dt.bfloat16
AF = mybir.ActivationFunctionType
ALU = mybir.AluOpType
AX = mybir.AxisListType

NCORES = 8
D = 2048
NCH = 16
DFF = 5632
TP = 1024
TS = 16
T = TP + TS
EPS = 1e-6
SCALE = 128 ** -0.5
NEG = -30000.0
N_BUCKETS = 32
REL_MAX_DIST = 2048


class TokSet:
    def __init__(self, n, tiles):
        self.n = n
        self.tiles = tiles


MAIN = TokSet(T, [(0, 347), (347, 347), (694, 346)])
PREV = TokSet(TP, [(0, 342), (342, 341), (683, 341)])

C_GQA, C_GKA, C_GQM, C_GKM, C_GSGU, C_PM0, C_PM1, C_ZERO = 0, 1, 2, 3, 4, 8, 9, 10
C_TRIL = 12
C_BSGU = 140
C_BSS = 652
C_ASGU = 656
C_Z = 720
C_BIASS = 832
C_BIASN = 928
NCST = 1312


class Op:
    __slots__ = ("eng", "fn", "deps", "signal", "ticket", "dma", "slot")


class Prog:
    ENGS = ("pe", "act", "dve", "pool", "sp")

    def __init__(self, nc, stack):
        self.nc = nc
        self.stack = stack
        self.pending = []
        self.last_w = {}
        self.readers = {}
        self.eng_sem = {e: stack.enter_context(nc.semaphore("s_" + e)) for e in self.ENGS}
        self.eng_cnt = {e: 0 for e in self.ENGS}
        self.slot_sem = {}
        self.slot_cnt = {}
        self.waited = {e: {} for e in self.ENGS}
        self.frontier = {}
        self.barrier_ops = []

    def add(self, eng, fn, reads=(), writes=(), slot=None, after=()):
        op = Op()
        op.eng, op.fn, op.slot = eng, fn, slot
        op.dma = slot is not None
        op.signal = op.dma
        op.ticket = None
        deps = []
        for k in reads:
            w = self.last_w.get(k)
            if w is not None:
                deps.append(w)
        for k in writes:
            w = self.last_w.get(k)
            if w is not None:
                deps.append(w)
            deps.extend(self.readers.get(k, ()))
        deps.extend(after)
        deps.extend(self.barrier_ops)
        seen = set()
        op.deps = []
        for d in deps:
            if d is op or id(d) in seen:
                continue
            seen.add(id(d))
            if d.dma or d.eng != eng or eng != "pe":
                d.signal = True
                op.deps.append(d)
        for k in reads:
            self.readers.setdefault(k, []).append(op)
        for k in writes:
            self.last_w[k] = op
            self.readers[k] = []
        if op.dma:
            if slot not in self.slot_sem:
                self.slot_sem[slot] = self.stack.enter_context(self.nc.semaphore("d_" + str(len(self.slot_sem))))
                self.slot_cnt[slot] = 0
            self.slot_cnt[slot] += 16
            op.ticket = self.slot_cnt[slot]
            self.frontier[("slot", slot)] = op
        else:
            self.frontier[("eng", eng)] = op
        self.pending.append(op)
        return op

    def barrier(self):
        ops = list(self.frontier.values())
        for o in ops:
            o.signal = True
        self.barrier_ops = ops
        self.last_w = {}
        self.readers = {}

    def _sem_of(self, op):
        return self.slot_sem[op.slot] if op.dma else self.eng_sem[op.eng]

    def flush(self, final=False):
        nc = self.nc
        for op in self.pending:
            if not op.dma and op.signal:
                self.eng_cnt[op.eng] += 1
                op.ticket = self.eng_cnt[op.eng]
        per = {e: [o for o in self.pending if o.eng == e] for e in self.ENGS}

        def emit(ename, eng):
            waited = self.waited[ename]
            for op in per[ename]:
                for d in op.deps:
                    sem = self._sem_of(d)
                    key = id(sem)
                    if waited.get(key, 0) >= d.ticket:
                        continue
                    eng.wait_ge(sem, d.ticket)
                    waited[key] = d.ticket
                ins = op.fn(eng)
                if op.signal:
                    ins.then_inc(self._sem_of(op), 16 if op.dma else 1)
            if final and ename == "sp":
                for slot, sem in self.slot_sem.items():
                    if waited.get(id(sem), 0) < self.slot_cnt[slot]:
                        eng.wait_ge(sem, self.slot_cnt[slot])
                for e2 in self.ENGS:
                    if e2 != "sp" and self.eng_cnt[e2] > 0:
                        eng.wait_ge(self.eng_sem[e2], self.eng_cnt[e2])

        with nc.Block() as block:
            @block.tensor
            def _(e):
                emit("pe", e)

            @block.scalar
            def _(e):
                emit("act", e)

            @block.vector
            def _(e):
                emit("dve", e)

            @block.gpsimd
            def _(e):
                emit("pool", e)

            @block.sync
            def _(e):
                emit("sp", e)
        self.pending = []


class Alloc:
    def __init__(self, b, off):
        self.b = b
        self.off = off

    def __call__(self, shape, dt):
        n = 1
        for s in shape:
            n *= s
        words = n if dt == F32 else (n + 1) // 2
        v = self.b.view(self.off, shape, dt)
        self.off += words
        return v


def _run(gen):
    while True:
        try:
            next(gen)
        except StopIteration as st:
            return st.value


class Pipe:
    def __init__(self, lag=1):
        self.prev = None

    def add(self, front, back):
        fgen = front()
        bgen = self.prev[0](self.prev[1]) if self.prev else None
        res, fdone, bdone = None, False, bgen is None
        while not (fdone and bdone):
            if not fdone:
                try:
                    next(fgen)
                except StopIteration as st:
                    res, fdone = st.value, True
            if not bdone:
                try:
                    next(bgen)
                except StopIteration:
                    bdone = True
        self.prev = (back, res)

    def drain(self):
        if self.prev:
            _run(self.prev[0](self.prev[1]))
        self.prev = None


class PipeLag:
    def __init__(self, lag):
        self.lag = lag
        self.q = []

    def add(self, front, back):
        self.q.append((back, front()))
        if len(self.q) > self.lag:
            b, r = self.q.pop(0)
            b(r)

    def drain(self):
        while self.q:
            b, r = self.q.pop(0)
            b(r)


class Builder:
    NW = 52000

    def __init__(self, nc, stack, cfg):
        self.nc = nc
        self.stack = stack
        self.cfg = cfg
        self.P = Prog(nc, stack)
        self.psum_i = 0
        self.ring = list(range(8))
        self.ctr = {}

    def din(self, name, shape, dt=F32):
        return self.nc.dram_tensor(name, list(shape), dt, kind="ExternalInput").ap()

    def dout(self, name, shape, dt=F32):
        return self.nc.dram_tensor(name, list(shape), dt, kind="ExternalOutput").ap()

    def dscr(self, name, shape, dt):
        return self.nc.dram_tensor(name, list(shape), dt).ap()

    def psum(self):
        i = self.ring[self.psum_i % len(self.ring)]
        self.psum_i += 1
        return self.PS[i], ("ps", i)

    def nxt(self, name, n):
        v = self.ctr.get(name, 0)
        self.ctr[name] = v + 1
        return v % n

    def view(self, off, shape, dt):
        n = 1
        for s in shape:
            n *= s
        words = n if dt == F32 else (n + 1) // 2
        assert off + words <= self.NW, (off, words)
        a = self.arena[:, off:off + words]
        if dt != F32:
            a = a.bitcast(dt)[:, 0:n]
        if len(shape) == 2:
            a = a.rearrange("p (a b) -> p a b", a=shape[0])
        elif len(shape) == 3:
            a = a.rearrange("p (a b c) -> p a b c", a=shape[0], b=shape[1])
        return a

    def setup(self):
        nc, P, st = self.nc, self.P, self.stack
        self.PS = [st.enter_context(nc.psum_tensor("ps%d" % i, [128, 512], F32)) for i in range(8)]
        self.arena = st.enter_context(nc.sbuf_tensor("arena", [128, self.NW], F32))
        al = Alloc(self, 0)
        self.ident = al([128], F32)
        self.ones = {n: al([128], BF16) for n in (2048, 1024, 512, 128, 1)}
        self.gains = al([4, NCH], F32)
        self.cst = al([NCST], F32)
        self.epsc = al([2], F32)
        self.rstd = al([T], F32)
        self.sq = [al([348], BF16) for _ in range(3)]
        self.mkT = al([4, 256], BF16)
        self.mvtok = al([2, 512], BF16)
        self.WR_off = al.off
        self.WR = [al([NCH, 256], BF16) for _ in range(4)]
        self.BIGB_off = al.off
        self.BIGB = al([NCH * T], BF16)
        self.CAT = self.view(self.BIGB_off, [NCH, T], BF16)
        self.XN_off = al.off
        self.XN = al([NCH, T], BF16)
        self.R_off = al.off
        self.R = al([NCH, T], F32)
        self.free_off = al.off
        d_ident = self.din("ident", [128, 128])
        d_gains = self.din("gains", [128, 4 * NCH])
        d_cst = self.din("cst", [128, NCST])
        P.add("sp", lambda e: e.dma_start(out=self.ident, in_=d_ident), writes=[("ident",)], slot="c0")
        P.add("sp", lambda e: e.dma_start(out=self.gains.rearrange("p a c -> p (a c)"), in_=d_gains),
              writes=[("gains",)], slot="c1")
        P.add("sp", lambda e: e.dma_start(out=self.cst, in_=d_cst), writes=[("cst",)], slot="c2")
        for n in (2048, 1024, 512, 128, 1):
            P.add("pool", lambda e, n=n: e.memset(self.ones[n], 1.0 / n), writes=[("ones", n)])
        P.add("pool", lambda e: e.memset(self.epsc, EPS), writes=[("epsc",)])

    def ccol(self, c, n=1, rows=128):
        return self.cst[:rows, c:c + n]

    def load_slab(self, w, r0, nrows, c0, ncols=256):
        i = self.nxt("wr", 4)
        slab = self.WR[i]
        nk = nrows // 128
        self.P.add("pool", lambda e: e.dma_start(
            out=slab[:, :nk, :ncols], in_=w[r0:r0 + nrows, c0:c0 + ncols].rearrange("(k p) c -> p k c", p=128)),
            writes=[("WR", i)], slot=("WR", i))
        return slab, ("WR", i)

    def load_transpose(self, src, ntok, dst, dst_key, stg):
        P = self.P
        nblk = (ntok + 127) // 128
        for b in range(nblk):
            t0 = b * 128
            nt = min(128, ntok - t0)
            s = self.nxt("ltstg", len(stg))
            P.add("sp", lambda e, s=s, t0=t0, nt=nt: e.dma_start(out=stg[s][:nt, :], in_=src[t0:t0 + nt, :]),
                  writes=[("ltstg", s)], slot=("ltstg", s))
            for cg in range(4):
                ps, psk = self.psum()
                for j in range(4):
                    c = cg * 4 + j
                    P.add("pe", lambda e, ps=ps, s=s, c=c, j=j, nt=nt: e.transpose(
                        out=ps[:, j * 128:j * 128 + nt], in_=stg[s][:nt, c * 128:(c + 1) * 128],
                        identity=self.ident[:nt, :nt]),
                        reads=[("ltstg", s), ("ident",)], writes=[psk])
                src_v = lambda ps, nt: ps[:, :].rearrange("p (j t) -> p j t", j=4)[:, :, :nt]
                if cg % 2 == 0:
                    fn = lambda e, ps=ps, cg=cg, t0=t0, nt=nt: e.activation(
                        out=dst[:, cg * 4:cg * 4 + 4, t0:t0 + nt], in_=src_v(ps, nt), func=AF.Copy)
                    eng = "act"
                else:
                    fn = lambda e, ps=ps, cg=cg, t0=t0, nt=nt: e.tensor_copy(
                        out=dst[:, cg * 4:cg * 4 + 4, t0:t0 + nt], in_=src_v(ps, nt))
                    eng = "dve"
                P.add(eng, fn, reads=[psk], writes=[(dst_key, cg * 4 + j, "all") for j in range(4)])

    def rms_stats(self, src_fn, keys_fn, nchunks, ones_n, ts, rstd, rstd_key):
        P = self.P
        for ti, (t0, n) in enumerate(ts.tiles):
            ps, psk = self.psum()
            for c in range(nchunks):
                i = self.nxt("sq", 3)
                sq, sqk = self.sq[i], ("sq", i)
                P.add("act", lambda e, sq=sq, c=c, t0=t0, n=n: e.activation(
                    out=sq[:, :n], in_=src_fn(c, t0, n), func=AF.Square),
                    reads=keys_fn(c, ti), writes=[sqk])
                P.add("pe", lambda e, ps=ps, sq=sq, c=c, n=n: e.matmul(
                    ps[:, :n], lhsT=self.ones[ones_n], rhs=sq[:, :n], start=(c == 0), stop=(c == nchunks - 1)),
                    reads=[sqk, ("ones", ones_n)], writes=[psk])
            P.add("act", lambda e, ps=ps, t0=t0, n=n: e.activation(
                out=rstd[:, t0:t0 + n], in_=ps[:, :n], func=AF.Ln, bias=self.epsc[:, 0:1], scale=1.0),
                reads=[psk, ("epsc",)], writes=[(rstd_key, ti)])
            P.add("act", lambda e, t0=t0, n=n: e.activation(
                out=rstd[:, t0:t0 + n], in_=rstd[:, t0:t0 + n], func=AF.Exp, scale=-0.5),
                reads=[(rstd_key, ti)], writes=[(rstd_key, ti)])

    def rmsnorm_R(self, which, ts):
        P, R, XN = self.P, self.R, self.XN
        rk = lambda c, ti: [("R", c, ti), ("R", c, "all")]
        self.rms_stats(lambda c, t0, n: R[:, c, t0:t0 + n], rk, NCH, 2048, ts, self.rstd, "rstd")
        for ti, (t0, n) in enumerate(ts.tiles):
            for c in range(NCH):
                P.add("dve", lambda e, c=c, t0=t0, n=n: e.scalar_tensor_tensor(
                    out=XN[:, c, t0:t0 + n], in0=R[:, c, t0:t0 + n], scalar=self.gains[:, which, c:c + 1],
                    in1=self.rstd[:, t0:t0 + n], op0=ALU.mult, op1=ALU.mult),
                    reads=rk(c, ti) + [("rstd", ti), ("gains",)], writes=[("XN", c, ti)])

    def ffn(self, wg, wu, wd, ts, sg):
        P, XN, R = self.P, self.XN, self.R
        groups = [(0, 5), (5, 5), (10, 4), (14, 4), (18, 4)]
        ACTB = self.view(self.BIGB_off, [10, T], BF16)
        WD = [self.view(self.BIGB_off + 5 * T + i * 5 * 256, [10, 256], BF16) for i in range(2)]
        for g0, gn in groups:
            for s in range(g0, g0 + gn):
                c0 = s * 256
                sl_g, kg = self.load_slab(wg, 0, D, c0)
                sl_u, ku = self.load_slab(wu, 0, D, c0)
                for j in range(2):
                    fl = (s - g0) * 2 + j
                    for ti, (t0, n) in enumerate(ts.tiles):
                        pg, pgk = self.psum()
                        pu, puk = self.psum()
                        for ko in range(NCH):
                            P.add("pe", lambda e, pg=pg, sl=sl_g, ko=ko, j=j, t0=t0, n=n: e.matmul(
                                pg[:, :n], lhsT=sl[:, ko, j * 128:(j + 1) * 128], rhs=XN[:, ko, t0:t0 + n],
                                start=(ko == 0), stop=(ko == NCH - 1)),
                                reads=[kg, ("XN", ko, ti)], writes=[pgk])
                        for ko in range(NCH):
                            P.add("pe", lambda e, pu=pu, sl=sl_u, ko=ko, j=j, t0=t0, n=n: e.matmul(
                                pu[:, :n], lhsT=sl[:, ko, j * 128:(j + 1) * 128], rhs=XN[:, ko, t0:t0 + n],
                                start=(ko == 0), stop=(ko == NCH - 1)),
                                reads=[ku, ("XN", ko, ti)], writes=[puk])
                        i = self.nxt("sg", 3)
                        sgt, sgk = sg[i], ("sg", i)
                        P.add("act", lambda e, sgt=sgt, pg=pg, n=n: e.activation(out=sgt[:, :n], in_=pg[:, :n], func=AF.Silu),
                              reads=[pgk], writes=[sgk])
                        P.add("dve", lambda e, sgt=sgt, pu=pu, fl=fl, t0=t0, n=n: e.tensor_tensor(
                            out=ACTB[:, fl, t0:t0 + n], in0=sgt[:, :n], in1=pu[:, :n], op=ALU.mult),
                            reads=[sgk, puk], writes=[("ACTB", fl, ti)])
            nk = gn * 2
            r0 = g0 * 256
            for ds in range(8):
                slot = self.nxt("wd", 2)
                c0 = ds * 256
                P.add("pool", lambda e, slot=slot, c0=c0, r0=r0, nk=nk: e.dma_start(
                    out=WD[slot][:, :nk, :], in_=wd[r0:r0 + nk * 128, c0:c0 + 256].rearrange("(k p) c -> p k c", p=128)),
                    writes=[("WD", slot)], slot=("WD", slot))
                for j in range(2):
                    m = ds * 2 + j
                    for ti, (t0, n) in enumerate(ts.tiles):
                        pd, pdk = self.psum()
                        for k in range(nk):
                            P.add("pe", lambda e, pd=pd, slot=slot, k=k, j=j, t0=t0, n=n, nk=nk: e.matmul(
                                pd[:, :n], lhsT=WD[slot][:, k, j * 128:(j + 1) * 128], rhs=ACTB[:, k, t0:t0 + n],
                                start=(k == 0), stop=(k == nk - 1)),
                                reads=[("WD", slot), ("ACTB", k, ti)], writes=[pdk])
                        P.add("dve", lambda e, pd=pd, m=m, t0=t0, n=n: e.scalar_tensor_tensor(
                            out=R[:, m, t0:t0 + n], in0=pd[:, :n], scalar=0.5, in1=R[:, m, t0:t0 + n],
                            op0=ALU.mult, op1=ALU.add),
                            reads=[pdk, ("R", m, ti), ("R", m, "all")], writes=[("R", m, ti)])

    def transpose_store(self, src_fn, keys_fn, nchunks, ntok, dst, stg):
        P = self.P
        nblk = (ntok + 127) // 128
        ncg = (nchunks + 3) // 4
        for b in range(nblk):
            t0 = b * 128
            nt = min(128, ntok - t0)
            s = self.nxt("tsstg", len(stg))
            for cg in range(ncg):
                ps, psk = self.psum()
                nj = min(4, nchunks - cg * 4)
                for j in range(nj):
                    c = cg * 4 + j
                    P.add("pe", lambda e, ps=ps, c=c, j=j, t0=t0, nt=nt: e.transpose(
                        out=ps[:nt, j * 128:(j + 1) * 128], in_=src_fn(c, t0, nt), identity=self.ident),
                        reads=keys_fn(c) + [("ident",)], writes=[psk])
                if cg % 2 == 0:
                    fn = lambda e, ps=ps, cg=cg, nt=nt, s=s, nj=nj: e.activation(
                        out=stg[s][:nt, cg * 512:cg * 512 + nj * 128], in_=ps[:nt, :nj * 128], func=AF.Copy)
                    eng = "act"
                else:
                    fn = lambda e, ps=ps, cg=cg, nt=nt, s=s, nj=nj: e.tensor_copy(
                        out=stg[s][:nt, cg * 512:cg * 512 + nj * 128], in_=ps[:nt, :nj * 128])
                    eng = "dve"
                P.add(eng, fn, reads=[psk, ("tsstg", s)], writes=[("tsstg", s, cg)])
            P.add("sp", lambda e, s=s, t0=t0, nt=nt: e.dma_start(out=dst[t0:t0 + nt, :], in_=stg[s][:nt, :nchunks * 128]),
                  reads=[("tsstg", s, cg) for cg in range(ncg)], writes=[("tsstg", s)], slot=("tsstg", s))

    def g_project_chunk(self, slab, slabk, j, xin, xkey, ts, zT, zk):
        P = self.P
        for ti, (t0, n) in enumerate(ts.tiles):
            ps, psk = self.psum()
            for ko in range(NCH):
                P.add("pe", lambda e, ps=ps, ko=ko, t0=t0, n=n: e.matmul(
                    ps[:, :n], lhsT=slab[:, ko, j * 128:(j + 1) * 128], rhs=xin[:, ko, t0:t0 + n],
                    start=(ko == 0), stop=(ko == NCH - 1)),
                    reads=[slabk, (xkey, ko, ti)], writes=[psk])
                if ko == 7:
                    yield
            P.add("act", lambda e, ps=ps, t0=t0, n=n: e.activation(out=zT[:, t0:t0 + n], in_=ps[:, :n], func=AF.Copy),
                  reads=[psk], writes=[(zk, ti)])
            yield

    def project_chunk(self, *a):
        _run(self.g_project_chunk(*a))

    def g_chunk_norm(self, zT, zk, gcol, ts, out_f32=None, out_bf=None, ob_key=None, ob_keyfn=None):
        P = self.P
        rstd = self.rstd
        pss = []
        for ti, (t0, n) in enumerate(ts.tiles):
            ps, psk = self.psum()
            pss.append((ps, psk))
            i = self.nxt("sq", 3)
            sq, sqk = self.sq[i], ("sq", i)
            P.add("act", lambda e, sq=sq, t0=t0, n=n: e.activation(out=sq[:, :n], in_=zT[:, t0:t0 + n], func=AF.Square),
                  reads=[(zk, ti)], writes=[sqk])
            P.add("pe", lambda e, ps=ps, sq=sq, n=n: e.matmul(ps[:, :n], lhsT=self.ones[128], rhs=sq[:, :n], start=True, stop=True),
                  reads=[sqk, ("ones", 128)], writes=[psk])
        yield
        for ti, (t0, n) in enumerate(ts.tiles):
            ps, psk = pss[ti]
            P.add("act", lambda e, ps=ps, t0=t0, n=n: e.activation(
                out=rstd[:, t0:t0 + n], in_=ps[:, :n], func=AF.Ln, bias=self.epsc[:, 0:1], scale=1.0),
                reads=[psk, ("epsc",)], writes=[("rstd", ti)])
        for ti, (t0, n) in enumerate(ts.tiles):
            P.add("act", lambda e, t0=t0, n=n: e.activation(
                out=rstd[:, t0:t0 + n], in_=rstd[:, t0:t0 + n], func=AF.Exp, scale=-0.5),
                reads=[("rstd", ti)], writes=[("rstd", ti)])
        yield
        for ti, (t0, n) in enumerate(ts.tiles):
            if out_bf is not None:
                wk = [ob_keyfn(ti)] if ob_keyfn else [(ob_key, ti)]
                P.add("dve", lambda e, t0=t0, n=n: e.scalar_tensor_tensor(
                    out=out_bf[:, t0:t0 + n], in0=zT[:, t0:t0 + n], scalar=gcol, in1=rstd[:, t0:t0 + n],
                    op0=ALU.mult, op1=ALU.mult), reads=[(zk, ti), ("rstd", ti), ("cst",)], writes=wk)
            if out_f32 is not None:
                P.add("dve", lambda e, t0=t0, n=n: e.scalar_tensor_tensor(
                    out=out_f32[:, t0:t0 + n], in0=zT[:, t0:t0 + n], scalar=gcol, in1=rstd[:, t0:t0 + n],
                    op0=ALU.mult, op1=ALU.mult), reads=[(zk, ti), ("rstd", ti), ("cst",)], writes=[(zk, ti)])
        yield

    def chunk_norm(self, *a, **k):
        _run(self.g_chunk_norm(*a, **k))

    def g_to_tokmajor(self, zT, zkeys, ntok, stgt, stk):
        P = self.P
        nblk = (ntok + 127) // 128
        for b0 in range(0, nblk, 4):
            ps, psk = self.psum()
            nb = min(4, nblk - b0)
            for j in range(nb):
                t0 = (b0 + j) * 128
                nt = min(128, ntok - t0)
                P.add("pe", lambda e, ps=ps, j=j, t0=t0, nt=nt: e.transpose(
                    out=ps[:nt, j * 128:(j + 1) * 128], in_=zT[:, t0:t0 + nt], identity=self.ident),
                    reads=zkeys + [("ident",)], writes=[psk])
            nfull = sum(1 for j in range(nb) if (b0 + j) * 128 + 128 <= ntok)
            if nfull:
                P.add("act", lambda e, ps=ps, b0=b0, nfull=nfull: e.activation(
                    out=stgt[:, b0:b0 + nfull, :], in_=ps[:, :nfull * 128].rearrange("p (j f) -> p j f", j=nfull),
                    func=AF.Copy), reads=[psk, (stk, "dma"), (stk, "dma2")], writes=[(stk, "f", b0)])
            if nfull < nb:
                j = nfull
                nt = ntok - (b0 + j) * 128
                P.add("act", lambda e, ps=ps, b0=b0, j=j, nt=nt: e.activation(
                    out=stgt[:nt, b0 + j, :], in_=ps[:nt, j * 128:(j + 1) * 128], func=AF.Copy),
                    reads=[psk, (stk, "dma"), (stk, "dma2")], writes=[(stk, "p")])
            yield

    def to_tokmajor(self, *a):
        _run(self.g_to_tokmajor(*a))

    def stk_keys(self, stk, ntok):
        nblk = (ntok + 127) // 128
        ks = [(stk, "f", b0) for b0 in range(0, nblk, 4) if b0 * 128 + 128 <= ntok]
        if ntok % 128:
            ks.append((stk, "p"))
        return ks

    def build(self):
        nc, P, cfg = self.nc, self.P, self.cfg
        dn = self.din
        mem = dn("mem", [256, D])
        wmem = dn("wmem", [D, 1024])
        gm = dn("g_mem", [128, NCH])
        y = self.dout("y", [T, D])
        o_wk, o_wv = self.dout("o_wk", [T, 1024]), self.dout("o_wv", [T, 1024])
        o_mk, o_mv = self.dout("o_mk", [256, 512]), self.dout("o_mv", [256, 512])
        o_cv = self.dout("o_cv", [TS, 512])
        kvp_k = self.dscr("kvp_k", [1024, TP], BF16)
        kvp_v = self.dscr("kvp_v", [TP, 1024], BF16)
        kvl_k = self.dscr("kvl_k", [1024, TP], BF16)
        kvl_v = self.dscr("kvl_v", [TP, 1024], BF16)
        hscr = self.dscr("hscr", [NCH, 128, T], F32)
        qscr = self.dscr("qscr", [TS, 1024], F32)
        qmscr = self.dscr("qmscr", [TS, 512], F32)
        self.setup()
        R, XN = self.R, self.XN

        al = Alloc(self, self.R_off)
        stg = [al([D], F32), al([D], F32)]
        memT = self.view(self.XN_off, [NCH, 256], F32)
        memN = al([NCH, 256], BF16)
        zTm = [al([256], F32) for _ in range(2)]
        stgm = [al([2, 128], F32) for _ in range(2)]
        gmem = al([NCH], F32)
        MT = TokSet(256, [(0, 256)])
        self.load_transpose(mem, 256, memT, "memT", stg)
        mk_ = lambda c, ti: [("memT", c, "all")]
        self.rms_stats(lambda c, t0, n: memT[:, c, t0:t0 + n], mk_, NCH, 2048, MT, self.rstd, "rstd")
        P.add("sp", lambda e: e.dma_start(out=gmem, in_=gm), writes=[("gmem",)], slot="c3")
        for c in range(NCH):
            P.add("dve", lambda e, c=c: e.scalar_tensor_tensor(
                out=memN[:, c, :], in0=memT[:, c, :], scalar=gmem[:, c:c + 1], in1=self.rstd[:, 0:256],
                op0=ALU.mult, op1=ALU.mult), reads=[("memT", c, "all"), ("rstd", 0), ("gmem",)], writes=[("memN", c, 0)])
        for sidx in range(4):
            slab, sk = self.load_slab(wmem, 0, D, sidx * 256)
            for j in range(2):
                hc = sidx * 2 + j
                zi = self.nxt("zTm", 2)
                z, zk = zTm[zi], ("zTm", zi)
                self.project_chunk(slab, sk, j, memN, "memN", MT, z, zk)
                si = self.nxt("stgm", 2)
                sm, smk = stgm[si], ("stgm", si)
                if hc < 4:
                    self.chunk_norm(z, zk, self.ccol(C_GKM), MT, out_f32=z, out_bf=self.mkT[:, hc, :], ob_key=("mkT", hc))
                    self.to_tokmajor(z, [(zk, 0)], 256, sm, smk)
                    P.add("sp", lambda e, sm=sm, hc=hc: e.dma_start(
                        out=o_mk[:, hc * 128:(hc + 1) * 128].rearrange("(b p) c -> p b c", p=128), in_=sm),
                        reads=self.stk_keys(smk, 256), writes=[(smk, "dma")], slot=("stgm", si))
                else:
                    hv = hc - 4
                    self.to_tokmajor(z, [(zk, 0)], 256, sm, smk)
                    P.add("sp", lambda e, sm=sm, hv=hv: e.dma_start(
                        out=o_mv[:, hv * 128:(hv + 1) * 128].rearrange("(b p) c -> p b c", p=128), in_=sm),
                        reads=self.stk_keys(smk, 256), writes=[(smk, "dma")], slot=("stgm", si))
                    P.add("dve", lambda e, sm=sm, hv=hv: e.tensor_copy(out=self.mvtok[:, :, hv * 128:(hv + 1) * 128], in_=sm),
                          reads=self.stk_keys(smk, 256), writes=[("mvtok", hv)])
        P.barrier()
        P.flush()
        if cfg.get("stop") == "M":
            P.flush(final=True)
            return

        xs = dn("xs", [T, D])
        w1g, w1u, w1d = dn("w1g", [D, DFF]), dn("w1u", [D, DFF]), dn("w1d", [DFF, D])
        win = dn("win", [D, 4608])
        wsgu = dn("wsgu", [4, 128, 128])

        def ffn1_pass(xsrc, ts):
            al = Alloc(self, self.free_off)
            stg = [al([D], F32), al([D], F32)]
            sg = [al([348], F32) for _ in range(3)]
            self.load_transpose(xsrc, ts.n, R, "R", stg)
            self.rmsnorm_R(0, ts)
            self.ffn(w1g, w1u, w1d, ts, sg)
            self.rmsnorm_R(1, ts)

        def kv_project(ts, dk, dv, main, al, zTs=None, stgs=None):
            if zTs is None:
                zTs = [al([T], F32) for _ in range(3)]
                stgs = [al([9, 128], F32) for _ in range(2)]
            nzt_ = len(zTs)
            nbs = [al([T], BF16) for _ in range(2)]
            vbs = [al([8, 128], BF16) for _ in range(2)]
            pipe = Pipe(1)
            slabs = {}

            def front(hc):
                sidx, j = hc // 2, hc % 2
                if j == 0:
                    slabs[sidx] = self.load_slab(win, 0, D, 1024 + sidx * 256)
                slab, sk = slabs[sidx]
                zi = self.nxt("zT%d" % nzt_, nzt_)
                z, zk = zTs[zi], ("zT", zi)
                yield from self.g_project_chunk(slab, sk, j, XN, "XN", ts, z, zk)
                return (hc, z, zk)

            def back(r):
                hc, z, zk = r
                si = self.nxt("stgs", 2)
                sm, smk = stgs[si], ("stgs", si)
                if hc < 8:
                    ni = self.nxt("nb", 2)
                    nb, nbk = nbs[ni], ("nb", ni)
                    yield from self.g_chunk_norm(z, zk, self.ccol(C_GKA), ts, out_f32=(z if main else None), out_bf=nb, ob_key=nbk)
                    P.add("sp", lambda e, nb=nb, hc=hc: e.dma_start(out=dk[hc * 128:(hc + 1) * 128, :], in_=nb[:, 0:TP]),
                          reads=[(nbk, ti) for ti in range(3)], writes=[(nbk, ti) for ti in range(3)], slot=("nb", ni))
                    if main:
                        yield from self.g_to_tokmajor(z, [(zk, ti) for ti in range(3)], T, sm, smk)
                        P.add("sp", lambda e, sm=sm, hc=hc: e.dma_start(
                            out=o_wk[0:TP, hc * 128:(hc + 1) * 128].rearrange("(b p) c -> p b c", p=128), in_=sm[:, 0:8, :]),
                            reads=self.stk_keys(smk, T), writes=[(smk, "dma")], slot=("stgs", si))
                        P.add("sp", lambda e, sm=sm, hc=hc: e.dma_start(
                            out=o_wk[TP:T, hc * 128:(hc + 1) * 128], in_=sm[:TS, 8, :]),
                            reads=self.stk_keys(smk, T), writes=[(smk, "dma2")], slot=("stgs2", si))
                        P.add("dve", lambda e, sm=sm, hc=hc: e.tensor_copy(out=self.ks[:TS, hc * 128:(hc + 1) * 128], in_=sm[:TS, 8, :]),
                              reads=self.stk_keys(smk, T), writes=[("ks", hc)])
                else:
                    hv = hc - 8
                    yield from self.g_to_tokmajor(z, [(zk, ti) for ti in range(3)], ts.n, sm, smk)
                    vi = self.nxt("vb", 2)
                    vb, vbk = vbs[vi], ("vb", vi)
                    P.add("dve", lambda e, sm=sm, vb=vb: e.tensor_copy(out=vb, in_=sm[:, 0:8, :]),
                          reads=self.stk_keys(smk, ts.n), writes=[vbk])
                    P.add("sp", lambda e, vb=vb, hv=hv: e.dma_start(
                        out=dv[:, hv * 128:(hv + 1) * 128].rearrange("(b p) c -> p b c", p=128), in_=vb),
                        reads=[vbk], writes=[vbk], slot=("vb", vi))
                    if main:
                        P.add("sp", lambda e, sm=sm, hv=hv: e.dma_start(
                            out=o_wv[0:TP, hv * 128:(hv + 1) * 128].rearrange("(b p) c -> p b c", p=128), in_=sm[:, 0:8, :]),
                            reads=self.stk_keys(smk, T), writes=[(smk, "dma")], slot=("stgs", si))
                        P.add("sp", lambda e, sm=sm, hv=hv: e.dma_start(
                            out=o_wv[TP:T, hv * 128:(hv + 1) * 128], in_=sm[:TS, 8, :]),
                            reads=self.stk_keys(smk, T), writes=[(smk, "dma2")], slot=("stgs2", si))
                        P.add("dve", lambda e, sm=sm, hv=hv: e.tensor_copy(out=self.vsa[:TS, hv, 0:128], in_=sm[:TS, 8, :]),
                              reads=self.stk_keys(smk, T), writes=[("vsa", hv)])
                yield

            for hc in range(16):
                pipe.add(lambda hc=hc: front(hc), back)
            pipe.drain()

        if cfg.get("pass0", True):
            xprev = dn("xprev", [TP, D])
            ffn1_pass(xprev, PREV)
            kv_project(PREV, kvp_k, kvp_v, False, Alloc(self, self.BIGB_off))
            P.barrier()
            P.flush()

        ffn1_pass(xs, MAIN)
        for c in range(NCH):
            P.add("sp", lambda e, c=c: e.dma_start(out=hscr[c], in_=R[:, c, :]),
                  reads=[("R", c, ti) for ti in range(3)] + [("R", c, "all")], writes=[("hscr", c)], slot=("hs", c % 2))
        P.barrier()
        P.flush()

        CAT = self.CAT
        al = Alloc(self, self.R_off)
        self.qs = al([1024], F32)
        self.ks = al([1024], F32)
        self.vsa = al([8, 130], F32)
        self.us = al([512], F32)
        self.vns = al([512], F32)
        self.qms = al([512], F32)
        self.oas = al([2048], F32)
        samp_off = al.off
        P.add("pool", lambda e: e.memset(self.vsa[:TS, :, 128:130], 1.0), writes=[("vsa1",)])
        zTs = [al([T], F32) for _ in range(4)]
        stgs = [al([9, 128], F32) for _ in range(2)]
        WT = al([4, 128], BF16)
        wtmp = al([4, 128], F32)
        for g in range(4):
            P.add("sp", lambda e, g=g: e.dma_start(out=wtmp[:, g, :], in_=wsgu[g]), writes=[("wtmp", g)], slot=("wtmp", g))
        for g in range(4):
            P.add("dve", lambda e, g=g: e.tensor_tensor(out=wtmp[:, g, :], in0=wtmp[:, g, :], in1=self.cst[:, C_TRIL:C_TRIL + 128], op=ALU.mult),
                  reads=[("wtmp", g), ("cst",)], writes=[("wtmp", g)])
            ps, psk = self.psum()
            P.add("pe", lambda e, ps=ps, g=g: e.transpose(out=ps[:, 0:128], in_=wtmp[:, g, :], identity=self.ident),
                  reads=[("wtmp", g), ("ident",)], writes=[psk])
            P.add("act", lambda e, ps=ps, g=g: e.activation(out=WT[:, g, :], in_=ps[:, 0:128], func=AF.Copy),
                  reads=[psk], writes=[("WT", g)])

        kv_project(MAIN, kvl_k, kvl_v, True, al, zTs, stgs)
        vtok = al([8, 128], BF16)
        obf = al([TP], F32)
        Pm = [al([512], BF16) for _ in range(2)]
        rden = al([512], F32)
        qmn = al([T], BF16)

        def tiny_T(zsrc, zkeys, dst_fn, dkey):
            ps, psk = self.psum()
            P.add("pe", lambda e, ps=ps: e.transpose(out=ps[:TS, 0:128], in_=zsrc[:, TP:T], identity=self.ident),
                  reads=zkeys + [("ident",)], writes=[psk])
            P.add("act", lambda e, ps=ps: e.activation(out=dst_fn(), in_=ps[:TS, 0:128], func=AF.Copy),
                  reads=[psk], writes=[dkey])

        pipeA = Pipe(1)
        nzt = len(zTs)

        def projA(col0, ncols, j, slabref):
            if slabref[0] is None:
                slabref[0] = self.load_slab(win, 0, D, col0, ncols)
            slab, sk = slabref[0]
            zi = self.nxt("zT%d" % nzt, nzt)
            z, zk = zTs[zi], ("zT", zi)
            yield from self.g_project_chunk(slab, sk, j, XN, "XN", MAIN, z, zk)
            return z, zk

        ustate = {}
        for g in range(4):
            def back_u(r, g=g):
                u, uk = r
                ustate[g] = (u, uk)
                tiny_T(u, [(uk, 2)], lambda g=g: self.us[:TS, g * 128:(g + 1) * 128], ("us", g))
                yield

            def back_v(r, g=g):
                v, vk = r
                u, uk = ustate[g]
                yield from self.g_chunk_norm(v, vk, self.ccol(C_GSGU + g), MAIN, out_f32=v)
                si = self.nxt("stgs", 2)
                sm, smk = stgs[si], ("stgs", si)
                yield from self.g_to_tokmajor(v, [(vk, ti) for ti in range(3)], T, sm, smk)
                P.add("dve", lambda e, sm=sm: e.tensor_copy(out=vtok, in_=sm[:, 0:8, :]),
                      reads=self.stk_keys(smk, T), writes=[("vtok",)])
                P.add("dve", lambda e, sm=sm, g=g: e.tensor_copy(out=self.vns[:TS, g * 128:(g + 1) * 128], in_=sm[:TS, 8, :]),
                      reads=self.stk_keys(smk, T), writes=[("vns", g)])
                for half in range(2):
                    ps, psk = self.psum()
                    for j in range(4):
                        n = half * 4 + j
                        P.add("pe", lambda e, ps=ps, n=n, j=j, g=g: e.matmul(
                            ps[:, j * 128:(j + 1) * 128], lhsT=vtok[:, n, :], rhs=WT[:, g, :], start=True, stop=True),
                            reads=[("vtok",), ("WT", g)], writes=[psk])
                    q0 = half * 512
                    P.add("dve", lambda e, ps=ps, g=g, q0=q0: e.tensor_tensor(
                        out=obf[:, q0:q0 + 512].rearrange("p (j t) -> p j t", j=4),
                        in0=ps[:, :].rearrange("p (j t) -> p j t", j=4),
                        in1=self.cst[:, C_BSGU + g * 128:C_BSGU + (g + 1) * 128].unsqueeze(1).to_broadcast([128, 4, 128]),
                        op=ALU.add), reads=[psk, ("cst",)], writes=[("obf", half)])
                    P.add("dve", lambda e, g=g, q0=q0, u=u: e.tensor_tensor(
                        out=CAT[:, 8 + g, q0:q0 + 512], in0=obf[:, q0:q0 + 512], in1=u[:, q0:q0 + 512], op=ALU.mult),
                        reads=[("obf", half)] + [(uk, ti) for ti in range(3)], writes=[("CAT", 8 + g, "p", half)])
                    yield

            pipeA.add(lambda g=g: projA(3072 + g * 128, 128, 0, [None]), back_u)
            pipeA.add(lambda g=g: projA(3584 + g * 128, 128, 0, [None]), back_v)

        for hm in range(4):
            def back_qm(r, hm=hm):
                z, zk = r
                yield from self.g_chunk_norm(z, zk, self.ccol(C_GQM), MAIN, out_f32=z, out_bf=qmn, ob_key="qmn")
                tiny_T(z, [(zk, 2)], lambda hm=hm: self.qms[:TS, hm * 128:(hm + 1) * 128], ("qms", hm))
                for qt in range(2):
                    q0 = qt * 512
                    po, pok = self.psum()
                    pdn, pdk = self.psum()
                    pms = []
                    for mt in range(2):
                        ps, psk = self.psum()
                        P.add("pe", lambda e, ps=ps, mt=mt, hm=hm, q0=q0: e.matmul(
                            ps[:, :], lhsT=self.mkT[:, hm, mt * 128:(mt + 1) * 128], rhs=qmn[:, q0:q0 + 512], start=True, stop=True),
                            reads=[("qmn", ti) for ti in range(3)], writes=[psk])
                        pi = self.nxt("Pm", 2)
                        pm_, pmk = Pm[pi], ("Pm", pi)
                        P.add("act", lambda e, ps=ps, pm_=pm_: e.activation(out=pm_, in_=ps[:, :], func=AF.Exp, scale=SCALE),
                              reads=[psk], writes=[pmk])
                        pms.append((pm_, pmk))
                    yield
                    for mt in range(2):
                        pm_, pmk = pms[mt]
                        P.add("pe", lambda e, po=po, pm_=pm_, mt=mt, hm=hm: e.matmul(
                            po[:, :], lhsT=self.mvtok[:, mt, hm * 128:(hm + 1) * 128], rhs=pm_, start=(mt == 0), stop=(mt == 1)),
                            reads=[pmk], writes=[pok])
                        P.add("pe", lambda e, pdn=pdn, pm_=pm_, mt=mt: e.matmul(
                            pdn[:, :], lhsT=self.ones[1], rhs=pm_, start=(mt == 0), stop=(mt == 1)),
                            reads=[pmk], writes=[pdk])
                    P.add("act", lambda e, pdn=pdn: e.activation(out=rden, in_=pdn[:, :], func=AF.Ln), reads=[pdk], writes=[("rden",)])
                    P.add("act", lambda e: e.activation(out=rden, in_=rden, func=AF.Exp, scale=-1.0), reads=[("rden",)], writes=[("rden",)])
                    P.add("dve", lambda e, po=po, hm=hm, q0=q0: e.tensor_tensor(
                        out=CAT[:, 12 + hm, q0:q0 + 512], in0=po[:, :], in1=rden, op=ALU.mult),
                        reads=[pok, ("rden",)], writes=[("CAT", 12 + hm, "p", qt)])
                    yield

            pipeA.add(lambda hm=hm: projA(4096 + hm * 128, 128, 0, [None]), back_qm)

        qslab = {}
        for h in range(8):
            def back_q(r, h=h):
                z, zk = r
                yield from self.g_chunk_norm(z, zk, self.ccol(C_GQA), MAIN, out_f32=z, out_bf=CAT[:, h, :],
                                             ob_keyfn=lambda ti, h=h: ("QN", h, ti))
                tiny_T(z, [(zk, 2)], lambda h=h: self.qs[:TS, h * 128:(h + 1) * 128], ("qs", h))
                yield

            ref = qslab.setdefault(h // 2, [None])
            pipeA.add(lambda h=h, ref=ref: projA((h // 2) * 256, 256, h % 2, ref), back_q)
        pipeA.drain()
        P.add("sp", lambda e: e.dma_start(out=o_cv, in_=self.vns[:TS, :]), reads=[("vns", g) for g in range(4)],
              writes=[("ocv",)], slot="ocv")
        P.add("sp", lambda e: e.dma_start(out=qscr, in_=self.qs[:TS, :]), reads=[("qs", h) for h in range(8)],
              writes=[("qscr",)], slot="qscr")
        P.add("sp", lambda e: e.dma_start(out=qmscr, in_=self.qms[:TS, :]), reads=[("qms", h) for h in range(4)],
              writes=[("qmscr",)], slot="qmscr")
        P.barrier()
        P.flush()
        if cfg.get("stop") == "A":
            P.flush(final=True)
            return

        pbias = dn("pbias", [8, 128, 576])
        al = Alloc(self, samp_off)
        KT = [al([2048], BF16) for _ in range(2)]
        V1 = [al([9, 128], BF16) for _ in range(2)]
        V4 = [al([3, 4, 128], BF16) for _ in range(2)]
        V16 = [al([16, 128], BF16) for _ in range(2)]
        PB = [al([576], F32) for _ in range(2)]
        tmp = [al([256], F32) for _ in range(3)]
        Pt = [al([256], BF16) for _ in range(6)]
        rden = al([TP], F32)
        self.ring = [0, 1, 2, 3]
        OB = [self.PS[4], self.PS[5]]
        DB = [self.PS[6], self.PS[7]]
        OK_ = [("ps", 4), ("ps", 5)]
        DK_ = [("ps", 6), ("ps", 7)]
        zc = self.ccol(C_ZERO)
        pipeB = PipeLag(3)

        def score(lhsT_ap, rhs_ap, bias_ap, pmcol, width, ldk, qk):
            ps, psk = self.psum()
            P.add("pe", lambda e, ps=ps: e.matmul(ps[:, 0:width], lhsT=lhsT_ap, rhs=rhs_ap, start=True, stop=True),
                  reads=ldk + qk, writes=[psk])
            ti_ = self.nxt("tmpB", 3)
            tm, tmk = tmp[ti_], ("tmpB", ti_)
            P.add("dve", lambda e, ps=ps, tm=tm: e.scalar_tensor_tensor(
                out=tm[:, :width], in0=ps[:, :width], scalar=SCALE, in1=bias_ap, op0=ALU.mult, op1=ALU.add),
                reads=[psk] + ldk, writes=[tmk])
            pi = self.nxt("PtB", 6)
            pt, ptk = Pt[pi], ("PtB", pi)
            P.add("act", lambda e, tm=tm, pt=pt: e.activation(out=pt[:, :width], in_=tm[:, :width], func=AF.Exp,
                                                               bias=pmcol, scale=1.0),
                  reads=[tmk], writes=[ptk])
            return pt, ptk

        def pv(pt, ptk, poff, w, vl_ap, bank, out_sl, ldk):
            P.add("pe", lambda e: e.matmul(OB[bank][:, out_sl], lhsT=vl_ap, rhs=pt[:, poff:poff + w],
                                           start=False, stop=False, skip_group_check=True),
                  reads=[ptk] + ldk, writes=[OK_[bank]])
            P.add("pe", lambda e: e.matmul(DB[bank][:, out_sl], lhsT=self.ones[1], rhs=pt[:, poff:poff + w],
                                           start=False, stop=False, skip_group_check=True),
                  reads=[ptk], writes=[DK_[bank]])

        for h in range(8):
            hi = h % 2
            kt, v1, v4, v16, pb = KT[hi], V1[hi], V4[hi], V16[hi], PB[hi]
            hsl = slice(h * 128, (h + 1) * 128)
            ld = lambda fn, k: P.add("sp", fn, writes=[("ld", hi, k)], slot=("ld", hi, k))
            ld(lambda e, kt=kt, hsl=hsl: e.dma_start(out=kt[:, 0:TP], in_=kvp_k[hsl, :]), 0)
            ld(lambda e, kt=kt, hsl=hsl: e.dma_start(out=kt[:, TP:2 * TP], in_=kvl_k[hsl, :]), 1)
            ld(lambda e, v1=v1, hsl=hsl: e.dma_start(out=v1[:, 0, :], in_=kvp_v[896:1024, hsl]), 2)
            ld(lambda e, v1=v1, hsl=hsl: e.dma_start(out=v1[:, 1:9, :], in_=kvl_v[:, hsl].rearrange("(t p) c -> p t c", p=128)), 3)
            ld(lambda e, v4=v4, hsl=hsl: e.dma_start(out=v4[:, 0, :, :], in_=kvp_v[512:1024, hsl].rearrange("(i r) c -> i r c", r=4)), 4)
            ld(lambda e, v4=v4, hsl=hsl: e.dma_start(out=v4[:, 1, :, :], in_=kvl_v[0:512, hsl].rearrange("(i r) c -> i r c", r=4)), 5)
            ld(lambda e, v4=v4, hsl=hsl: e.dma_start(out=v4[:, 2, :, :], in_=kvl_v[512:1024, hsl].rearrange("(i r) c -> i r c", r=4)), 6)
            ld(lambda e, v16=v16, hsl=hsl: e.dma_start(out=v16[0:64, :, :], in_=kvp_v[:, hsl].rearrange("(i r) c -> i r c", r=16)), 7)
            ld(lambda e, v16=v16, hsl=hsl: e.dma_start(out=v16[64:128, :, :], in_=kvl_v[:, hsl].rearrange("(i r) c -> i r c", r=16)), 8)
            ld(lambda e, pb=pb, h=h: e.dma_start(out=pb, in_=pbias[h]), 9)
            ldk = [("ld", hi, k) for k in range(10)]
            qn = CAT[:, h, :]
            qk = [("QN", h)]
            items = []

            for kt_ in range(7, 16):
                qlo = max(0, (kt_ - 8) * 128)
                qhi = min(TP, (kt_ - 6) * 128)
                w = qhi - qlo
                joff = qlo - (kt_ - 8) * 128

                def fr(kt_=kt_, qlo=qlo, w=w, joff=joff, kt=kt, qn=qn, pb=pb, ldk=ldk, qk=qk):
                    return score(kt[:, kt_ * 128:(kt_ + 1) * 128], qn[:, qlo:qlo + w], pb[:, joff:joff + w],
                                 self.ccol(C_PM0) if kt_ == 7 else zc, w, ldk, qk)

                def bk(r, kt_=kt_, qlo=qlo, w=w, v1=v1, ldk=ldk):
                    pt, ptk = r
                    for o in range(0, w, 128):
                        q0 = qlo + o
                        pv(pt, ptk, o, 128, v1[:, kt_ - 7, :], q0 // 512, slice(q0 % 512, q0 % 512 + 128), ldk)
                items.append((fr, bk))
            for r in range(4):
                for ct in range(1, 4):
                    clo = max(256, ct * 128)
                    chi = min(512, ct * 128 + 256)
                    w = chi - clo
                    joff = clo - ct * 128

                    def fr(ct=ct, r=r, clo=clo, w=w, joff=joff, kt=kt, qn=qn, pb=pb, ldk=ldk, qk=qk):
                        return score(kt[:, ct * 512 + r:(ct + 1) * 512:4], qn[:, (clo - 256) * 4 + r:(clo - 256 + w) * 4:4],
                                     pb[:, 256 + joff:256 + joff + w], self.ccol(C_PM0) if ct == 1 else zc, w, ldk, qk)

                    def bk(rr_, ct=ct, r=r, clo=clo, w=w, v4=v4, ldk=ldk):
                        pt, ptk = rr_
                        for o in range(0, w, 128):
                            c0 = clo + o
                            pv(pt, ptk, o, 128, v4[:, ct - 1, r, :], (c0 - 256) // 128, slice(r, 512, 4), ldk)
                    items.append((fr, bk))
            for r0 in range(0, 16, 4):
                def fr(r0=r0, kt=kt, qn=qn, pb=pb, ldk=ldk, qk=qk):
                    ps, psk = self.psum()
                    for rr in range(4):
                        r = r0 + rr
                        P.add("pe", lambda e, ps=ps, rr=rr, a=kt[:, r:2048:16], b_=qn[:, r:TP:16]: e.matmul(
                            ps[:, rr * 64:(rr + 1) * 64], lhsT=a, rhs=b_, start=True, stop=True),
                            reads=ldk + qk, writes=[psk])
                    ti_ = self.nxt("tmpB", 3)
                    tm, tmk = tmp[ti_], ("tmpB", ti_)
                    P.add("dve", lambda e, ps=ps, tm=tm, b16=pb[:, 512:576].unsqueeze(1).to_broadcast([128, 4, 64]): e.scalar_tensor_tensor(
                        out=tm[:, :].rearrange("p (a b) -> p a b", a=4), in0=ps[:, 0:256].rearrange("p (a b) -> p a b", a=4),
                        scalar=SCALE, in1=b16, op0=ALU.mult, op1=ALU.add),
                        reads=[psk] + ldk, writes=[tmk])
                    pi = self.nxt("PtB", 6)
                    pt, ptk = Pt[pi], ("PtB", pi)
                    P.add("act", lambda e, tm=tm, pt=pt: e.activation(out=pt, in_=tm, func=AF.Exp, bias=self.ccol(C_PM1), scale=1.0),
                          reads=[tmk], writes=[ptk])
                    return pt, ptk

                def bk(rr_, r0=r0, v16=v16, ldk=ldk):
                    pt, ptk = rr_
                    for rr in range(4):
                        r = r0 + rr
                        for bank in range(2):
                            pv(pt, ptk, rr * 64 + bank * 32, 32, v16[:, r, :], bank, slice(r, 512, 16), ldk)
                items.append((fr, bk))

            def first_back(r, bk0=items[0][1]):
                for b in range(2):
                    P.add("dve", lambda e, b=b: e.memset(OB[b][:, :], 0.0), writes=[OK_[b]])
                    P.add("dve", lambda e, b=b: e.memset(DB[b][:, :], 0.0), writes=[DK_[b]])
                bk0(r)

            def last_back(r, bkl=items[-1][1], h=h, qk=qk):
                bkl(r)
                for b in range(2):
                    P.add("act", lambda e, b=b: e.activation(out=rden[:, b * 512:(b + 1) * 512], in_=DB[b][:, :], func=AF.Ln),
                          reads=[DK_[b]], writes=[("rdenB", b)])
                    P.add("act", lambda e, b=b: e.activation(out=rden[:, b * 512:(b + 1) * 512], in_=rden[:, b * 512:(b + 1) * 512],
                                                             func=AF.Exp, scale=-1.0),
                          reads=[("rdenB", b)], writes=[("rdenB", b)])
                    P.add("dve", lambda e, b=b, h=h: e.tensor_tensor(
                        out=CAT[:, h, b * 512:(b + 1) * 512], in0=OB[b][:, :], in1=rden[:, b * 512:(b + 1) * 512], op=ALU.mult),
                        reads=[OK_[b], ("rdenB", b)], writes=qk)

            items[0] = (items[0][0], first_back)
            items[-1] = (items[-1][0], last_back)
            for fr_, bk_ in items:
                pipeB.add(fr_, bk_)
        pipeB.drain()
        self.ring = list(range(8))
        P.barrier()
        P.flush()
        if cfg.get("stop") == "B":
            P.flush(final=True)
            return
        if cfg.get("stop") == "Bd":
            dbg = Alloc(self, samp_off)([D], F32)
            for b in range(8):
                P.add("dve", lambda e, b=b: e.tensor_copy(out=dbg.rearrange("p (c t) -> p c t", c=NCH), in_=CAT[:, :, b * 128:(b + 1) * 128]),
                      writes=[("dbg",)])
                P.add("sp", lambda e, b=b: e.dma_start(out=y[b * 128:(b + 1) * 128, :], in_=dbg), reads=[("dbg",)], writes=[("dbg",)], slot="dbg")
            P.flush(final=True)
            return

        cwk, cwv = dn("cwk", [4, 2048, 1024]), dn("cwv", [4, 2048, 1024])
        cmk, cmv = dn("cmk", [4, 256, 512]), dn("cmv", [4, 256, 512])
        alx = Alloc(self, self.XN_off)
        Kc0 = alx([9, 1024], BF16)
        qb = [alx([1024], F32) for _ in range(2)]
        prod = alx([1024], F32)
        assert alx.off <= self.R_off
        alw = Alloc(self, self.WR_off)
        Kc1 = alw([9, 1024], BF16)
        MKs1 = alw([2, 512], BF16)
        assert alw.off <= self.BIGB_off
        al = Alloc(self, samp_off)
        Vc = al([9, 8, 130], BF16)
        qmb = [al([512], F32) for _ in range(2)]
        L = al([4, 3, 8], F32)
        Pe = al([4, 24, 16], BF16)
        Ln = al([16, 3, 8], F32)
        Pn = al([16, 8], F32)
        MKs0 = al([2, 512], BF16)
        MVs = al([2, 4, 130], BF16)
        Lm = al([4, 2, 4], F32)
        Pme = al([4, 8, 16], BF16)
        rd = al([16], F32)
        self.ring = [0, 1, 2]
        OS = [self.PS[3], self.PS[4], self.PS[5]]
        OSK = [("ps", 3), ("ps", 4), ("ps", 5)]
        OM = [self.PS[6], self.PS[7]]
        OMK = [("ps", 6), ("ps", 7)]
        for i in range(3):
            P.add("dve", lambda e, i=i: e.memset(OS[i][:, :], 0.0), writes=[OSK[i]])
        for i in range(2):
            P.add("dve", lambda e, i=i: e.memset(OM[i][:, :], 0.0), writes=[OMK[i]])
        P.add("pool", lambda e: e.memset(Vc[:, :, :, 128:130], 1.0), writes=[("Vc1",)])
        P.add("pool", lambda e: e.memset(MVs[:, :, :, 128:130], 1.0), writes=[("MVs1",)])
        Z = self.cst[:, C_Z:C_Z + 112].rearrange("p (t x) -> p t x", t=4)
        biass = self.cst[:, C_BIASS:C_BIASS + 96].rearrange("p (a b c) -> p a b c", a=4, b=3)
        for bs in range(4):
            kb = bs % 2
            Kc = (Kc0, Kc1)[kb]
            MKs = (MKs0, MKs1)[kb]
            P.add("pool", lambda e, bs=bs, Kc=Kc: e.dma_start(out=Kc[:, 0, :], in_=cwk[bs, 1920:2048, :]), writes=[("Kc", kb, 0)], slot=("Kc", kb, 0))
            P.add("pool", lambda e, bs=bs, Kc=Kc: e.dma_start(out=Kc[:, 1:5, :], in_=cwk[bs, 1536:2048, :].rearrange("(i t) c -> i t c", t=4)),
                  writes=[("Kc", kb, 1)], slot=("Kc", kb, 1))
            P.add("pool", lambda e, bs=bs, Kc=Kc: e.dma_start(out=Kc[:, 5:9, :], in_=cwk[bs].rearrange("(i r) c -> i r c", r=16)[:, 0:4, :]),
                  writes=[("Kc", kb, 2)], slot=("Kc", kb, 2))
            P.add("pool", lambda e, bs=bs, MKs=MKs: e.dma_start(out=MKs, in_=cmk[bs].rearrange("(m p) c -> p m c", p=128)),
                  writes=[("MKs", kb)], slot=("MKs", kb))
            P.add("pool", lambda e, bs=bs: e.dma_start(out=Vc[:, 0, :, 0:128], in_=cwv[bs, 1920:2048, :].rearrange("i (h d) -> i h d", h=8)),
                  reads=[("Vc1",)], writes=[("Vc", 0)], slot=("Vc", 0))
            for t in range(4):
                P.add("pool", lambda e, bs=bs, t=t: e.dma_start(
                    out=Vc[:, 1 + t, :, 0:128],
                    in_=cwv[bs, 1536:2048, :].rearrange("(i t) (h d) -> i t h d", t=4, h=8)[:, t, :, :]),
                    reads=[("Vc1",)], writes=[("Vc", 1 + t)], slot=("Vc", 1 + t))
                P.add("pool", lambda e, bs=bs, t=t: e.dma_start(
                    out=Vc[:, 5 + t, :, 0:128],
                    in_=cwv[bs].rearrange("(i r) (h d) -> i r h d", r=16, h=8)[:, t, :, :]),
                    reads=[("Vc1",)], writes=[("Vc", 5 + t)], slot=("Vc", 5 + t))
            for mt in range(2):
                P.add("pool", lambda e, bs=bs, mt=mt: e.dma_start(
                    out=MVs[:, mt, :, 0:128], in_=cmv[bs, mt * 128:(mt + 1) * 128, :].rearrange("i (h d) -> i h d", h=4)),
                    reads=[("MVs1",)], writes=[("MVs", mt)], slot=("MVs", mt))
            kck = [("Kc", kb, i) for i in range(3)]
            vck = [("Vc", i) for i in range(9)]
            for t in range(4):
                q = bs * 4 + t
                bi = self.nxt("qb", 2)
                qbt, qbk = qb[bi], ("qb", bi)
                qmt, qmk = qmb[bi], ("qmb", bi)
                P.add("sp", lambda e, qbt=qbt, q=q: e.dma_start(out=qbt, in_=qscr[q:q + 1, :].to_broadcast([128, 1024])),
                      writes=[qbk], slot=("qb", bi))
                P.add("sp", lambda e, qmt=qmt, q=q: e.dma_start(out=qmt, in_=qmscr[q:q + 1, :].to_broadcast([128, 512])),
                      writes=[qmk], slot=("qmb", bi))
                for dl in range(3):
                    ktile = Kc[:, 0, :] if dl == 0 else Kc[:, 1 + (dl - 1) * 4 + t, :]
                    P.add("dve", lambda e, ktile=ktile, qbt=qbt: e.tensor_tensor(out=prod, in0=ktile, in1=qbt, op=ALU.mult),
                          reads=kck + [qbk], writes=[("prod",)])
                    P.add("dve", lambda e, t=t, dl=dl: e.tensor_reduce(
                        out=L[:, t, dl, :], in_=prod.rearrange("p (h d) -> p h d", h=8), axis=AX.X, op=ALU.add),
                        reads=[("prod",)], writes=[("L", t)])
                P.add("dve", lambda e, qbt=qbt: e.tensor_tensor(out=prod[:TS, :], in0=self.ks[:TS, :], in1=qbt[:TS, :], op=ALU.mult),
                      reads=[qbk], writes=[("prod",)])
                P.add("dve", lambda e, q=q: e.tensor_reduce(
                    out=Ln[:TS, q, 0, :], in_=prod[:TS, :].rearrange("p (h d) -> p h d", h=8), axis=AX.X, op=ALU.add),
                    reads=[("prod",)], writes=[("Ln", q)])
                for mt in range(2):
                    P.add("dve", lambda e, mt=mt, qmt=qmt, MKs=MKs: e.tensor_tensor(out=prod[:, 0:512], in0=MKs[:, mt, :], in1=qmt, op=ALU.mult),
                          reads=[("MKs", kb), qmk], writes=[("prod",)])
                    P.add("dve", lambda e, t=t, mt=mt: e.tensor_reduce(
                        out=Lm[:, t, mt, :], in_=prod[:, 0:512].rearrange("p (h d) -> p h d", h=4), axis=AX.X, op=ALU.add),
                        reads=[("prod",)], writes=[("Lm", t)])
            P.add("dve", lambda e: e.scalar_tensor_tensor(out=L, in0=L, scalar=SCALE, in1=biass, op0=ALU.mult, op1=ALU.add),
                  reads=[("L", t) for t in range(4)] + [("cst",)], writes=[("Lb",)])
            P.add("act", lambda e: e.activation(out=L, in_=L, func=AF.Exp), reads=[("Lb",)], writes=[("Lp",)])
            P.add("act", lambda e: e.activation(out=Lm, in_=Lm, func=AF.Exp, scale=SCALE),
                  reads=[("Lm", t) for t in range(4)], writes=[("Lmp",)])
            w0 = 12 - 4 * bs
            for t in range(4):
                P.add("dve", lambda e, t=t, w0=w0: e.tensor_tensor(
                    out=Pe[:, t, :, :], in0=L[:, t, :, :].rearrange("p a b -> p (a b)").unsqueeze(2).to_broadcast([128, 24, 16]),
                    in1=Z[:, t, w0:w0 + 16].unsqueeze(1).to_broadcast([128, 24, 16]), op=ALU.mult),
                    reads=[("Lp",), ("cst",)], writes=[("Pe", t)])
                P.add("dve", lambda e, t=t, w0=w0: e.tensor_tensor(
                    out=Pme[:, t, :, :], in0=Lm[:, t, :, :].rearrange("p a b -> p (a b)").unsqueeze(2).to_broadcast([128, 8, 16]),
                    in1=Z[:, t, w0:w0 + 16].unsqueeze(1).to_broadcast([128, 8, 16]), op=ALU.mult),
                    reads=[("Lmp",), ("cst",)], writes=[("Pme", t)])
            for t in range(4):
                for dl in range(3):
                    vi = 0 if dl == 0 else 1 + (dl - 1) * 4 + t
                    for h in range(8):
                        bk, off = h // 3, (h % 3) * 129
                        P.add("pe", lambda e, t=t, dl=dl, h=h, vi=vi, bk=bk, off=off: e.matmul(
                            OS[bk][:TS, off:off + 129], lhsT=Pe[:, t, dl * 8 + h, :], rhs=Vc[:, vi, h, 0:129],
                            start=False, stop=False, skip_group_check=True),
                            reads=[("Pe", t)] + vck, writes=[OSK[bk]])
                for mt in range(2):
                    for h in range(4):
                        bk, off = h // 3, (h % 3) * 129
                        P.add("pe", lambda e, t=t, mt=mt, h=h, bk=bk, off=off: e.matmul(
                            OM[bk][:TS, off:off + 129], lhsT=Pme[:, t, mt * 4 + h, :], rhs=MVs[:, mt, h, 0:129],
                            start=False, stop=False, skip_group_check=True),
                            reads=[("Pme", t), ("MVs", 0), ("MVs", 1)], writes=[OMK[bk]])
        biasn = self.cst[:TS, C_BIASN:C_BIASN + 384].rearrange("p (q c h) -> p q c h", q=16, c=3)
        Lnf = Ln[:TS, :, :, :]
        P.add("dve", lambda e: e.tensor_copy(out=Ln[:TS, :, 1, :], in_=Ln[:TS, :, 0, :]), reads=[("Ln", q) for q in range(16)], writes=[("Ln1",)])
        P.add("dve", lambda e: e.tensor_copy(out=Ln[:TS, :, 2, :], in_=Ln[:TS, :, 0, :]), reads=[("Ln1",)], writes=[("Ln2",)])
        P.add("dve", lambda e: e.scalar_tensor_tensor(out=Lnf, in0=Lnf, scalar=SCALE, in1=biasn, op0=ALU.mult, op1=ALU.add),
              reads=[("Ln2",), ("cst",)], writes=[("Lnb",)])
        P.add("act", lambda e: e.activation(out=Lnf, in_=Lnf, func=AF.Exp), reads=[("Lnb",)], writes=[("Lnp",)])
        P.add("dve", lambda e: e.tensor_reduce(out=Pn[:TS, :, :], in_=Ln[:TS, :, :, :].rearrange("p q c h -> p q h c"),
                                               axis=AX.X, op=ALU.add), reads=[("Lnp",)], writes=[("Pn",)])
        for h in range(8):
            bk, off = h // 3, (h % 3) * 129
            P.add("pe", lambda e, h=h, bk=bk, off=off: e.matmul(
                OS[bk][:TS, off:off + 129], lhsT=Pn[:TS, :, h], rhs=self.vsa[:TS, h, 0:129],
                start=False, stop=False, skip_group_check=True),
                reads=[("Pn",)], writes=[OSK[bk]])
        for h in range(8):
            bk, off = h // 3, (h % 3) * 129
            P.add("dve", lambda e, h=h, bk=bk, off=off: e.reciprocal(out=rd[:TS, h:h + 1], in_=OS[bk][:TS, off + 128:off + 129]),
                  reads=[OSK[bk]], writes=[("rd", h)])
            P.add("dve", lambda e, h=h, bk=bk, off=off: e.tensor_scalar(
                out=self.oas[:TS, h * 128:(h + 1) * 128], in0=OS[bk][:TS, off:off + 128], scalar1=rd[:TS, h:h + 1], scalar2=None,
                op0=ALU.mult), reads=[OSK[bk], ("rd", h)], writes=[("oas", h)])
        for h in range(4):
            bk, off = h // 3, (h % 3) * 129
            P.add("dve", lambda e, h=h, bk=bk, off=off: e.reciprocal(out=rd[:TS, 8 + h:9 + h], in_=OM[bk][:TS, off + 128:off + 129]),
                  reads=[OMK[bk]], writes=[("rd", 8 + h)])
            P.add("dve", lambda e, h=h, bk=bk, off=off: e.tensor_scalar(
                out=self.oas[:TS, 1536 + h * 128:1536 + (h + 1) * 128], in0=OM[bk][:TS, off:off + 128],
                scalar1=rd[:TS, 8 + h:9 + h], scalar2=None, op0=ALU.mult), reads=[OMK[bk], ("rd", 8 + h)], writes=[("oas", 12 + h)])
        ps, psk = self.psum()
        for g in range(4):
            P.add("pe", lambda e, ps=ps, g=g: e.matmul(
                ps[:TS, g * 128:(g + 1) * 128], lhsT=self.cst[:TS, C_ASGU + g * 16:C_ASGU + (g + 1) * 16],
                rhs=self.vns[:TS, g * 128:(g + 1) * 128], start=True, stop=True),
                reads=[("cst",)], writes=[psk])
        for g in range(4):
            P.add("dve", lambda e, ps=ps, g=g: e.scalar_tensor_tensor(
                out=self.oas[:TS, 1024 + g * 128:1024 + (g + 1) * 128], in0=ps[:TS, g * 128:(g + 1) * 128],
                scalar=self.cst[:TS, C_BSS + g:C_BSS + g + 1], in1=self.us[:TS, g * 128:(g + 1) * 128],
                op0=ALU.add, op1=ALU.mult), reads=[psk, ("cst",)], writes=[("oas", 8 + g)])
        ps, psk = self.psum()
        for c in range(NCH):
            P.add("pe", lambda e, ps=ps, c=c: e.transpose(out=ps[:, c * 16:(c + 1) * 16], in_=self.oas[:TS, c * 128:(c + 1) * 128],
                                                           identity=self.ident[:TS, :TS]),
                  reads=[("oas", c), ("ident",)], writes=[psk])
        P.add("dve", lambda e, ps=ps: e.tensor_copy(out=CAT[:, :, TP:T], in_=ps[:, 0:256].rearrange("p (c t) -> p c t", c=NCH)),
              reads=[psk], writes=[("CATs",)])
        self.ring = list(range(8))
        P.barrier()
        P.flush()

        wout = dn("wout", [D, D])
        w2g, w2u, w2d = dn("w2g", [D, DFF]), dn("w2u", [D, DFF]), dn("w2d", [DFF, D])
        al = Alloc(self, self.free_off)
        hb = [al([T], F32) for _ in range(2)]
        sg = [al([348], F32) for _ in range(3)]
        stg = [al([D], F32)]
        for (c0, c1, on) in ((0, 8, 1024), (8, 12, 512), (12, 16, 512)):
            self.rms_stats(lambda c, t0, n, c0=c0: CAT[:, c0 + c, t0:t0 + n], lambda c, ti: [], c1 - c0, on, MAIN, self.rstd, "rstd")
            for ti, (t0, n) in enumerate(MAIN.tiles):
                for c in range(c0, c1):
                    P.add("dve", lambda e, c=c, t0=t0, n=n: e.scalar_tensor_tensor(
                        out=CAT[:, c, t0:t0 + n], in0=CAT[:, c, t0:t0 + n], scalar=self.gains[:, 2, c:c + 1],
                        in1=self.rstd[:, t0:t0 + n], op0=ALU.mult, op1=ALU.mult),
                        reads=[("rstd", ti), ("gains",)], writes=[("CATn", c, ti)])
        for sidx in range(8):
            slab, sk = self.load_slab(wout, 0, D, sidx * 256)
            for j in range(2):
                m = sidx * 2 + j
                hi = self.nxt("hb", 2)
                hbt, hbk = hb[hi], ("hb", hi)
                P.add("sp", lambda e, hbt=hbt, m=m: e.dma_start(out=hbt, in_=hscr[m]), writes=[hbk], slot=("hb", hi))
                for ti, (t0, n) in enumerate(MAIN.tiles):
                    ps, psk = self.psum()
                    for ko in range(NCH):
                        P.add("pe", lambda e, ps=ps, slab=slab, ko=ko, j=j, t0=t0, n=n: e.matmul(
                            ps[:, :n], lhsT=slab[:, ko, j * 128:(j + 1) * 128], rhs=CAT[:, ko, t0:t0 + n],
                            start=(ko == 0), stop=(ko == NCH - 1)),
                            reads=[sk, ("CATn", ko, ti)], writes=[psk])
                    P.add("dve", lambda e, ps=ps, m=m, t0=t0, n=n, hbt=hbt: e.tensor_tensor(
                        out=R[:, m, t0:t0 + n], in0=ps[:, :n], in1=hbt[:, t0:t0 + n], op=ALU.add),
                        reads=[psk, hbk], writes=[("R", m, ti)])
        P.barrier()
        P.flush()
        self.rmsnorm_R(3, MAIN)
        self.ffn(w2g, w2u, w2d, MAIN, sg)
        stg_b = self.view(self.free_off, [D], F32)
        self.transpose_store(lambda c, t0, nt: R[:, c, t0:t0 + nt],
                             lambda c: [("R", c, ti) for ti in range(3)] + [("R", c, "all")], NCH, T, y, stg + [stg_b])
        P.flush(final=True)


def build_nc(cfg=None):
    cfg = cfg or {}
    nc = bass.Bass("TRN2", target_bir_lowering=False)
    with contextlib.ExitStack() as stack:
        b = Builder(nc, stack, cfg)
        b.build()
    return nc


def _rel_bucket(dist):
    dist = np.asarray(dist, np.int64)
    max_exact = N_BUCKETS // 2
    large = max_exact + (np.log(np.maximum(dist, 1) / max_exact) / np.log(REL_MAX_DIST / max_exact)
                         * (N_BUCKETS - max_exact)).astype(np.int64)
    large = np.minimum(large, N_BUCKETS - 1)
    return np.where(dist < max_exact, dist, large).astype(np.int32)


def make_in_maps(inputs):
    f = lambda k: np.asarray(inputs[k], np.float32)
    xp, xsm = f("x_prompt"), f("x_sample")
    rel = f("rel_bias")
    ident = np.eye(128, dtype=np.float32)

    def gl(v):
        return np.ascontiguousarray(np.asarray(v, np.float32).reshape(NCH, 128).T)

    gains = np.ascontiguousarray(np.concatenate(
        [gl(f("g_ffn1")[0]), gl(f("g_mix")[0]), gl(f("g_mix_out")[0]), gl(f("g_ffn2")[0])], axis=1))
    g_mem = gl(f("g_mem")[0])
    i = np.arange(128)[:, None]
    pbias = np.empty((8, 128, 576), np.float32)
    for dil, lo in ((1, 0), (4, 256)):
        jj = np.arange(256)[None, :]
        step = jj - i
        valid = (step >= 0) & (step <= 128)
        bk = _rel_bucket(np.clip(step, 0, 128) * dil)
        for h in range(8):
            pbias[h, :, lo:lo + 256] = np.where(valid, rel[bk, h], NEG)
    jj = np.arange(64)[None, :]
    step = 64 + jj - i
    valid = step >= 0
    bk = _rel_bucket(np.clip(step, 0, 128) * 16)
    for h in range(8):
        pbias[h, :, 512:576] = np.where(valid, rel[bk, h], NEG)

    w_sgu = f("w_sgu")[0]
    b_sgu = f("b_sgu")[0]
    g_sgu = f("g_sgu")[0]
    maps = []
    for c in range(NCORES):
        b, half = c // 2, c % 2
        cst = np.zeros((128, NCST), np.float32)
        cst[:, C_GQA] = f("g_qa")[0]
        cst[:, C_GKA] = f("g_ka")[0]
        cst[:, C_GQM] = f("g_qm")[0]
        cst[:, C_GKM] = f("g_km")[0]
        for g in range(4):
            cst[:, C_GSGU + g] = g_sgu[g]
        if half == 0:
            cst[:, C_PM0] = NEG
            cst[:64, C_PM1] = NEG
        cst[:, C_TRIL:C_TRIL + 128] = np.tril(np.ones((128, 128), np.float32))
        cst[:, C_BSGU:C_BSGU + 512] = b_sgu.reshape(1, 512)
        for tok in range(16):
            cst[tok, C_BSS:C_BSS + 4] = b_sgu[:, tok % 4]
        for g in range(4):
            for tk in range(16):
                for tq in range(16):
                    if tk // 4 == tq // 4 and tk % 4 <= tq % 4:
                        cst[tk, C_ASGU + g * 16 + tq] = w_sgu[g, tq % 4, tk % 4]
        for t in range(4):
            cst[:, C_Z + t * 28 + 12 + t] = 1.0
        ii = np.arange(128)
        for t in range(4):
            for dl, dil in enumerate((1, 4, 16)):
                if dl == 0:
                    j = 128 + t - ii
                    valid = ii >= t
                else:
                    j = 128 - ii
                    valid = np.ones(128, bool)
                bk = _rel_bucket(np.clip(j, 0, 128) * dil)
                for h in range(8):
                    cst[:, C_BIASS + (t * 3 + dl) * 8 + h] = np.where(valid, rel[bk, h], NEG)
        b0 = _rel_bucket(np.array([0]))[0]
        for tk in range(16):
            for q in range(16):
                for cc in range(3):
                    col = C_BIASN + (q * 3 + cc) * 8
                    same = tk // 4 == q // 4
                    tk_, tq_ = tk % 4, q % 4
                    if cc == 0 and same and tk_ <= tq_:
                        cst[tk, col:col + 8] = rel[_rel_bucket(np.array([tq_ - tk_]))[0], :]
                    elif cc > 0 and same and tk_ == tq_:
                        cst[tk, col:col + 8] = rel[b0, :]
                    else:
                        cst[tk, col:col + 8] = NEG
        xs = np.concatenate([xp[b, half * TP:(half + 1) * TP], xsm[4 * c:4 * c + 4].reshape(TS, D)], axis=0)
        xprev = xp[b, 0:TP]
        maps.append({
            "xs": np.ascontiguousarray(xs), "xprev": np.ascontiguousarray(xprev), "mem": f("mem_prompt")[b],
            "w1g": f("w1_gate")[0], "w1u": f("w1_up")[0], "w1d": f("w1_down")[0],
            "w2g": f("w2_gate")[0], "w2u": f("w2_up")[0], "w2d": f("w2_down")[0],
            "win": f("w_in")[0], "wout": f("w_out")[0], "wmem": f("w_mem_kv")[0], "wsgu": w_sgu,
            "pbias": pbias,
            "cwk": f("cache_win_k")[0, 4 * c:4 * c + 4].reshape(4, 2048, 1024),
            "cwv": f("cache_win_v")[0, 4 * c:4 * c + 4].reshape(4, 2048, 1024),
            "cmk": f("cache_mem_k")[0, 4 * c:4 * c + 4].reshape(4, 256, 512),
            "cmv": f("cache_mem_v")[0, 4 * c:4 * c + 4].reshape(4, 256, 512),
            "ident": ident, "gains": gains, "cst": cst, "g_mem": g_mem,
        })
    return maps


def assemble(outs):
    n = len(outs)
    y_p = np.zeros((4, 2048, D), np.float32)
    y_s = np.zeros((32, 4, D), np.float32)
    wk_p = np.zeros((1, 4, 2048, 8, 128), np.float32)
    wv_p = np.zeros((1, 4, 2048, 8, 128), np.float32)
    mk_p = np.zeros((1, 4, 256, 4, 128), np.float32)
    mv_p = np.zeros((1, 4, 256, 4, 128), np.float32)
    wk_s = np.zeros((1, 32, 4, 8, 128), np.float32)
    wv_s = np.zeros((1, 32, 4, 8, 128), np.float32)
    cv_s = np.zeros((1, 32, 4, 4, 128), np.float32)
    for c in range(n):
        b, half = c // 2, c % 2
        o = outs[c]
        sl = slice(half * TP, (half + 1) * TP)
        y_p[b, sl] = o["y"][:TP]
        y_s[4 * c:4 * c + 4] = o["y"][TP:].reshape(4, 4, D)
        wk_p[0, b, sl] = o["o_wk"][:TP].reshape(TP, 8, 128)
        wv_p[0, b, sl] = o["o_wv"][:TP].reshape(TP, 8, 128)
        wk_s[0, 4 * c:4 * c + 4] = o["o_wk"][TP:].reshape(4, 4, 8, 128)
        wv_s[0, 4 * c:4 * c + 4] = o["o_wv"][TP:].reshape(4, 4, 8, 128)
        cv_s[0, 4 * c:4 * c + 4] = o["o_cv"].reshape(4, 4, 4, 128)
        if half == 0:
            mk_p[0, b] = o["o_mk"].reshape(256, 4, 128)
            mv_p[0, b] = o["o_mv"].reshape(256, 4, 128)
    return (y_p, y_s, wk_p, wv_p, mk_p, mv_p, wk_s, wv_s, cv_s)


def kernel(**inputs):
    nc = build_nc()
    maps = make_in_maps(inputs)
    res = run_bass_kernel_spmd(nc, maps, core_ids=list(range(NCORES)))
    return assemble(res.results)
```

```python
import contextlib
import numpy as np
import concourse.bass as bass
import concourse.mybir as mybir
from concourse.bass_utils import run_bass_kernel_spmd

F32 = mybir.dt.float32
BF16 = mybir.dt.bfloat16
AF = mybir.ActivationFunctionType
ALU = mybir.AluOpType
AX = mybir.AxisListType

NCORES = 8
D = 2048
NCH = 16
DFF = 5632
TP = 1024
TS = 16
T = TP + TS
EPS = 1e-6
SCALE = 128 ** -0.5
NEG = -30000.0
N_BUCKETS = 32
REL_MAX_DIST = 2048


class TokSet:
    def __init__(self, n, tiles):
        self.n = n
        self.tiles = tiles


MAIN = TokSet(T, [(0, 347), (347, 347), (694, 346)])
PREV = TokSet(TP, [(0, 342), (342, 341), (683, 341)])

C_GQA, C_GKA, C_GQM, C_GKM, C_GSGU, C_PM0, C_PM1, C_ZERO = 0, 1, 2, 3, 4, 8, 9, 10
C_TRIL = 12
C_BSGU = 140
C_BSS = 652
C_ASGU = 656
C_Z = 720
C_BIASS = 832
C_BIASN = 928
NCST = 1312


class Op:
    __slots__ = ("eng", "fn", "deps", "signal", "ticket", "dma", "slot")


class Prog:
    ENGS = ("pe", "act", "dve", "pool", "sp")

    def __init__(self, nc, stack):
        self.nc = nc
        self.stack = stack
        self.pending = []
        self.last_w = {}
        self.readers = {}
        self.eng_sem = {e: stack.enter_context(nc.semaphore("s_" + e)) for e in self.ENGS}
        self.eng_cnt = {e: 0 for e in self.ENGS}
        self.slot_sem = {}
        self.slot_cnt = {}
        self.waited = {e: {} for e in self.ENGS}
        self.frontier = {}
        self.barrier_ops = []

    def add(self, eng, fn, reads=(), writes=(), slot=None, after=()):
        op = Op()
        op.eng, op.fn, op.slot = eng, fn, slot
        op.dma = slot is not None
        op.signal = op.dma
        op.ticket = None
        deps = []
        for k in reads:
            w = self.last_w.get(k)
            if w is not None:
                deps.append(w)
        for k in writes:
            w = self.last_w.get(k)
            if w is not None:
                deps.append(w)
            deps.extend(self.readers.get(k, ()))
        deps.extend(after)
        deps.extend(self.barrier_ops)
        seen = set()
        op.deps = []
        for d in deps:
            if d is op or id(d) in seen:
                continue
            seen.add(id(d))
            if d.dma or d.eng != eng or eng != "pe":
                d.signal = True
                op.deps.append(d)
        for k in reads:
            self.readers.setdefault(k, []).append(op)
        for k in writes:
            self.last_w[k] = op
            self.readers[k] = []
        if op.dma:
            if slot not in self.slot_sem:
                self.slot_sem[slot] = self.stack.enter_context(self.nc.semaphore("d_" + str(len(self.slot_sem))))
                self.slot_cnt[slot] = 0
            self.slot_cnt[slot] += 16
            op.ticket = self.slot_cnt[slot]
            self.frontier[("slot", slot)] = op
        else:
            self.frontier[("eng", eng)] = op
        self.pending.append(op)
        return op

    def barrier(self):
        ops = list(self.frontier.values())
        for o in ops:
            o.signal = True
        self.barrier_ops = ops
        self.last_w = {}
        self.readers = {}

    def _sem_of(self, op):
        return self.slot_sem[op.slot] if op.dma else self.eng_sem[op.eng]

    def flush(self, final=False):
        nc = self.nc
        for op in self.pending:
            if not op.dma and op.signal:
                self.eng_cnt[op.eng] += 1
                op.ticket = self.eng_cnt[op.eng]
        per = {e: [o for o in self.pending if o.eng == e] for e in self.ENGS}

        def emit(ename, eng):
            waited = self.waited[ename]
            for op in per[ename]:
                for d in op.deps:
                    sem = self._sem_of(d)
                    key = id(sem)
                    if waited.get(key, 0) >= d.ticket:
                        continue
                    eng.wait_ge(sem, d.ticket)
                    waited[key] = d.ticket
                ins = op.fn(eng)
                if op.signal:
                    ins.then_inc(self._sem_of(op), 16 if op.dma else 1)
            if final and ename == "sp":
                for slot, sem in self.slot_sem.items():
                    if waited.get(id(sem), 0) < self.slot_cnt[slot]:
                        eng.wait_ge(sem, self.slot_cnt[slot])
                for e2 in self.ENGS:
                    if e2 != "sp" and self.eng_cnt[e2] > 0:
                        eng.wait_ge(self.eng_sem[e2], self.eng_cnt[e2])

        with nc.Block() as block:
            @block.tensor
            def _(e):
                emit("pe", e)

            @block.scalar
            def _(e):
                emit("act", e)

            @block.vector
            def _(e):
                emit("dve", e)

            @block.gpsimd
            def _(e):
                emit("pool", e)

            @block.sync
            def _(e):
                emit("sp", e)
        self.pending = []


class Alloc:
    def __init__(self, b, off):
        self.b = b
        self.off = off

    def __call__(self, shape, dt):
        n = 1
        for s in shape:
            n *= s
        words = n if dt == F32 else (n + 1) // 2
        v = self.b.view(self.off, shape, dt)
        self.off += words
        return v


def _run(gen):
    while True:
        try:
            next(gen)
        except StopIteration as st:
            return st.value


class Pipe:
    def __init__(self, lag=1):
        self.prev = None

    def add(self, front, back):
        fgen = front()
        bgen = self.prev[0](self.prev[1]) if self.prev else None
        res, fdone, bdone = None, False, bgen is None
        while not (fdone and bdone):
            if not fdone:
                try:
                    next(fgen)
                except StopIteration as st:
                    res, fdone = st.value, True
            if not bdone:
                try:
                    next(bgen)
                except StopIteration:
                    bdone = True
        self.prev = (back, res)

    def drain(self):
        if self.prev:
            _run(self.prev[0](self.prev[1]))
        self.prev = None


class PipeLag:
    def __init__(self, lag):
        self.lag = lag
        self.q = []

    def add(self, front, back):
        self.q.append((back, front()))
        if len(self.q) > self.lag:
            b, r = self.q.pop(0)
            b(r)

    def drain(self):
        while self.q:
            b, r = self.q.pop(0)
            b(r)


class Builder:
    NW = 52000

    def __init__(self, nc, stack, cfg):
        self.nc = nc
        self.stack = stack
        self.cfg = cfg
        self.P = Prog(nc, stack)
        self.psum_i = 0
        self.ring = list(range(8))
        self.ctr = {}

    def din(self, name, shape, dt=F32):
        return self.nc.dram_tensor(name, list(shape), dt, kind="ExternalInput").ap()

    def dout(self, name, shape, dt=F32):
        return self.nc.dram_tensor(name, list(shape), dt, kind="ExternalOutput").ap()

    def dscr(self, name, shape, dt):
        return self.nc.dram_tensor(name, list(shape), dt).ap()

    def psum(self):
        i = self.ring[self.psum_i % len(self.ring)]
        self.psum_i += 1
        return self.PS[i], ("ps", i)

    def nxt(self, name, n):
        v = self.ctr.get(name, 0)
        self.ctr[name] = v + 1
        return v % n

    def view(self, off, shape, dt):
        n = 1
        for s in shape:
            n *= s
        words = n if dt == F32 else (n + 1) // 2
        assert off + words <= self.NW, (off, words)
        a = self.arena[:, off:off + words]
        if dt != F32:
            a = a.bitcast(dt)[:, 0:n]
        if len(shape) == 2:
            a = a.rearrange("p (a b) -> p a b", a=shape[0])
        elif len(shape) == 3:
            a = a.rearrange("p (a b c) -> p a b c", a=shape[0], b=shape[1])
        return a

    def setup(self):
        nc, P, st = self.nc, self.P, self.stack
        self.PS = [st.enter_context(nc.psum_tensor("ps%d" % i, [128, 512], F32)) for i in range(8)]
        self.arena = st.enter_context(nc.sbuf_tensor("arena", [128, self.NW], F32))
        al = Alloc(self, 0)
        self.ident = al([128], F32)
        self.ones = {n: al([128], BF16) for n in (2048, 1024, 512, 128, 1)}
        self.gains = al([4, NCH], F32)
        self.cst = al([NCST], F32)
        self.epsc = al([2], F32)
        self.rstd = al([T], F32)
        self.sq = [al([348], BF16) for _ in range(3)]
        self.mkT = al([4, 256], BF16)
        self.mvtok = al([2, 512], BF16)
        self.WR_off = al.off
        self.WR = [al([NCH, 256], BF16) for _ in range(4)]
        self.BIGB_off = al.off
        self.BIGB = al([NCH * T], BF16)
        self.CAT = self.view(self.BIGB_off, [NCH, T], BF16)
        self.XN_off = al.off
        self.XN = al([NCH, T], BF16)
        self.R_off = al.off
        self.R = al([NCH, T], F32)
        self.free_off = al.off
        d_ident = self.din("ident", [128, 128])
        d_gains = self.din("gains", [128, 4 * NCH])
        d_cst = self.din("cst", [128, NCST])
        P.add("sp", lambda e: e.dma_start(out=self.ident, in_=d_ident), writes=[("ident",)], slot="c0")
        P.add("sp", lambda e: e.dma_start(out=self.gains.rearrange("p a c -> p (a c)"), in_=d_gains),
              writes=[("gains",)], slot="c1")
        P.add("sp", lambda e: e.dma_start(out=self.cst, in_=d_cst), writes=[("cst",)], slot="c2")
        for n in (2048, 1024, 512, 128, 1):
            P.add("pool", lambda e, n=n: e.memset(self.ones[n], 1.0 / n), writes=[("ones", n)])
        P.add("pool", lambda e: e.memset(self.epsc, EPS), writes=[("epsc",)])

    def ccol(self, c, n=1, rows=128):
        return self.cst[:rows, c:c + n]

    def load_slab(self, w, r0, nrows, c0, ncols=256):
        i = self.nxt("wr", 4)
        slab = self.WR[i]
        nk = nrows // 128
        self.P.add("pool", lambda e: e.dma_start(
            out=slab[:, :nk, :ncols], in_=w[r0:r0 + nrows, c0:c0 + ncols].rearrange("(k p) c -> p k c", p=128)),
            writes=[("WR", i)], slot=("WR", i))
        return slab, ("WR", i)

    def load_transpose(self, src, ntok, dst, dst_key, stg):
        P = self.P
        nblk = (ntok + 127) // 128
        for b in range(nblk):
            t0 = b * 128
            nt = min(128, ntok - t0)
            s = self.nxt("ltstg", len(stg))
            P.add("sp", lambda e, s=s, t0=t0, nt=nt: e.dma_start(out=stg[s][:nt, :], in_=src[t0:t0 + nt, :]),
                  writes=[("ltstg", s)], slot=("ltstg", s))
            for cg in range(4):
                ps, psk = self.psum()
                for j in range(4):
                    c = cg * 4 + j
                    P.add("pe", lambda e, ps=ps, s=s, c=c, j=j, nt=nt: e.transpose(
                        out=ps[:, j * 128:j * 128 + nt], in_=stg[s][:nt, c * 128:(c + 1) * 128],
                        identity=self.ident[:nt, :nt]),
                        reads=[("ltstg", s), ("ident",)], writes=[psk])
                src_v = lambda ps, nt: ps[:, :].rearrange("p (j t) -> p j t", j=4)[:, :, :nt]
                if cg % 2 == 0:
                    fn = lambda e, ps=ps, cg=cg, t0=t0, nt=nt: e.activation(
                        out=dst[:, cg * 4:cg * 4 + 4, t0:t0 + nt], in_=src_v(ps, nt), func=AF.Copy)
                    eng = "act"
                else:
                    fn = lambda e, ps=ps, cg=cg, t0=t0, nt=nt: e.tensor_copy(
                        out=dst[:, cg * 4:cg * 4 + 4, t0:t0 + nt], in_=src_v(ps, nt))
                    eng = "dve"
                P.add(eng, fn, reads=[psk], writes=[(dst_key, cg * 4 + j, "all") for j in range(4)])

    def rms_stats(self, src_fn, keys_fn, nchunks, ones_n, ts, rstd, rstd_key):
        P = self.P
        for ti, (t0, n) in enumerate(ts.tiles):
            ps, psk = self.psum()
            for c in range(nchunks):
                i = self.nxt("sq", 3)
                sq, sqk = self.sq[i], ("sq", i)
                P.add("act", lambda e, sq=sq, c=c, t0=t0, n=n: e.activation(
                    out=sq[:, :n], in_=src_fn(c, t0, n), func=AF.Square),
                    reads=keys_fn(c, ti), writes=[sqk])
                P.add("pe", lambda e, ps=ps, sq=sq, c=c, n=n: e.matmul(
                    ps[:, :n], lhsT=self.ones[ones_n], rhs=sq[:, :n], start=(c == 0), stop=(c == nchunks - 1)),
                    reads=[sqk, ("ones", ones_n)], writes=[psk])
            P.add("act", lambda e, ps=ps, t0=t0, n=n: e.activation(
                out=rstd[:, t0:t0 + n], in_=ps[:, :n], func=AF.Ln, bias=self.epsc[:, 0:1], scale=1.0),
                reads=[psk, ("epsc",)], writes=[(rstd_key, ti)])
            P.add("act", lambda e, t0=t0, n=n: e.activation(
                out=rstd[:, t0:t0 + n], in_=rstd[:, t0:t0 + n], func=AF.Exp, scale=-0.5),
                reads=[(rstd_key, ti)], writes=[(rstd_key, ti)])

    def rmsnorm_R(self, which, ts):
        P, R, XN = self.P, self.R, self.XN
        rk = lambda c, ti: [("R", c, ti), ("R", c, "all")]
        self.rms_stats(lambda c, t0, n: R[:, c, t0:t0 + n], rk, NCH, 2048, ts, self.rstd, "rstd")
        for ti, (t0, n) in enumerate(ts.tiles):
            for c in range(NCH):
                P.add("dve", lambda e, c=c, t0=t0, n=n: e.scalar_tensor_tensor(
                    out=XN[:, c, t0:t0 + n], in0=R[:, c, t0:t0 + n], scalar=self.gains[:, which, c:c + 1],
                    in1=self.rstd[:, t0:t0 + n], op0=ALU.mult, op1=ALU.mult),
                    reads=rk(c, ti) + [("rstd", ti), ("gains",)], writes=[("XN", c, ti)])

    def ffn(self, wg, wu, wd, ts, sg):
        P, XN, R = self.P, self.XN, self.R
        groups = [(0, 4), (4, 4), (8, 4), (12, 4), (16, 3), (19, 3)]
        ACTB = self.view(self.BIGB_off, [8, T], BF16)
        WD = [self.view(self.BIGB_off + 4 * T + i * 4 * 256, [8, 256], BF16) for i in range(3)]
        for g0, gn in groups:
            for s in range(g0, g0 + gn):
                c0 = s * 256
                sl_g, kg = self.load_slab(wg, 0, D, c0)
                sl_u, ku = self.load_slab(wu, 0, D, c0)
                for j in range(2):
                    fl = (s - g0) * 2 + j
                    for ti, (t0, n) in enumerate(ts.tiles):
                        pg, pgk = self.psum()
                        pu, puk = self.psum()
                        for ko in range(NCH):
                            P.add("pe", lambda e, pg=pg, sl=sl_g, ko=ko, j=j, t0=t0, n=n: e.matmul(
                                pg[:, :n], lhsT=sl[:, ko, j * 128:(j + 1) * 128], rhs=XN[:, ko, t0:t0 + n],
                                start=(ko == 0), stop=(ko == NCH - 1)),
                                reads=[kg, ("XN", ko, ti)], writes=[pgk])
                        for ko in range(NCH):
                            P.add("pe", lambda e, pu=pu, sl=sl_u, ko=ko, j=j, t0=t0, n=n: e.matmul(
                                pu[:, :n], lhsT=sl[:, ko, j * 128:(j + 1) * 128], rhs=XN[:, ko, t0:t0 + n],
                                start=(ko == 0), stop=(ko == NCH - 1)),
                                reads=[ku, ("XN", ko, ti)], writes=[puk])
                        i = self.nxt("sg", 3)
                        sgt, sgk = sg[i], ("sg", i)
                        P.add("act", lambda e, sgt=sgt, pg=pg, n=n: e.activation(out=sgt[:, :n], in_=pg[:, :n], func=AF.Silu),
                              reads=[pgk], writes=[sgk])
                        P.add("dve", lambda e, sgt=sgt, pu=pu, fl=fl, t0=t0, n=n: e.tensor_tensor(
                            out=ACTB[:, fl, t0:t0 + n], in0=sgt[:, :n], in1=pu[:, :n], op=ALU.mult),
                            reads=[sgk, puk], writes=[("ACTB", fl, ti)])
            nk = gn * 2
            r0 = g0 * 256
            for ds in range(8):
                slot = self.nxt("wd", 3)
                c0 = ds * 256
                P.add("pool", lambda e, slot=slot, c0=c0, r0=r0, nk=nk: e.dma_start(
                    out=WD[slot][:, :nk, :], in_=wd[r0:r0 + nk * 128, c0:c0 + 256].rearrange("(k p) c -> p k c", p=128)),
                    writes=[("WD", slot)], slot=("WD", slot))
                for j in range(2):
                    m = ds * 2 + j
                    for ti, (t0, n) in enumerate(ts.tiles):
                        pd, pdk = self.psum()
                        for k in range(nk):
                            P.add("pe", lambda e, pd=pd, slot=slot, k=k, j=j, t0=t0, n=n, nk=nk: e.matmul(
                                pd[:, :n], lhsT=WD[slot][:, k, j * 128:(j + 1) * 128], rhs=ACTB[:, k, t0:t0 + n],
                                start=(k == 0), stop=(k == nk - 1)),
                                reads=[("WD", slot), ("ACTB", k, ti)], writes=[pdk])
                        P.add("dve", lambda e, pd=pd, m=m, t0=t0, n=n: e.scalar_tensor_tensor(
                            out=R[:, m, t0:t0 + n], in0=pd[:, :n], scalar=0.5, in1=R[:, m, t0:t0 + n],
                            op0=ALU.mult, op1=ALU.add),
                            reads=[pdk, ("R", m, ti), ("R", m, "all")], writes=[("R", m, ti)])

    def transpose_store(self, src_fn, keys_fn, nchunks, ntok, dst, stg):
        P = self.P
        nblk = (ntok + 127) // 128
        ncg = (nchunks + 3) // 4
        for b in range(nblk):
            t0 = b * 128
            nt = min(128, ntok - t0)
            s = self.nxt("tsstg", len(stg))
            for cg in range(ncg):
                ps, psk = self.psum()
                nj = min(4, nchunks - cg * 4)
                for j in range(nj):
                    c = cg * 4 + j
                    P.add("pe", lambda e, ps=ps, c=c, j=j, t0=t0, nt=nt: e.transpose(
                        out=ps[:nt, j * 128:(j + 1) * 128], in_=src_fn(c, t0, nt), identity=self.ident),
                        reads=keys_fn(c) + [("ident",)], writes=[psk])
                if cg % 2 == 0:
                    fn = lambda e, ps=ps, cg=cg, nt=nt, s=s, nj=nj: e.activation(
                        out=stg[s][:nt, cg * 512:cg * 512 + nj * 128], in_=ps[:nt, :nj * 128], func=AF.Copy)
                    eng = "act"
                else:
                    fn = lambda e, ps=ps, cg=cg, nt=nt, s=s, nj=nj: e.tensor_copy(
                        out=stg[s][:nt, cg * 512:cg * 512 + nj * 128], in_=ps[:nt, :nj * 128])
                    eng = "dve"
                P.add(eng, fn, reads=[psk, ("tsstg", s)], writes=[("tsstg", s, cg)])
            P.add("sp", lambda e, s=s, t0=t0, nt=nt: e.dma_start(out=dst[t0:t0 + nt, :], in_=stg[s][:nt, :nchunks * 128]),
                  reads=[("tsstg", s, cg) for cg in range(ncg)], writes=[("tsstg", s)], slot=("tsstg", s))

    def g_project_chunk(self, slab, slabk, j, xin, xkey, ts, zT, zk):
        P = self.P
        for ti, (t0, n) in enumerate(ts.tiles):
            ps, psk = self.psum()
            for ko in range(NCH):
                P.add("pe", lambda e, ps=ps, ko=ko, t0=t0, n=n: e.matmul(
                    ps[:, :n], lhsT=slab[:, ko, j * 128:(j + 1) * 128], rhs=xin[:, ko, t0:t0 + n],
                    start=(ko == 0), stop=(ko == NCH - 1)),
                    reads=[slabk, (xkey, ko, ti)], writes=[psk])
                if ko == 7:
                    yield
            P.add("act", lambda e, ps=ps, t0=t0, n=n: e.activation(out=zT[:, t0:t0 + n], in_=ps[:, :n], func=AF.Copy),
                  reads=[psk], writes=[(zk, ti)])
            yield

    def project_chunk(self, *a):
        _run(self.g_project_chunk(*a))

    def g_chunk_norm(self, zT, zk, gcol, ts, out_f32=None, out_bf=None, ob_key=None, ob_keyfn=None):
        P = self.P
        rstd = self.rstd
        pss = []
        for ti, (t0, n) in enumerate(ts.tiles):
            ps, psk = self.psum()
            pss.append((ps, psk))
            i = self.nxt("sq", 3)
            sq, sqk = self.sq[i], ("sq", i)
            P.add("act", lambda e, sq=sq, t0=t0, n=n: e.activation(out=sq[:, :n], in_=zT[:, t0:t0 + n], func=AF.Square),
                  reads=[(zk, ti)], writes=[sqk])
            P.add("pe", lambda e, ps=ps, sq=sq, n=n: e.matmul(ps[:, :n], lhsT=self.ones[128], rhs=sq[:, :n], start=True, stop=True),
                  reads=[sqk, ("ones", 128)], writes=[psk])
        yield
        for ti, (t0, n) in enumerate(ts.tiles):
            ps, psk = pss[ti]
            P.add("act", lambda e, ps=ps, t0=t0, n=n: e.activation(
                out=rstd[:, t0:t0 + n], in_=ps[:, :n], func=AF.Ln, bias=self.epsc[:, 0:1], scale=1.0),
                reads=[psk, ("epsc",)], writes=[("rstd", ti)])
        for ti, (t0, n) in enumerate(ts.tiles):
            P.add("act", lambda e, t0=t0, n=n: e.activation(
                out=rstd[:, t0:t0 + n], in_=rstd[:, t0:t0 + n], func=AF.Exp, scale=-0.5),
                reads=[("rstd", ti)], writes=[("rstd", ti)])
        yield
        for ti, (t0, n) in enumerate(ts.tiles):
            if out_bf is not None:
                wk = [ob_keyfn(ti)] if ob_keyfn else [(ob_key, ti)]
                P.add("dve", lambda e, t0=t0, n=n: e.scalar_tensor_tensor(
                    out=out_bf[:, t0:t0 + n], in0=zT[:, t0:t0 + n], scalar=gcol, in1=rstd[:, t0:t0 + n],
                    op0=ALU.mult, op1=ALU.mult), reads=[(zk, ti), ("rstd", ti), ("cst",)], writes=wk)
            if out_f32 is not None:
                P.add("dve", lambda e, t0=t0, n=n: e.scalar_tensor_tensor(
                    out=out_f32[:, t0:t0 + n], in0=zT[:, t0:t0 + n], scalar=gcol, in1=rstd[:, t0:t0 + n],
                    op0=ALU.mult, op1=ALU.mult), reads=[(zk, ti), ("rstd", ti), ("cst",)], writes=[(zk, ti)])
        yield

    def chunk_norm(self, *a, **k):
        _run(self.g_chunk_norm(*a, **k))

    def g_to_tokmajor(self, zT, zkeys, ntok, stgt, stk):
        P = self.P
        nblk = (ntok + 127) // 128
        for b0 in range(0, nblk, 4):
            ps, psk = self.psum()
            nb = min(4, nblk - b0)
            for j in range(nb):
                t0 = (b0 + j) * 128
                nt = min(128, ntok - t0)
                P.add("pe", lambda e, ps=ps, j=j, t0=t0, nt=nt: e.transpose(
                    out=ps[:nt, j * 128:(j + 1) * 128], in_=zT[:, t0:t0 + nt], identity=self.ident),
                    reads=zkeys + [("ident",)], writes=[psk])
            nfull = sum(1 for j in range(nb) if (b0 + j) * 128 + 128 <= ntok)
            if nfull:
                P.add("act", lambda e, ps=ps, b0=b0, nfull=nfull: e.activation(
                    out=stgt[:, b0:b0 + nfull, :], in_=ps[:, :nfull * 128].rearrange("p (j f) -> p j f", j=nfull),
                    func=AF.Copy), reads=[psk, (stk, "dma"), (stk, "dma2")], writes=[(stk, "f", b0)])
            if nfull < nb:
                j = nfull
                nt = ntok - (b0 + j) * 128
                P.add("act", lambda e, ps=ps, b0=b0, j=j, nt=nt: e.activation(
                    out=stgt[:nt, b0 + j, :], in_=ps[:nt, j * 128:(j + 1) * 128], func=AF.Copy),
                    reads=[psk, (stk, "dma"), (stk, "dma2")], writes=[(stk, "p")])
            yield

    def to_tokmajor(self, *a):
        _run(self.g_to_tokmajor(*a))

    def stk_keys(self, stk, ntok):
        nblk = (ntok + 127) // 128
        ks = [(stk, "f", b0) for b0 in range(0, nblk, 4) if b0 * 128 + 128 <= ntok]
        if ntok % 128:
            ks.append((stk, "p"))
        return ks

    def build(self):
        nc, P, cfg = self.nc, self.P, self.cfg
        dn = self.din
        mem = dn("mem", [256, D])
        wmem = dn("wmem", [D, 1024])
        gm = dn("g_mem", [128, NCH])
        y = self.dout("y", [T, D])
        o_wk, o_wv = self.dout("o_wk", [T, 1024]), self.dout("o_wv", [T, 1024])
        o_mk, o_mv = self.dout("o_mk", [256, 512]), self.dout("o_mv", [256, 512])
        o_cv = self.dout("o_cv", [TS, 512])
        kvp_k = self.dscr("kvp_k", [1024, TP], BF16)
        kvp_v = self.dscr("kvp_v", [TP, 1024], BF16)
        kvl_k = self.dscr("kvl_k", [1024, TP], BF16)
        kvl_v = self.dscr("kvl_v", [TP, 1024], BF16)
        hscr = self.dscr("hscr", [NCH, 128, T], F32)
        qscr = self.dscr("qscr", [TS, 1024], F32)
        qmscr = self.dscr("qmscr", [TS, 512], F32)
        self.setup()
        R, XN = self.R, self.XN

        al = Alloc(self, self.R_off)
        stg = [al([D], F32), al([D], F32)]
        memT = self.view(self.XN_off, [NCH, 256], F32)
        memN = al([NCH, 256], BF16)
        zTm = [al([256], F32) for _ in range(2)]
        stgm = [al([2, 128], F32) for _ in range(2)]
        gmem = al([NCH], F32)
        MT = TokSet(256, [(0, 256)])
        self.load_transpose(mem, 256, memT, "memT", stg)
        mk_ = lambda c, ti: [("memT", c, "all")]
        self.rms_stats(lambda c, t0, n: memT[:, c, t0:t0 + n], mk_, NCH, 2048, MT, self.rstd, "rstd")
        P.add("sp", lambda e: e.dma_start(out=gmem, in_=gm), writes=[("gmem",)], slot="c3")
        for c in range(NCH):
            P.add("dve", lambda e, c=c: e.scalar_tensor_tensor(
                out=memN[:, c, :], in0=memT[:, c, :], scalar=gmem[:, c:c + 1], in1=self.rstd[:, 0:256],
                op0=ALU.mult, op1=ALU.mult), reads=[("memT", c, "all"), ("rstd", 0), ("gmem",)], writes=[("memN", c, 0)])
        for sidx in range(4):
            slab, sk = self.load_slab(wmem, 0, D, sidx * 256)
            for j in range(2):
                hc = sidx * 2 + j
                zi = self.nxt("zTm", 2)
                z, zk = zTm[zi], ("zTm", zi)
                self.project_chunk(slab, sk, j, memN, "memN", MT, z, zk)
                si = self.nxt("stgm", 2)
                sm, smk = stgm[si], ("stgm", si)
                if hc < 4:
                    self.chunk_norm(z, zk, self.ccol(C_GKM), MT, out_f32=z, out_bf=self.mkT[:, hc, :], ob_key=("mkT", hc))
                    self.to_tokmajor(z, [(zk, 0)], 256, sm, smk)
                    P.add("sp", lambda e, sm=sm, hc=hc: e.dma_start(
                        out=o_mk[:, hc * 128:(hc + 1) * 128].rearrange("(b p) c -> p b c", p=128), in_=sm),
                        reads=self.stk_keys(smk, 256), writes=[(smk, "dma")], slot=("stgm", si))
                else:
                    hv = hc - 4
                    self.to_tokmajor(z, [(zk, 0)], 256, sm, smk)
                    P.add("sp", lambda e, sm=sm, hv=hv: e.dma_start(
                        out=o_mv[:, hv * 128:(hv + 1) * 128].rearrange("(b p) c -> p b c", p=128), in_=sm),
                        reads=self.stk_keys(smk, 256), writes=[(smk, "dma")], slot=("stgm", si))
                    P.add("dve", lambda e, sm=sm, hv=hv: e.tensor_copy(out=self.mvtok[:, :, hv * 128:(hv + 1) * 128], in_=sm),
                          reads=self.stk_keys(smk, 256), writes=[("mvtok", hv)])
        P.barrier()
        P.flush()
        if cfg.get("stop") == "M":
            P.flush(final=True)
            return

        xs = dn("xs", [T, D])
        w1g, w1u, w1d = dn("w1g", [D, DFF]), dn("w1u", [D, DFF]), dn("w1d", [DFF, D])
        win = dn("win", [D, 4608])
        wsgu = dn("wsgu", [4, 128, 128])

        def ffn1_pass(xsrc, ts):
            al = Alloc(self, self.free_off)
            stg = [al([D], F32), al([D], F32)]
            sg = [al([348], F32) for _ in range(3)]
            self.load_transpose(xsrc, ts.n, R, "R", stg)
            self.rmsnorm_R(0, ts)
            self.ffn(w1g, w1u, w1d, ts, sg)
            self.rmsnorm_R(1, ts)

        def kv_project(ts, dk, dv, main, al, zTs=None, stgs=None):
            if zTs is None:
                zTs = [al([T], F32) for _ in range(3)]
                stgs = [al([9, 128], F32) for _ in range(2)]
            nzt_ = len(zTs)
            nbs = [al([T], BF16) for _ in range(2)]
            vbs = [al([8, 128], BF16) for _ in range(2)]
            pipe = Pipe(1)
            slabs = {}

            def front(hc):
                sidx, j = hc // 2, hc % 2
                if j == 0:
                    slabs[sidx] = self.load_slab(win, 0, D, 1024 + sidx * 256)
                slab, sk = slabs[sidx]
                zi = self.nxt("zT%d" % nzt_, nzt_)
                z, zk = zTs[zi], ("zT", zi)
                yield from self.g_project_chunk(slab, sk, j, XN, "XN", ts, z, zk)
                return (hc, z, zk)

            def back(r):
                hc, z, zk = r
                si = self.nxt("stgs", 2)
                sm, smk = stgs[si], ("stgs", si)
                if hc < 8:
                    ni = self.nxt("nb", 2)
                    nb, nbk = nbs[ni], ("nb", ni)
                    yield from self.g_chunk_norm(z, zk, self.ccol(C_GKA), ts, out_f32=(z if main else None), out_bf=nb, ob_key=nbk)
                    P.add("sp", lambda e, nb=nb, hc=hc: e.dma_start(out=dk[hc * 128:(hc + 1) * 128, :], in_=nb[:, 0:TP]),
                          reads=[(nbk, ti) for ti in range(3)], writes=[(nbk, ti) for ti in range(3)], slot=("nb", ni))
                    if main:
                        yield from self.g_to_tokmajor(z, [(zk, ti) for ti in range(3)], T, sm, smk)
                        P.add("sp", lambda e, sm=sm, hc=hc: e.dma_start(
                            out=o_wk[0:TP, hc * 128:(hc + 1) * 128].rearrange("(b p) c -> p b c", p=128), in_=sm[:, 0:8, :]),
                            reads=self.stk_keys(smk, T), writes=[(smk, "dma")], slot=("stgs", si))
                        P.add("sp", lambda e, sm=sm, hc=hc: e.dma_start(
                            out=o_wk[TP:T, hc * 128:(hc + 1) * 128], in_=sm[:TS, 8, :]),
                            reads=self.stk_keys(smk, T), writes=[(smk, "dma2")], slot=("stgs2", si))
                        P.add("dve", lambda e, sm=sm, hc=hc: e.tensor_copy(out=self.ks[:TS, hc * 128:(hc + 1) * 128], in_=sm[:TS, 8, :]),
                              reads=self.stk_keys(smk, T), writes=[("ks", hc)])
                else:
                    hv = hc - 8
                    yield from self.g_to_tokmajor(z, [(zk, ti) for ti in range(3)], ts.n, sm, smk)
                    vi = self.nxt("vb", 2)
                    vb, vbk = vbs[vi], ("vb", vi)
                    P.add("dve", lambda e, sm=sm, vb=vb: e.tensor_copy(out=vb, in_=sm[:, 0:8, :]),
                          reads=self.stk_keys(smk, ts.n), writes=[vbk])
                    P.add("sp", lambda e, vb=vb, hv=hv: e.dma_start(
                        out=dv[:, hv * 128:(hv + 1) * 128].rearrange("(b p) c -> p b c", p=128), in_=vb),
                        reads=[vbk], writes=[vbk], slot=("vb", vi))
                    if main:
                        P.add("sp", lambda e, sm=sm, hv=hv: e.dma_start(
                            out=o_wv[0:TP, hv * 128:(hv + 1) * 128].rearrange("(b p) c -> p b c", p=128), in_=sm[:, 0:8, :]),
                            reads=self.stk_keys(smk, T), writes=[(smk, "dma")], slot=("stgs", si))
                        P.add("sp", lambda e, sm=sm, hv=hv: e.dma_start(
                            out=o_wv[TP:T, hv * 128:(hv + 1) * 128], in_=sm[:TS, 8, :]),
                            reads=self.stk_keys(smk, T), writes=[(smk, "dma2")], slot=("stgs2", si))
                        P.add("dve", lambda e, sm=sm, hv=hv: e.tensor_copy(out=self.vsa[:TS, hv, 0:128], in_=sm[:TS, 8, :]),
                              reads=self.stk_keys(smk, T), writes=[("vsa", hv)])
                yield

            for hc in range(16):
                pipe.add(lambda hc=hc: front(hc), back)
            pipe.drain()

        if cfg.get("pass0", True):
            xprev = dn("xprev", [TP, D])
            ffn1_pass(xprev, PREV)
            kv_project(PREV, kvp_k, kvp_v, False, Alloc(self, self.BIGB_off))
            P.barrier()
            P.flush()

        ffn1_pass(xs, MAIN)
        for c in range(NCH):
            P.add("sp", lambda e, c=c: e.dma_start(out=hscr[c], in_=R[:, c, :]),
                  reads=[("R", c, ti) for ti in range(3)] + [("R", c, "all")], writes=[("hscr", c)], slot=("hs", c % 2))
        P.barrier()
        P.flush()

        CAT = self.CAT
        al = Alloc(self, self.R_off)
        self.qs = al([1024], F32)
        self.ks = al([1024], F32)
        self.vsa = al([8, 130], F32)
        self.us = al([512], F32)
        self.vns = al([512], F32)
        self.qms = al([512], F32)
        self.oas = al([2048], F32)
        samp_off = al.off
        P.add("pool", lambda e: e.memset(self.vsa[:TS, :, 128:130], 1.0), writes=[("vsa1",)])
        zTs = [al([T], F32) for _ in range(4)]
        stgs = [al([9, 128], F32) for _ in range(2)]
        WT = al([4, 128], BF16)
        wtmp = al([4, 128], F32)
        for g in range(4):
            P.add("sp", lambda e, g=g: e.dma_start(out=wtmp[:, g, :], in_=wsgu[g]), writes=[("wtmp", g)], slot=("wtmp", g))
        for g in range(4):
            P.add("dve", lambda e, g=g: e.tensor_tensor(out=wtmp[:, g, :], in0=wtmp[:, g, :], in1=self.cst[:, C_TRIL:C_TRIL + 128], op=ALU.mult),
                  reads=[("wtmp", g), ("cst",)], writes=[("wtmp", g)])
            ps, psk = self.psum()
            P.add("pe", lambda e, ps=ps, g=g: e.transpose(out=ps[:, 0:128], in_=wtmp[:, g, :], identity=self.ident),
                  reads=[("wtmp", g), ("ident",)], writes=[psk])
            P.add("act", lambda e, ps=ps, g=g: e.activation(out=WT[:, g, :], in_=ps[:, 0:128], func=AF.Copy),
                  reads=[psk], writes=[("WT", g)])

        kv_project(MAIN, kvl_k, kvl_v, True, al, zTs, stgs)
        vtok = al([8, 128], BF16)
        obf = al([TP], F32)
        Pm = [al([512], BF16) for _ in range(2)]
        rden = al([512], F32)
        qmn = al([T], BF16)

        def tiny_T(zsrc, zkeys, dst_fn, dkey):
            ps, psk = self.psum()
            P.add("pe", lambda e, ps=ps: e.transpose(out=ps[:TS, 0:128], in_=zsrc[:, TP:T], identity=self.ident),
                  reads=zkeys + [("ident",)], writes=[psk])
            P.add("act", lambda e, ps=ps: e.activation(out=dst_fn(), in_=ps[:TS, 0:128], func=AF.Copy),
                  reads=[psk], writes=[dkey])

        pipeA = Pipe(1)
        nzt = len(zTs)

        def projA(col0, ncols, j, slabref):
            if slabref[0] is None:
                slabref[0] = self.load_slab(win, 0, D, col0, ncols)
            slab, sk = slabref[0]
            zi = self.nxt("zT%d" % nzt, nzt)
            z, zk = zTs[zi], ("zT", zi)
            yield from self.g_project_chunk(slab, sk, j, XN, "XN", MAIN, z, zk)
            return z, zk

        ustate = {}
        for g in range(4):
            def back_u(r, g=g):
                u, uk = r
                ustate[g] = (u, uk)
                tiny_T(u, [(uk, 2)], lambda g=g: self.us[:TS, g * 128:(g + 1) * 128], ("us", g))
                yield

            def back_v(r, g=g):
                v, vk = r
                u, uk = ustate[g]
                yield from self.g_chunk_norm(v, vk, self.ccol(C_GSGU + g), MAIN, out_f32=v)
                si = self.nxt("stgs", 2)
                sm, smk = stgs[si], ("stgs", si)
                yield from self.g_to_tokmajor(v, [(vk, ti) for ti in range(3)], T, sm, smk)
                P.add("dve", lambda e, sm=sm: e.tensor_copy(out=vtok, in_=sm[:, 0:8, :]),
                      reads=self.stk_keys(smk, T), writes=[("vtok",)])
                P.add("dve", lambda e, sm=sm, g=g: e.tensor_copy(out=self.vns[:TS, g * 128:(g + 1) * 128], in_=sm[:TS, 8, :]),
                      reads=self.stk_keys(smk, T), writes=[("vns", g)])
                for half in range(2):
                    ps, psk = self.psum()
                    for j in range(4):
                        n = half * 4 + j
                        P.add("pe", lambda e, ps=ps, n=n, j=j, g=g: e.matmul(
                            ps[:, j * 128:(j + 1) * 128], lhsT=vtok[:, n, :], rhs=WT[:, g, :], start=True, stop=True),
                            reads=[("vtok",), ("WT", g)], writes=[psk])
                    q0 = half * 512
                    P.add("dve", lambda e, ps=ps, g=g, q0=q0: e.tensor_tensor(
                        out=obf[:, q0:q0 + 512].rearrange("p (j t) -> p j t", j=4),
                        in0=ps[:, :].rearrange("p (j t) -> p j t", j=4),
                        in1=self.cst[:, C_BSGU + g * 128:C_BSGU + (g + 1) * 128].unsqueeze(1).to_broadcast([128, 4, 128]),
                        op=ALU.add), reads=[psk, ("cst",)], writes=[("obf", half)])
                    P.add("dve", lambda e, g=g, q0=q0, u=u: e.tensor_tensor(
                        out=CAT[:, 8 + g, q0:q0 + 512], in0=obf[:, q0:q0 + 512], in1=u[:, q0:q0 + 512], op=ALU.mult),
                        reads=[("obf", half)] + [(uk, ti) for ti in range(3)], writes=[("CAT", 8 + g, "p", half)])
                    yield

            pipeA.add(lambda g=g: projA(3072 + g * 128, 128, 0, [None]), back_u)
            pipeA.add(lambda g=g: projA(3584 + g * 128, 128, 0, [None]), back_v)

        for hm in range(4):
            def back_qm(r, hm=hm):
                z, zk = r
                yield from self.g_chunk_norm(z, zk, self.ccol(C_GQM), MAIN, out_f32=z, out_bf=qmn, ob_key="qmn")
                tiny_T(z, [(zk, 2)], lambda hm=hm: self.qms[:TS, hm * 128:(hm + 1) * 128], ("qms", hm))
                for qt in range(2):
                    q0 = qt * 512
                    po, pok = self.psum()
                    pdn, pdk = self.psum()
                    pms = []
                    for mt in range(2):
                        ps, psk = self.psum()
                        P.add("pe", lambda e, ps=ps, mt=mt, hm=hm, q0=q0: e.matmul(
                            ps[:, :], lhsT=self.mkT[:, hm, mt * 128:(mt + 1) * 128], rhs=qmn[:, q0:q0 + 512], start=True, stop=True),
                            reads=[("qmn", ti) for ti in range(3)], writes=[psk])
                        pi = self.nxt("Pm", 2)
                        pm_, pmk = Pm[pi], ("Pm", pi)
                        P.add("act", lambda e, ps=ps, pm_=pm_: e.activation(out=pm_, in_=ps[:, :], func=AF.Exp, scale=SCALE),
                              reads=[psk], writes=[pmk])
                        pms.append((pm_, pmk))
                    yield
                    for mt in range(2):
                        pm_, pmk = pms[mt]
                        P.add("pe", lambda e, po=po, pm_=pm_, mt=mt, hm=hm: e.matmul(
                            po[:, :], lhsT=self.mvtok[:, mt, hm * 128:(hm + 1) * 128], rhs=pm_, start=(mt == 0), stop=(mt == 1)),
                            reads=[pmk], writes=[pok])
                        P.add("pe", lambda e, pdn=pdn, pm_=pm_, mt=mt: e.matmul(
                            pdn[:, :], lhsT=self.ones[1], rhs=pm_, start=(mt == 0), stop=(mt == 1)),
                            reads=[pmk], writes=[pdk])
                    P.add("act", lambda e, pdn=pdn: e.activation(out=rden, in_=pdn[:, :], func=AF.Ln), reads=[pdk], writes=[("rden",)])
                    P.add("act", lambda e: e.activation(out=rden, in_=rden, func=AF.Exp, scale=-1.0), reads=[("rden",)], writes=[("rden",)])
                    P.add("dve", lambda e, po=po, hm=hm, q0=q0: e.tensor_tensor(
                        out=CAT[:, 12 + hm, q0:q0 + 512], in0=po[:, :], in1=rden, op=ALU.mult),
                        reads=[pok, ("rden",)], writes=[("CAT", 12 + hm, "p", qt)])
                    yield

            pipeA.add(lambda hm=hm: projA(4096 + hm * 128, 128, 0, [None]), back_qm)

        qslab = {}
        for h in range(8):
            def back_q(r, h=h):
                z, zk = r
                yield from self.g_chunk_norm(z, zk, self.ccol(C_GQA), MAIN, out_f32=z, out_bf=CAT[:, h, :],
                                             ob_keyfn=lambda ti, h=h: ("QN", h, ti))
                tiny_T(z, [(zk, 2)], lambda h=h: self.qs[:TS, h * 128:(h + 1) * 128], ("qs", h))
                yield

            ref = qslab.setdefault(h // 2, [None])
            pipeA.add(lambda h=h, ref=ref: projA((h // 2) * 256, 256, h % 2, ref), back_q)
        pipeA.drain()
        P.add("sp", lambda e: e.dma_start(out=o_cv, in_=self.vns[:TS, :]), reads=[("vns", g) for g in range(4)],
              writes=[("ocv",)], slot="ocv")
        P.add("sp", lambda e: e.dma_start(out=qscr, in_=self.qs[:TS, :]), reads=[("qs", h) for h in range(8)],
              writes=[("qscr",)], slot="qscr")
        P.add("sp", lambda e: e.dma_start(out=qmscr, in_=self.qms[:TS, :]), reads=[("qms", h) for h in range(4)],
              writes=[("qmscr",)], slot="qmscr")
        P.barrier()
        P.flush()
        if cfg.get("stop") == "A":
            P.flush(final=True)
            return

        pbias = dn("pbias", [8, 128, 576])
        al = Alloc(self, samp_off)
        KT = [al([2048], BF16) for _ in range(2)]
        V1 = [al([9, 128], BF16) for _ in range(2)]
        V4 = [al([3, 4, 128], BF16) for _ in range(2)]
        V16 = [al([16, 128], BF16) for _ in range(2)]
        PB = [al([576], F32) for _ in range(2)]
        tmp = [al([256], F32) for _ in range(3)]
        Pt = [al([256], BF16) for _ in range(6)]
        rden = al([TP], F32)
        self.ring = [0, 1, 2, 3]
        OB = [self.PS[4], self.PS[5]]
        DB = [self.PS[6], self.PS[7]]
        OK_ = [("ps", 4), ("ps", 5)]
        DK_ = [("ps", 6), ("ps", 7)]
        zc = self.ccol(C_ZERO)
        pipeB = PipeLag(3)

        def score(lhsT_ap, rhs_ap, bias_ap, pmcol, width, ldk, qk):
            ps, psk = self.psum()
            P.add("pe", lambda e, ps=ps: e.matmul(ps[:, 0:width], lhsT=lhsT_ap, rhs=rhs_ap, start=True, stop=True),
                  reads=ldk + qk, writes=[psk])
            ti_ = self.nxt("tmpB", 3)
            tm, tmk = tmp[ti_], ("tmpB", ti_)
            P.add("dve", lambda e, ps=ps, tm=tm: e.scalar_tensor_tensor(
                out=tm[:, :width], in0=ps[:, :width], scalar=SCALE, in1=bias_ap, op0=ALU.mult, op1=ALU.add),
                reads=[psk] + ldk, writes=[tmk])
            pi = self.nxt("PtB", 6)
            pt, ptk = Pt[pi], ("PtB", pi)
            P.add("act", lambda e, tm=tm, pt=pt: e.activation(out=pt[:, :width], in_=tm[:, :width], func=AF.Exp,
                                                               bias=pmcol, scale=1.0),
                  reads=[tmk], writes=[ptk])
            return pt, ptk

        def pv(pt, ptk, poff, w, vl_ap, bank, out_sl, ldk):
            P.add("pe", lambda e: e.matmul(OB[bank][:, out_sl], lhsT=vl_ap, rhs=pt[:, poff:poff + w],
                                           start=False, stop=False, skip_group_check=True),
                  reads=[ptk] + ldk, writes=[OK_[bank]])
            P.add("pe", lambda e: e.matmul(DB[bank][:, out_sl], lhsT=self.ones[1], rhs=pt[:, poff:poff + w],
                                           start=False, stop=False, skip_group_check=True),
                  reads=[ptk], writes=[DK_[bank]])

        for h in range(8):
            hi = h % 2
            kt, v1, v4, v16, pb = KT[hi], V1[hi], V4[hi], V16[hi], PB[hi]
            hsl = slice(h * 128, (h + 1) * 128)
            ld = lambda fn, k: P.add("sp", fn, writes=[("ld", hi, k)], slot=("ld", hi, k))
            ld(lambda e, kt=kt, hsl=hsl: e.dma_start(out=kt[:, 0:TP], in_=kvp_k[hsl, :]), 0)
            ld(lambda e, kt=kt, hsl=hsl: e.dma_start(out=kt[:, TP:2 * TP], in_=kvl_k[hsl, :]), 1)
            ld(lambda e, v1=v1, hsl=hsl: e.dma_start(out=v1[:, 0, :], in_=kvp_v[896:1024, hsl]), 2)
            ld(lambda e, v1=v1, hsl=hsl: e.dma_start(out=v1[:, 1:9, :], in_=kvl_v[:, hsl].rearrange("(t p) c -> p t c", p=128)), 3)
            ld(lambda e, v4=v4, hsl=hsl: e.dma_start(out=v4[:, 0, :, :], in_=kvp_v[512:1024, hsl].rearrange("(i r) c -> i r c", r=4)), 4)
            ld(lambda e, v4=v4, hsl=hsl: e.dma_start(out=v4[:, 1, :, :], in_=kvl_v[0:512, hsl].rearrange("(i r) c -> i r c", r=4)), 5)
            ld(lambda e, v4=v4, hsl=hsl: e.dma_start(out=v4[:, 2, :, :], in_=kvl_v[512:1024, hsl].rearrange("(i r) c -> i r c", r=4)), 6)
            ld(lambda e, v16=v16, hsl=hsl: e.dma_start(out=v16[0:64, :, :], in_=kvp_v[:, hsl].rearrange("(i r) c -> i r c", r=16)), 7)
            ld(lambda e, v16=v16, hsl=hsl: e.dma_start(out=v16[64:128, :, :], in_=kvl_v[:, hsl].rearrange("(i r) c -> i r c", r=16)), 8)
            ld(lambda e, pb=pb, h=h: e.dma_start(out=pb, in_=pbias[h]), 9)
            ldk = [("ld", hi, k) for k in range(10)]
            qn = CAT[:, h, :]
            qk = [("QN", h)]
            items = []

            for kt_ in range(7, 16):
                qlo = max(0, (kt_ - 8) * 128)
                qhi = min(TP, (kt_ - 6) * 128)
                w = qhi - qlo
                joff = qlo - (kt_ - 8) * 128

                def fr(kt_=kt_, qlo=qlo, w=w, joff=joff, kt=kt, qn=qn, pb=pb, ldk=ldk, qk=qk):
                    return score(kt[:, kt_ * 128:(kt_ + 1) * 128], qn[:, qlo:qlo + w], pb[:, joff:joff + w],
                                 self.ccol(C_PM0) if kt_ == 7 else zc, w, ldk, qk)

                def bk(r, kt_=kt_, qlo=qlo, w=w, v1=v1, ldk=ldk):
                    pt, ptk = r
                    for o in range(0, w, 128):
                        q0 = qlo + o
                        pv(pt, ptk, o, 128, v1[:, kt_ - 7, :], q0 // 512, slice(q0 % 512, q0 % 512 + 128), ldk)
                items.append((fr, bk))
            for r in range(4):
                for ct in range(1, 4):
                    clo = max(256, ct * 128)
                    chi = min(512, ct * 128 + 256)
                    w = chi - clo
                    joff = clo - ct * 128

                    def fr(ct=ct, r=r, clo=clo, w=w, joff=joff, kt=kt, qn=qn, pb=pb, ldk=ldk, qk=qk):
                        return score(kt[:, ct * 512 + r:(ct + 1) * 512:4], qn[:, (clo - 256) * 4 + r:(clo - 256 + w) * 4:4],
                                     pb[:, 256 + joff:256 + joff + w], self.ccol(C_PM0) if ct == 1 else zc, w, ldk, qk)

                    def bk(rr_, ct=ct, r=r, clo=clo, w=w, v4=v4, ldk=ldk):
                        pt, ptk = rr_
                        for o in range(0, w, 128):
                            c0 = clo + o
                            pv(pt, ptk, o, 128, v4[:, ct - 1, r, :], (c0 - 256) // 128, slice(r, 512, 4), ldk)
                    items.append((fr, bk))
            for r0 in range(0, 16, 4):
                def fr(r0=r0, kt=kt, qn=qn, pb=pb, ldk=ldk, qk=qk):
                    ps, psk = self.psum()
                    for rr in range(4):
                        r = r0 + rr
                        P.add("pe", lambda e, ps=ps, rr=rr, a=kt[:, r:2048:16], b_=qn[:, r:TP:16]: e.matmul(
                            ps[:, rr * 64:(rr + 1) * 64], lhsT=a, rhs=b_, start=True, stop=True),
                            reads=ldk + qk, writes=[psk])
                    ti_ = self.nxt("tmpB", 3)
                    tm, tmk = tmp[ti_], ("tmpB", ti_)
                    P.add("dve", lambda e, ps=ps, tm=tm, b16=pb[:, 512:576].unsqueeze(1).to_broadcast([128, 4, 64]): e.scalar_tensor_tensor(
                        out=tm[:, :].rearrange("p (a b) -> p a b", a=4), in0=ps[:, 0:256].rearrange("p (a b) -> p a b", a=4),
                        scalar=SCALE, in1=b16, op0=ALU.mult, op1=ALU.add),
                        reads=[psk] + ldk, writes=[tmk])
                    pi = self.nxt("PtB", 6)
                    pt, ptk = Pt[pi], ("PtB", pi)
                    P.add("act", lambda e, tm=tm, pt=pt: e.activation(out=pt, in_=tm, func=AF.Exp, bias=self.ccol(C_PM1), scale=1.0),
                          reads=[tmk], writes=[ptk])
                    return pt, ptk

                def bk(rr_, r0=r0, v16=v16, ldk=ldk):
                    pt, ptk = rr_
                    for rr in range(4):
                        r = r0 + rr
                        for bank in range(2):
                            pv(pt, ptk, rr * 64 + bank * 32, 32, v16[:, r, :], bank, slice(r, 512, 16), ldk)
                items.append((fr, bk))

            def first_back(r, bk0=items[0][1]):
                for b in range(2):
                    P.add("dve", lambda e, b=b: e.memset(OB[b][:, :], 0.0), writes=[OK_[b]])
                    P.add("dve", lambda e, b=b: e.memset(DB[b][:, :], 0.0), writes=[DK_[b]])
                bk0(r)

            def last_back(r, bkl=items[-1][1], h=h, qk=qk):
                bkl(r)
                for b in range(2):
                    P.add("act", lambda e, b=b: e.activation(out=rden[:, b * 512:(b + 1) * 512], in_=DB[b][:, :], func=AF.Ln),
                          reads=[DK_[b]], writes=[("rdenB", b)])
                    P.add("act", lambda e, b=b: e.activation(out=rden[:, b * 512:(b + 1) * 512], in_=rden[:, b * 512:(b + 1) * 512],
                                                             func=AF.Exp, scale=-1.0),
                          reads=[("rdenB", b)], writes=[("rdenB", b)])
                    P.add("dve", lambda e, b=b, h=h: e.tensor_tensor(
                        out=CAT[:, h, b * 512:(b + 1) * 512], in0=OB[b][:, :], in1=rden[:, b * 512:(b + 1) * 512], op=ALU.mult),
                        reads=[OK_[b], ("rdenB", b)], writes=qk)

            items[0] = (items[0][0], first_back)
            items[-1] = (items[-1][0], last_back)
            for fr_, bk_ in items:
                pipeB.add(fr_, bk_)
        pipeB.drain()
        self.ring = list(range(8))
        P.barrier()
        P.flush()
        if cfg.get("stop") == "B":
            P.flush(final=True)
            return
        if cfg.get("stop") == "Bd":
            dbg = Alloc(self, samp_off)([D], F32)
            for b in range(8):
                P.add("dve", lambda e, b=b: e.tensor_copy(out=dbg.rearrange("p (c t) -> p c t", c=NCH), in_=CAT[:, :, b * 128:(b + 1) * 128]),
                      writes=[("dbg",)])
                P.add("sp", lambda e, b=b: e.dma_start(out=y[b * 128:(b + 1) * 128, :], in_=dbg), reads=[("dbg",)], writes=[("dbg",)], slot="dbg")
            P.flush(final=True)
            return

        cwk, cwv = dn("cwk", [4, 2048, 1024]), dn("cwv", [4, 2048, 1024])
        cmk, cmv = dn("cmk", [4, 256, 512]), dn("cmv", [4, 256, 512])
        alx = Alloc(self, self.XN_off)
        Kc0 = alx([9, 1024], BF16)
        qb = [alx([1024], F32) for _ in range(2)]
        prod = alx([1024], F32)
        assert alx.off <= self.R_off
        alw = Alloc(self, self.WR_off)
        Kc1 = alw([9, 1024], BF16)
        MKs1 = alw([2, 512], BF16)
        assert alw.off <= self.BIGB_off
        al = Alloc(self, samp_off)
        Vc = al([9, 8, 130], BF16)
        qmb = [al([512], F32) for _ in range(2)]
        L = al([4, 3, 8], F32)
        Pe = al([4, 24, 16], BF16)
        Ln = al([16, 3, 8], F32)
        Pn = al([16, 8], F32)
        MKs0 = al([2, 512], BF16)
        MVs = al([2, 4, 130], BF16)
        Lm = al([4, 2, 4], F32)
        Pme = al([4, 8, 16], BF16)
        rd = al([16], F32)
        self.ring = [0, 1, 2]
        OS = [self.PS[3], self.PS[4], self.PS[5]]
        OSK = [("ps", 3), ("ps", 4), ("ps", 5)]
        OM = [self.PS[6], self.PS[7]]
        OMK = [("ps", 6), ("ps", 7)]
        for i in range(3):
            P.add("dve", lambda e, i=i: e.memset(OS[i][:, :], 0.0), writes=[OSK[i]])
        for i in range(2):
            P.add("dve", lambda e, i=i: e.memset(OM[i][:, :], 0.0), writes=[OMK[i]])
        P.add("pool", lambda e: e.memset(Vc[:, :, :, 128:130], 1.0), writes=[("Vc1",)])
        P.add("pool", lambda e: e.memset(MVs[:, :, :, 128:130], 1.0), writes=[("MVs1",)])
        Z = self.cst[:, C_Z:C_Z + 112].rearrange("p (t x) -> p t x", t=4)
        biass = self.cst[:, C_BIASS:C_BIASS + 96].rearrange("p (a b c) -> p a b c", a=4, b=3)
        for bs in range(4):
            kb = bs % 2
            Kc = (Kc0, Kc1)[kb]
            MKs = (MKs0, MKs1)[kb]
            P.add("pool", lambda e, bs=bs, Kc=Kc: e.dma_start(out=Kc[:, 0, :], in_=cwk[bs, 1920:2048, :]), writes=[("Kc", kb, 0)], slot=("Kc", kb, 0))
            P.add("pool", lambda e, bs=bs, Kc=Kc: e.dma_start(out=Kc[:, 1:5, :], in_=cwk[bs, 1536:2048, :].rearrange("(i t) c -> i t c", t=4)),
                  writes=[("Kc", kb, 1)], slot=("Kc", kb, 1))
            P.add("pool", lambda e, bs=bs, Kc=Kc: e.dma_start(out=Kc[:, 5:9, :], in_=cwk[bs].rearrange("(i r) c -> i r c", r=16)[:, 0:4, :]),
                  writes=[("Kc", kb, 2)], slot=("Kc", kb, 2))
            P.add("pool", lambda e, bs=bs, MKs=MKs: e.dma_start(out=MKs, in_=cmk[bs].rearrange("(m p) c -> p m c", p=128)),
                  writes=[("MKs", kb)], slot=("MKs", kb))
            P.add("pool", lambda e, bs=bs: e.dma_start(out=Vc[:, 0, :, 0:128], in_=cwv[bs, 1920:2048, :].rearrange("i (h d) -> i h d", h=8)),
                  reads=[("Vc1",)], writes=[("Vc", 0)], slot=("Vc", 0))
            for t in range(4):
                P.add("pool", lambda e, bs=bs, t=t: e.dma_start(
                    out=Vc[:, 1 + t, :, 0:128],
                    in_=cwv[bs, 1536:2048, :].rearrange("(i t) (h d) -> i t h d", t=4, h=8)[:, t, :, :]),
                    reads=[("Vc1",)], writes=[("Vc", 1 + t)], slot=("Vc", 1 + t))
                P.add("pool", lambda e, bs=bs, t=t: e.dma_start(
                    out=Vc[:, 5 + t, :, 0:128],
                    in_=cwv[bs].rearrange("(i r) (h d) -> i r h d", r=16, h=8)[:, t, :, :]),
                    reads=[("Vc1",)], writes=[("Vc", 5 + t)], slot=("Vc", 5 + t))
            for mt in range(2):
                P.add("pool", lambda e, bs=bs, mt=mt: e.dma_start(
                    out=MVs[:, mt, :, 0:128], in_=cmv[bs, mt * 128:(mt + 1) * 128, :].rearrange("i (h d) -> i h d", h=4)),
                    reads=[("MVs1",)], writes=[("MVs", mt)], slot=("MVs", mt))
            kck = [("Kc", kb, i) for i in range(3)]
            vck = [("Vc", i) for i in range(9)]
            for t in range(4):
                q = bs * 4 + t
                bi = self.nxt("qb", 2)
                qbt, qbk = qb[bi], ("qb", bi)
                qmt, qmk = qmb[bi], ("qmb", bi)
                P.add("sp", lambda e, qbt=qbt, q=q: e.dma_start(out=qbt, in_=qscr[q:q + 1, :].to_broadcast([128, 1024])),
                      writes=[qbk], slot=("qb", bi))
                P.add("sp", lambda e, qmt=qmt, q=q: e.dma_start(out=qmt, in_=qmscr[q:q + 1, :].to_broadcast([128, 512])),
                      writes=[qmk], slot=("qmb", bi))
                for dl in range(3):
                    ktile = Kc[:, 0, :] if dl == 0 else Kc[:, 1 + (dl - 1) * 4 + t, :]
                    P.add("dve", lambda e, ktile=ktile, qbt=qbt: e.tensor_tensor(out=prod, in0=ktile, in1=qbt, op=ALU.mult),
                          reads=kck + [qbk], writes=[("prod",)])
                    P.add("dve", lambda e, t=t, dl=dl: e.tensor_reduce(
                        out=L[:, t, dl, :], in_=prod.rearrange("p (h d) -> p h d", h=8), axis=AX.X, op=ALU.add),
                        reads=[("prod",)], writes=[("L", t)])
                P.add("dve", lambda e, qbt=qbt: e.tensor_tensor(out=prod[:TS, :], in0=self.ks[:TS, :], in1=qbt[:TS, :], op=ALU.mult),
                      reads=[qbk], writes=[("prod",)])
                P.add("dve", lambda e, q=q: e.tensor_reduce(
                    out=Ln[:TS, q, 0, :], in_=prod[:TS, :].rearrange("p (h d) -> p h d", h=8), axis=AX.X, op=ALU.add),
                    reads=[("prod",)], writes=[("Ln", q)])
                for mt in range(2):
                    P.add("dve", lambda e, mt=mt, qmt=qmt, MKs=MKs: e.tensor_tensor(out=prod[:, 0:512], in0=MKs[:, mt, :], in1=qmt, op=ALU.mult),
                          reads=[("MKs", kb), qmk], writes=[("prod",)])
                    P.add("dve", lambda e, t=t, mt=mt: e.tensor_reduce(
                        out=Lm[:, t, mt, :], in_=prod[:, 0:512].rearrange("p (h d) -> p h d", h=4), axis=AX.X, op=ALU.add),
                        reads=[("prod",)], writes=[("Lm", t)])
            P.add("dve", lambda e: e.scalar_tensor_tensor(out=L, in0=L, scalar=SCALE, in1=biass, op0=ALU.mult, op1=ALU.add),
                  reads=[("L", t) for t in range(4)] + [("cst",)], writes=[("Lb",)])
            P.add("act", lambda e: e.activation(out=L, in_=L, func=AF.Exp), reads=[("Lb",)], writes=[("Lp",)])
            P.add("act", lambda e: e.activation(out=Lm, in_=Lm, func=AF.Exp, scale=SCALE),
                  reads=[("Lm", t) for t in range(4)], writes=[("Lmp",)])
            w0 = 12 - 4 * bs
            for t in range(4):
                P.add("dve", lambda e, t=t, w0=w0: e.tensor_tensor(
                    out=Pe[:, t, :, :], in0=L[:, t, :, :].rearrange("p a b -> p (a b)").unsqueeze(2).to_broadcast([128, 24, 16]),
                    in1=Z[:, t, w0:w0 + 16].unsqueeze(1).to_broadcast([128, 24, 16]), op=ALU.mult),
                    reads=[("Lp",), ("cst",)], writes=[("Pe", t)])
                P.add("dve", lambda e, t=t, w0=w0: e.tensor_tensor(
                    out=Pme[:, t, :, :], in0=Lm[:, t, :, :].rearrange("p a b -> p (a b)").unsqueeze(2).to_broadcast([128, 8, 16]),
                    in1=Z[:, t, w0:w0 + 16].unsqueeze(1).to_broadcast([128, 8, 16]), op=ALU.mult),
                    reads=[("Lmp",), ("cst",)], writes=[("Pme", t)])
            for t in range(4):
                for dl in range(3):
                    vi = 0 if dl == 0 else 1 + (dl - 1) * 4 + t
                    for h in range(8):
                        bk, off = h // 3, (h % 3) * 129
                        P.add("pe", lambda e, t=t, dl=dl, h=h, vi=vi, bk=bk, off=off: e.matmul(
                            OS[bk][:TS, off:off + 129], lhsT=Pe[:, t, dl * 8 + h, :], rhs=Vc[:, vi, h, 0:129],
                            start=False, stop=False, skip_group_check=True),
                            reads=[("Pe", t)] + vck, writes=[OSK[bk]])
                for mt in range(2):
                    for h in range(4):
                        bk, off = h // 3, (h % 3) * 129
                        P.add("pe", lambda e, t=t, mt=mt, h=h, bk=bk, off=off: e.matmul(
                            OM[bk][:TS, off:off + 129], lhsT=Pme[:, t, mt * 4 + h, :], rhs=MVs[:, mt, h, 0:129],
                            start=False, stop=False, skip_group_check=True),
                            reads=[("Pme", t), ("MVs", 0), ("MVs", 1)], writes=[OMK[bk]])
        biasn = self.cst[:TS, C_BIASN:C_BIASN + 384].rearrange("p (q c h) -> p q c h", q=16, c=3)
        Lnf = Ln[:TS, :, :, :]
        P.add("dve", lambda e: e.tensor_copy(out=Ln[:TS, :, 1, :], in_=Ln[:TS, :, 0, :]), reads=[("Ln", q) for q in range(16)], writes=[("Ln1",)])
        P.add("dve", lambda e: e.tensor_copy(out=Ln[:TS, :, 2, :], in_=Ln[:TS, :, 0, :]), reads=[("Ln1",)], writes=[("Ln2",)])
        P.add("dve", lambda e: e.scalar_tensor_tensor(out=Lnf, in0=Lnf, scalar=SCALE, in1=biasn, op0=ALU.mult, op1=ALU.add),
              reads=[("Ln2",), ("cst",)], writes=[("Lnb",)])
        P.add("act", lambda e: e.activation(out=Lnf, in_=Lnf, func=AF.Exp), reads=[("Lnb",)], writes=[("Lnp",)])
        P.add("dve", lambda e: e.tensor_reduce(out=Pn[:TS, :, :], in_=Ln[:TS, :, :, :].rearrange("p q c h -> p q h c"),
                                               axis=AX.X, op=ALU.add), reads=[("Lnp",)], writes=[("Pn",)])
        for h in range(8):
            bk, off = h // 3, (h % 3) * 129
            P.add("pe", lambda e, h=h, bk=bk, off=off: e.matmul(
                OS[bk][:TS, off:off + 129], lhsT=Pn[:TS, :, h], rhs=self.vsa[:TS, h, 0:129],
                start=False, stop=False, skip_group_check=True),
                reads=[("Pn",)], writes=[OSK[bk]])
        for h in range(8):
            bk, off = h // 3, (h % 3) * 129
            P.add("dve", lambda e, h=h, bk=bk, off=off: e.reciprocal(out=rd[:TS, h:h + 1], in_=OS[bk][:TS, off + 128:off + 129]),
                  reads=[OSK[bk]], writes=[("rd", h)])
            P.add("dve", lambda e, h=h, bk=bk, off=off: e.tensor_scalar(
                out=self.oas[:TS, h * 128:(h + 1) * 128], in0=OS[bk][:TS, off:off + 128], scalar1=rd[:TS, h:h + 1], scalar2=None,
                op0=ALU.mult), reads=[OSK[bk], ("rd", h)], writes=[("oas", h)])
        for h in range(4):
            bk, off = h // 3, (h % 3) * 129
            P.add("dve", lambda e, h=h, bk=bk, off=off: e.reciprocal(out=rd[:TS, 8 + h:9 + h], in_=OM[bk][:TS, off + 128:off + 129]),
                  reads=[OMK[bk]], writes=[("rd", 8 + h)])
            P.add("dve", lambda e, h=h, bk=bk, off=off: e.tensor_scalar(
                out=self.oas[:TS, 1536 + h * 128:1536 + (h + 1) * 128], in0=OM[bk][:TS, off:off + 128],
                scalar1=rd[:TS, 8 + h:9 + h], scalar2=None, op0=ALU.mult), reads=[OMK[bk], ("rd", 8 + h)], writes=[("oas", 12 + h)])
        ps, psk = self.psum()
        for g in range(4):
            P.add("pe", lambda e, ps=ps, g=g: e.matmul(
                ps[:TS, g * 128:(g + 1) * 128], lhsT=self.cst[:TS, C_ASGU + g * 16:C_ASGU + (g + 1) * 16],
                rhs=self.vns[:TS, g * 128:(g + 1) * 128], start=True, stop=True),
                reads=[("cst",)], writes=[psk])
        for g in range(4):
            P.add("dve", lambda e, ps=ps, g=g: e.scalar_tensor_tensor(
                out=self.oas[:TS, 1024 + g * 128:1024 + (g + 1) * 128], in0=ps[:TS, g * 128:(g + 1) * 128],
                scalar=self.cst[:TS, C_BSS + g:C_BSS + g + 1], in1=self.us[:TS, g * 128:(g + 1) * 128],
                op0=ALU.add, op1=ALU.mult), reads=[psk, ("cst",)], writes=[("oas", 8 + g)])
        ps, psk = self.psum()
        for c in range(NCH):
            P.add("pe", lambda e, ps=ps, c=c: e.transpose(out=ps[:, c * 16:(c + 1) * 16], in_=self.oas[:TS, c * 128:(c + 1) * 128],
                                                           identity=self.ident[:TS, :TS]),
                  reads=[("oas", c), ("ident",)], writes=[psk])
        P.add("dve", lambda e, ps=ps: e.tensor_copy(out=CAT[:, :, TP:T], in_=ps[:, 0:256].rearrange("p (c t) -> p c t", c=NCH)),
              reads=[psk], writes=[("CATs",)])
        self.ring = list(range(8))
        P.barrier()
        P.flush()

        wout = dn("wout", [D, D])
        w2g, w2u, w2d = dn("w2g", [D, DFF]), dn("w2u", [D, DFF]), dn("w2d", [DFF, D])
        al = Alloc(self, self.free_off)
        hb = [al([T], F32) for _ in range(2)]
        sg = [al([348], F32) for _ in range(3)]
        stg = [al([D], F32)]
        for (c0, c1, on) in ((0, 8, 1024), (8, 12, 512), (12, 16, 512)):
            self.rms_stats(lambda c, t0, n, c0=c0: CAT[:, c0 + c, t0:t0 + n], lambda c, ti: [], c1 - c0, on, MAIN, self.rstd, "rstd")
            for ti, (t0, n) in enumerate(MAIN.tiles):
                for c in range(c0, c1):
                    P.add("dve", lambda e, c=c, t0=t0, n=n: e.scalar_tensor_tensor(
                        out=CAT[:, c, t0:t0 + n], in0=CAT[:, c, t0:t0 + n], scalar=self.gains[:, 2, c:c + 1],
                        in1=self.rstd[:, t0:t0 + n], op0=ALU.mult, op1=ALU.mult),
                        reads=[("rstd", ti), ("gains",)], writes=[("CATn", c, ti)])
        for sidx in range(8):
            slab, sk = self.load_slab(wout, 0, D, sidx * 256)
            for j in range(2):
                m = sidx * 2 + j
                hi = self.nxt("hb", 2)
                hbt, hbk = hb[hi], ("hb", hi)
                P.add("sp", lambda e, hbt=hbt, m=m: e.dma_start(out=hbt, in_=hscr[m]), writes=[hbk], slot=("hb", hi))
                for ti, (t0, n) in enumerate(MAIN.tiles):
                    ps, psk = self.psum()
                    for ko in range(NCH):
                        P.add("pe", lambda e, ps=ps, slab=slab, ko=ko, j=j, t0=t0, n=n: e.matmul(
                            ps[:, :n], lhsT=slab[:, ko, j * 128:(j + 1) * 128], rhs=CAT[:, ko, t0:t0 + n],
                            start=(ko == 0), stop=(ko == NCH - 1)),
                            reads=[sk, ("CATn", ko, ti)], writes=[psk])
                    P.add("dve", lambda e, ps=ps, m=m, t0=t0, n=n, hbt=hbt: e.tensor_tensor(
                        out=R[:, m, t0:t0 + n], in0=ps[:, :n], in1=hbt[:, t0:t0 + n], op=ALU.add),
                        reads=[psk, hbk], writes=[("R", m, ti)])
        P.barrier()
        P.flush()
        self.rmsnorm_R(3, MAIN)
        self.ffn(w2g, w2u, w2d, MAIN, sg)
        stg_b = self.view(self.free_off, [D], F32)
        self.transpose_store(lambda c, t0, nt: R[:, c, t0:t0 + nt],
                             lambda c: [("R", c, ti) for ti in range(3)] + [("R", c, "all")], NCH, T, y, stg + [stg_b])
        P.flush(final=True)


def build_nc(cfg=None):
    cfg = cfg or {}
    nc = bass.Bass("TRN2", target_bir_lowering=False)
    with contextlib.ExitStack() as stack:
        b = Builder(nc, stack, cfg)
        b.build()
    return nc


def _rel_bucket(dist):
    dist = np.asarray(dist, np.int64)
    max_exact = N_BUCKETS // 2
    large = max_exact + (np.log(np.maximum(dist, 1) / max_exact) / np.log(REL_MAX_DIST / max_exact)
                         * (N_BUCKETS - max_exact)).astype(np.int64)
    large = np.minimum(large, N_BUCKETS - 1)
    return np.where(dist < max_exact, dist, large).astype(np.int32)


def make_in_maps(inputs):
    f = lambda k: np.asarray(inputs[k], np.float32)
    xp, xsm = f("x_prompt"), f("x_sample")
    rel = f("rel_bias")
    ident = np.eye(128, dtype=np.float32)

    def gl(v):
        return np.ascontiguousarray(np.asarray(v, np.float32).reshape(NCH, 128).T)

    gains = np.ascontiguousarray(np.concatenate(
        [gl(f("g_ffn1")[0]), gl(f("g_mix")[0]), gl(f("g_mix_out")[0]), gl(f("g_ffn2")[0])], axis=1))
    g_mem = gl(f("g_mem")[0])
    i = np.arange(128)[:, None]
    pbias = np.empty((8, 128, 576), np.float32)
    for dil, lo in ((1, 0), (4, 256)):
        jj = np.arange(256)[None, :]
        step = jj - i
        valid = (step >= 0) & (step <= 128)
        bk = _rel_bucket(np.clip(step, 0, 128) * dil)
        for h in range(8):
            pbias[h, :, lo:lo + 256] = np.where(valid, rel[bk, h], NEG)
    jj = np.arange(64)[None, :]
    step = 64 + jj - i
    valid = step >= 0
    bk = _rel_bucket(np.clip(step, 0, 128) * 16)
    for h in range(8):
        pbias[h, :, 512:576] = np.where(valid, rel[bk, h], NEG)

    w_sgu = f("w_sgu")[0]
    b_sgu = f("b_sgu")[0]
    g_sgu = f("g_sgu")[0]
    maps = []
    for c in range(NCORES):
        b, half = c // 2, c % 2
        cst = np.zeros((128, NCST), np.float32)
        cst[:, C_GQA] = f("g_qa")[0]
        cst[:, C_GKA] = f("g_ka")[0]
        cst[:, C_GQM] = f("g_qm")[0]
        cst[:, C_GKM] = f("g_km")[0]
        for g in range(4):
            cst[:, C_GSGU + g] = g_sgu[g]
        if half == 0:
            cst[:, C_PM0] = NEG
            cst[:64, C_PM1] = NEG
        cst[:, C_TRIL:C_TRIL + 128] = np.tril(np.ones((128, 128), np.float32))
        cst[:, C_BSGU:C_BSGU + 512] = b_sgu.reshape(1, 512)
        for tok in range(16):
            cst[tok, C_BSS:C_BSS + 4] = b_sgu[:, tok % 4]
        for g in range(4):
            for tk in range(16):
                for tq in range(16):
                    if tk // 4 == tq // 4 and tk % 4 <= tq % 4:
                        cst[tk, C_ASGU + g * 16 + tq] = w_sgu[g, tq % 4, tk % 4]
        for t in range(4):
            cst[:, C_Z + t * 28 + 12 + t] = 1.0
        ii = np.arange(128)
        for t in range(4):
            for dl, dil in enumerate((1, 4, 16)):
                if dl == 0:
                    j = 128 + t - ii
                    valid = ii >= t
                else:
                    j = 128 - ii
                    valid = np.ones(128, bool)
                bk = _rel_bucket(np.clip(j, 0, 128) * dil)
                for h in range(8):
                    cst[:, C_BIASS + (t * 3 + dl) * 8 + h] = np.where(valid, rel[bk, h], NEG)
        b0 = _rel_bucket(np.array([0]))[0]
        for tk in range(16):
            for q in range(16):
                for cc in range(3):
                    col = C_BIASN + (q * 3 + cc) * 8
                    same = tk // 4 == q // 4
                    tk_, tq_ = tk % 4, q % 4
                    if cc == 0 and same and tk_ <= tq_:
                        cst[tk, col:col + 8] = rel[_rel_bucket(np.array([tq_ - tk_]))[0], :]
                    elif cc > 0 and same and tk_ == tq_:
                        cst[tk, col:col + 8] = rel[b0, :]
                    else:
                        cst[tk, col:col + 8] = NEG
        xs = np.concatenate([xp[b, half * TP:(half + 1) * TP], xsm[4 * c:4 * c + 4].reshape(TS, D)], axis=0)
        xprev = xp[b, 0:TP]
        maps.append({
            "xs": np.ascontiguousarray(xs), "xprev": np.ascontiguousarray(xprev), "mem": f("mem_prompt")[b],
            "w1g": f("w1_gate")[0], "w1u": f("w1_up")[0], "w1d": f("w1_down")[0],
            "w2g": f("w2_gate")[0], "w2u": f("w2_up")[0], "w2d": f("w2_down")[0],
            "win": f("w_in")[0], "wout": f("w_out")[0], "wmem": f("w_mem_kv")[0], "wsgu": w_sgu,
            "pbias": pbias,
            "cwk": f("cache_win_k")[0, 4 * c:4 * c + 4].reshape(4, 2048, 1024),
            "cwv": f("cache_win_v")[0, 4 * c:4 * c + 4].reshape(4, 2048, 1024),
            "cmk": f("cache_mem_k")[0, 4 * c:4 * c + 4].reshape(4, 256, 512),
            "cmv": f("cache_mem_v")[0, 4 * c:4 * c + 4].reshape(4, 256, 512),
            "ident": ident, "gains": gains, "cst": cst, "g_mem": g_mem,
        })
    return maps


def assemble(outs):
    n = len(outs)
    y_p = np.zeros((4, 2048, D), np.float32)
    y_s = np.zeros((32, 4, D), np.float32)
    wk_p = np.zeros((1, 4, 2048, 8, 128), np.float32)
    wv_p = np.zeros((1, 4, 2048, 8, 128), np.float32)
    mk_p = np.zeros((1, 4, 256, 4, 128), np.float32)
    mv_p = np.zeros((1, 4, 256, 4, 128), np.float32)
    wk_s = np.zeros((1, 32, 4, 8, 128), np.float32)
    wv_s = np.zeros((1, 32, 4, 8, 128), np.float32)
    cv_s = np.zeros((1, 32, 4, 4, 128), np.float32)
    for c in range(n):
        b, half = c // 2, c % 2
        o = outs[c]
        sl = slice(half * TP, (half + 1) * TP)
        y_p[b, sl] = o["y"][:TP]
        y_s[4 * c:4 * c + 4] = o["y"][TP:].reshape(4, 4, D)
        wk_p[0, b, sl] = o["o_wk"][:TP].reshape(TP, 8, 128)
        wv_p[0, b, sl] = o["o_wv"][:TP].reshape(TP, 8, 128)
        wk_s[0, 4 * c:4 * c + 4] = o["o_wk"][TP:].reshape(4, 4, 8, 128)
        wv_s[0, 4 * c:4 * c + 4] = o["o_wv"][TP:].reshape(4, 4, 8, 128)
        cv_s[0, 4 * c:4 * c + 4] = o["o_cv"].reshape(4, 4, 4, 128)
        if half == 0:
            mk_p[0, b] = o["o_mk"].reshape(256, 4, 128)
            mv_p[0, b] = o["o_mv"].reshape(256, 4, 128)
    return (y_p, y_s, wk_p, wv_p, mk_p, mv_p, wk_s, wv_s, cv_s)


def kernel(**inputs):
    nc = build_nc()
    maps = make_in_maps(inputs)
    res = run_bass_kernel_spmd(nc, maps, core_ids=list(range(NCORES)))
    return assemble(res.results)
```
